# Optimizing a Trainium2 kernel written in Bass

```python
import math
import jax, jax.numpy as jnp
from jax import lax
import numpy as np

D_MODEL = 1024
BATCH = 2
SEQ = 8192
DEPTH = 1

HEAD_DIM = 64
N_SB_HEADS = 8
N_FOX_HEADS = 8
SB_WIDTH = N_SB_HEADS * HEAD_DIM
FOX_WIDTH = N_FOX_HEADS * HEAD_DIM
Q_BLOCK = 128
D_FF = 2816
CONV_WIDTH = 3
PLE_DIM = 256
RMS_EPS = 1e-6
IN_SPLITS = (SB_WIDTH, SB_WIDTH, SB_WIDTH,
             FOX_WIDTH, FOX_WIDTH, FOX_WIDTH,
             N_FOX_HEADS, D_MODEL, D_MODEL)
IN_COLS = sum(IN_SPLITS)

kernel_name = 'hybrid_stickbreak_fox_convffn_block'


def _rmsnorm(x, g):
    xf = x.astype(jnp.float32)
    y = xf * lax.rsqrt(jnp.mean(xf * xf, axis=-1, keepdims=True) + RMS_EPS)
    return (y * g.astype(jnp.float32)).astype(x.dtype)


def _split_heads(t, n_heads):
    b, s, _ = t.shape
    return t.reshape(b, s, n_heads, HEAD_DIM).transpose(0, 2, 1, 3)


def _merge_heads(t):
    b, h, s, dh = t.shape
    return t.transpose(0, 2, 1, 3).reshape(b, s, h * dh)


def _query_blocks(t):
    b, h, s = t.shape[:3]
    nb = s // Q_BLOCK
    t = t.reshape((b, h, nb, Q_BLOCK) + t.shape[3:])
    return jnp.moveaxis(t, 2, 0)


def _unblock(o):
    nb, b, h, qb, dh = o.shape
    return jnp.moveaxis(o, 0, 2).reshape(b, h, nb * qb, dh)


def _stick_breaking_attention(q, k, v):
    s_len = k.shape[2]
    nb = s_len // Q_BLOCK
    scale = HEAD_DIM ** -0.5
    kpos = jnp.arange(s_len, dtype=jnp.int32)

    def block(args):
        qb, start = args
        qpos = start + jnp.arange(Q_BLOCK, dtype=jnp.int32)
        z = jnp.einsum('bhqd,bhkd->bhqk', qb, k, preferred_element_type=jnp.float32) * scale
        mask = kpos[None, :] < qpos[:, None]
        log_beta = jax.nn.log_sigmoid(z)
        log_one_minus = jnp.where(mask, jax.nn.log_sigmoid(-z), 0.0)
        between = lax.cumsum(log_one_minus, axis=3, reverse=True) - log_one_minus
        weights = jnp.where(mask, jnp.exp(log_beta + between), 0.0)
        return jnp.einsum('bhqk,bhkd->bhqd', weights.astype(v.dtype), v)

    starts = jnp.arange(nb, dtype=jnp.int32) * Q_BLOCK
    return _unblock(lax.map(block, (_query_blocks(q), starts)))


def _forgetting_attention(q, k, v, cum_log_f):
    s_len = k.shape[2]
    nb = s_len // Q_BLOCK
    scale = HEAD_DIM ** -0.5
    kpos = jnp.arange(s_len, dtype=jnp.int32)

    def block(args):
        qb, cq, start = args
        qpos = start + jnp.arange(Q_BLOCK, dtype=jnp.int32)
        logits = jnp.einsum('bhqd,bhkd->bhqk', qb, k, preferred_element_type=jnp.float32) * scale
        logits = logits + cq[..., :, None] - cum_log_f[:, :, None, :]
        mask = kpos[None, :] <= qpos[:, None]
        probs = jax.nn.softmax(jnp.where(mask, logits, -jnp.inf), axis=-1)
        return jnp.einsum('bhqk,bhkd->bhqd', probs.astype(v.dtype), v)

    starts = jnp.arange(nb, dtype=jnp.int32) * Q_BLOCK
    return _unblock(lax.map(block, (_query_blocks(q), _query_blocks(cum_log_f), starts)))


def _causal_depthwise_conv(u, w, b):
    c = u.shape[-1]
    y = lax.conv_general_dilated(
        u, w.astype(u.dtype).reshape(CONV_WIDTH, 1, c),
        window_strides=(1,), padding=[(CONV_WIDTH - 1, 0)],
        dimension_numbers=('NWC', 'WIO', 'NWC'), feature_group_count=c)
    return y + b.astype(u.dtype)


def setup_inputs(seed: int = 0) -> dict:
    key = jax.random.key(seed)
    ks = jax.random.split(key, 20)
    f32 = jnp.float32

    def nrm(k, shape, fan_in):
        return jax.random.normal(k, shape, f32) * (fan_in ** -0.5)

    def gain(k):
        return 1.0 + 0.05 * jax.random.normal(k, (DEPTH, D_MODEL), f32)

    return {
        'x': jax.random.normal(ks[0], (BATCH, SEQ, D_MODEL), f32),
        'p': jax.random.normal(ks[1], (DEPTH, BATCH, SEQ, PLE_DIM), f32),
        'norm_attn_pre': gain(ks[2]),
        'norm_attn_post': gain(ks[3]),
        'w_in': nrm(ks[4], (DEPTH, D_MODEL, IN_COLS), D_MODEL),
        'b_forget': 2.0 + 0.5 * jax.random.normal(ks[5], (DEPTH, N_FOX_HEADS), f32),
        'w_branch_sb': nrm(ks[6], (DEPTH, SB_WIDTH, D_MODEL), SB_WIDTH),
        'w_branch_fox': nrm(ks[7], (DEPTH, FOX_WIDTH, D_MODEL), FOX_WIDTH),
        'w_out': nrm(ks[8], (DEPTH, D_MODEL, D_MODEL), D_MODEL),
        'norm_ffn_pre': gain(ks[9]),
        'norm_ffn_post': gain(ks[10]),
        'w_up': nrm(ks[11], (DEPTH, D_MODEL, 2 * D_FF), D_MODEL),
        'conv_w': nrm(ks[12], (DEPTH, CONV_WIDTH, 2 * D_FF), CONV_WIDTH),
        'conv_b': 0.02 * jax.random.normal(ks[13], (DEPTH, 2 * D_FF), f32),
        'w_down': nrm(ks[14], (DEPTH, D_FF, D_MODEL), D_FF),
        'w_ple': nrm(ks[15], (DEPTH, PLE_DIM, D_MODEL), PLE_DIM),
        'w_ple_gate': nrm(ks[16], (DEPTH, D_MODEL, D_MODEL), D_MODEL),
    }


def reference(x, p, norm_attn_pre, norm_attn_post, w_in, b_forget, w_branch_sb, w_branch_fox,
              w_out, norm_ffn_pre, norm_ffn_post, w_up, conv_w, conv_b, w_down, w_ple, w_ple_gate):
    offsets = list(np.cumsum(IN_SPLITS)[:-1])
    for i in range(DEPTH):
        h = _rmsnorm(x, norm_attn_pre[i])
        proj = h @ w_in[i]
        q_sb, k_sb, v_sb, q_fx, k_fx, v_fx, f_logit, g_sb, g_fx = jnp.split(proj, offsets, axis=-1)

        y_sb = _stick_breaking_attention(_split_heads(q_sb, N_SB_HEADS),
                                         _split_heads(k_sb, N_SB_HEADS),
                                         _split_heads(v_sb, N_SB_HEADS))

        log_f = jax.nn.log_sigmoid(f_logit.astype(jnp.float32) + b_forget[i].astype(jnp.float32))
        cum_log_f = lax.cumsum(log_f, axis=1).transpose(0, 2, 1)
        y_fx = _forgetting_attention(_split_heads(q_fx, N_FOX_HEADS),
                                     _split_heads(k_fx, N_FOX_HEADS),
                                     _split_heads(v_fx, N_FOX_HEADS), cum_log_f)

        z_sb = _merge_heads(y_sb) @ w_branch_sb[i]
        z_fx = _merge_heads(y_fx) @ w_branch_fox[i]
        mixed = jax.nn.sigmoid(g_sb) * z_sb + jax.nn.sigmoid(g_fx) * z_fx
        x = x + _rmsnorm(mixed @ w_out[i], norm_attn_post[i])

        h = _rmsnorm(x, norm_ffn_pre[i])
        u = _causal_depthwise_conv(h @ w_up[i], conv_w[i], conv_b[i])
        u_gate, u_val = jnp.split(u, 2, axis=-1)
        ffn = (jax.nn.gelu(u_gate, approximate=True) * u_val) @ w_down[i]
        x = x + _rmsnorm(ffn, norm_ffn_post[i])

        x = x + jax.nn.sigmoid(x @ w_ple_gate[i]) * (p[i] @ w_ple[i])
    return x
```

```python
import numpy as np
from contextlib import ExitStack

import concourse.bass as bass
import concourse.mybir as mybir
from concourse.bass_utils import run_bass_kernel_spmd

F32 = mybir.dt.float32
BF16 = mybir.dt.bfloat16
AF = mybir.ActivationFunctionType
ALU = mybir.AluOpType

S = 8192
D = 1024
NBLK = S // 128
NGRP = S // 512
DFF = 2816
NFF = DFF // 128
PLE = 256
HALO = 128
EPS = 1e-6

ENGS = ("sp", "act", "dve", "pool", "pe")


class Buf:
    __slots__ = ("w", "r", "rd")

    def __init__(self):
        self.w = None
        self.r = {}
        self.rd = []


class Op:
    __slots__ = ("eng", "fn", "deps", "sem", "ticket", "dma", "ndep", "phase", "bar")


class Prog:
    def __init__(self):
        self.q = {e: [] for e in ENGS}
        self.phase = 0
        self.final = []
        self.pending = []

    def op(self, eng, fn, reads=(), writes=(), dma=None, extra=(), bar=None):
        o = Op()
        o.bar = bar
        o.eng = eng
        o.fn = fn
        o.dma = dma
        o.ndep = 0
        o.phase = self.phase
        o.sem = None
        o.ticket = 0
        deps = set()
        for b in reads:
            if b.w is not None:
                deps.add(b.w)
        for b in writes:
            if b.w is not None:
                deps.add(b.w)
            for r in b.r.values():
                deps.add(r)
            for r in b.rd:
                deps.add(r)
        for d in extra:
            if d is not None:
                deps.add(d)
        deps.discard(o)
        o.deps = deps
        for d in deps:
            d.ndep += 1
        for b in reads:
            if dma is not None:
                b.rd.append(o)
            else:
                b.r[eng] = o
        for b in writes:
            b.w = o
            b.r = {}
            b.rd = []
        self.q[eng].append(o)
        if dma is not None:
            self.pending.append(o)
        return o

    def barrier(self):
        deps = list(self.pending)
        for e in ENGS:
            for o in reversed(self.q[e]):
                if o.dma is None and o.fn is not None:
                    deps.append(o)
                    break
        self.pending = []
        for e in ENGS:
            self.op(e, None, extra=deps)

    def emit(self, nc, es):
        sems = {}

        def get_sem(key):
            if key not in sems:
                sems[key] = [es.enter_context(nc.semaphore("s_%s" % str(key).replace(":", "_"))), 0]
            return sems[key]

        all_total = {}
        for e in ENGS:
            for o in self.q[e]:
                if o.bar == "cc":
                    continue
                if o.dma is not None:
                    s = get_sem("d:" + o.dma)
                    s[1] += 16
                    o.sem = s[0]
                    o.ticket = s[1]
                    if o.dma.startswith("all:"):
                        all_total[o.dma] = s[1]
                elif o.ndep > 0:
                    s = get_sem("e:%s:%d" % (e, o.phase))
                    s[1] += 1
                    o.sem = s[0]
                    o.ticket = s[1]
        bar_max = {}
        for e in ENGS:
            for o in self.q[e]:
                if o.dma is not None and o.dma.startswith("all:"):
                    o.ticket = all_total[o.dma]
                if o.dma is not None and o.bar is not None:
                    kk = (o.dma, o.bar)
                    bar_max[kk] = max(bar_max.get(kk, 0), o.ticket)
        for e in ENGS:
            for o in self.q[e]:
                if o.dma is not None and o.bar is not None:
                    o.ticket = bar_max[(o.dma, o.bar)]
        final = list(self.final)
        q = self.q
        handles = {"sp": "sync", "act": "scalar", "dve": "vector", "pool": "gpsimd", "pe": "tensor"}

        def run(ename, eng):
            waited = {}
            for o in q[ename]:
                for d in o.deps:
                    if d.sem is None:
                        continue
                    if ename == "pe" and d.eng == "pe" and d.dma is None:
                        continue
                    if o.dma is not None and d.dma == o.dma and (
                            o.dma.startswith("all:") or (o.bar is not None and o.bar == d.bar)):
                        continue
                    k = id(d.sem)
                    if waited.get(k, 0) >= d.ticket:
                        continue
                    waited[k] = d.ticket
                    eng.wait_ge(d.sem, d.ticket)
                if o.fn is None:
                    continue
                ins = o.fn(eng)
                if o.bar == "cc":
                    ins.then_inc(o.sem)
                elif o.sem is not None:
                    ins.then_inc(o.sem, 16 if o.dma is not None else 1)
            if ename == "sp":
                for d in final:
                    eng.wait_ge(d.sem, d.ticket)

        with nc.Block() as block:
            for ename in ENGS:
                deco = getattr(block, handles[ename])

                def mk(ename):
                    def f(eng):
                        run(ename, eng)
                    return f
                deco(mk(ename))


def build(debug=None):
    nc = bass.Bass("TRN2", target_bir_lowering=False)
    es = ExitStack()
    P = Prog()

    def din(name, shape, dt=F32):
        return nc.dram_tensor(name, list(shape), dt, kind="ExternalInput").ap()

    xb = din("xb", [S, D])
    consts = din("consts", [128, 5, 128])
    wqk = din("wqk", [D, 512])
    wvf = din("wvf", [D, 258])
    gpre = din("gpre", [128, 8])
    bfg = din("bfg", [2, 1])
    TOKC = S // 4
    TOK2 = TOKC + HALO
    NT2 = TOK2 // 128
    if debug in ("p1", "p1a"):
        ybuf = nc.dram_tensor("ybuf", [S + HALO, 256], BF16, kind="ExternalOutput")
    else:
        ybuf = nc.dram_tensor("ybuf", [S + HALO, 256], BF16)
    ybuf_ap = ybuf.ap()
    full = debug not in ("p1", "p1a")
    if full:
        x_own = din("x_own", [TOK2, D])
        p_own = din("p_own", [TOK2, PLE])
        w_gate = din("w_gate", [D, 2048])
        w_bsb = din("w_bsb", [512, D])
        w_bfx = din("w_bfx", [512, D])
        w_out = din("w_out", [D, D])
        w_up = din("w_up", [D, 2 * DFF])
        w_down = din("w_down", [DFF, D])
        w_ple = din("w_ple", [PLE, D])
        w_pg = din("w_pg", [D, D])
        g_post_d = din("g_post", [128, D])
        g_fpost_d = din("g_fpost", [128, D])
        gfpre_d = din("gfpre", [128, 8])
        cw_d = din("cw", [128, 2 * NFF, 3])
        cb_d = din("cb", [128, 2 * NFF])
        flag_d = din("flag", [128, 1])
        out_d = nc.dram_tensor("out", [TOKC, D], F32, kind="ExternalOutput").ap()
        yall = nc.dram_tensor("yall", [8 * (S + HALO), 256], BF16)
        x1buf = nc.dram_tensor("x1buf", [TOK2, D], F32).ap()

    def sb(name, shape, dt):
        return es.enter_context(nc.sbuf_tensor(name, list(shape), dt))

    p1 = ExitStack()

    def sb1(name, shape, dt):
        return p1.enter_context(nc.sbuf_tensor(name, list(shape), dt))

    cst = sb1("cst", [128, 5, 128], BF16)
    IDENT, UTRI, ONES, MSB, MFX = (cst[:, i, :] for i in range(5))
    b_cst = Buf()
    QT_sb = sb1("QT_sb", [128, S], BF16)
    KT_sb = sb1("KT_sb", [128, S], BF16)
    QT_fx = [sb1("QT_fx%d" % h, [128, S], BF16) for h in range(2)]
    KT_fx = [sb1("KT_fx%d" % h, [128, S], BF16) for h in range(2)]
    V_sb = sb1("V_sb", [128, NBLK, 2, 64], BF16)
    V_fx = sb1("V_fx", [128, NBLK, 2, 66], BF16)
    b_QTsb = [Buf() for _ in range(NGRP)]
    b_KTsb = [Buf() for _ in range(NGRP)]
    b_QTfx = [[Buf() for _ in range(NGRP)] for _ in range(2)]
    b_KTfx = [[Buf() for _ in range(NGRP)] for _ in range(2)]
    b_QTfx_aug = [[[Buf() for _ in range(3)] for _ in range(NGRP)] for _ in range(2)]
    b_KTfx_aug = [[[Buf() for _ in range(3)] for _ in range(NGRP)] for _ in range(2)]
    b_Vsb = [Buf() for _ in range(NGRP)]
    b_Vfx = [Buf() for _ in range(NGRP)]
    b_Vfx_ones = Buf()
    b_aug_init = Buf()

    mhalf = sb1("mhalf", [128, 1], F32)
    b_mhalf = Buf()

    P.phase = 0
    with ExitStack() as pa:
        def sba(name, shape, dt):
            return pa.enter_context(nc.sbuf_tensor(name, list(shape), dt))

        def psa(name, shape, dt):
            return pa.enter_context(nc.psum_tensor(name, list(shape), dt))

        wstage = sba("wstage", [128, 4, 512], F32)
        b_wstage = Buf()
        wqk_bf = sba("wqk_bf", [128, 8, 512], BF16)
        b_wqk = Buf()
        wvf_bf = sba("wvf_bf", [128, 8, 258], BF16)
        b_wvf = Buf()
        gp = sba("gp", [128, 8], F32)
        b_gp = Buf()
        nb = sba("nb", [2, 1], F32)
        b_nb = Buf()
        NX = 2
        xt = [sba("xt%d" % i, [128, D], F32) for i in range(NX)]
        b_xt = [Buf() for _ in range(NX)]
        junk = [sba("junk%d" % i, [128, D], BF16) for i in range(1)]
        b_junk = [Buf() for _ in range(1)]
        ss = [sba("ss%d" % i, [128, 1], F32) for i in range(NX)]
        b_ss = [Buf() for _ in range(NX)]
        ms = [sba("ms%d" % i, [128, 1], F32) for i in range(NX)]
        b_ms = [Buf() for _ in range(NX)]
        rs = [sba("rs%d" % i, [128, 1], F32) for i in range(NX)]
        b_rs = [Buf() for _ in range(NX)]
        NXN = 5
        xn = [sba("xn%d" % i, [128, D], BF16) for i in range(NXN)]
        b_xn = [Buf() for _ in range(NXN)]
        hT = [sba("hT%d" % i, [128, 8, 512], BF16) for i in range(2)]
        b_hT = [Buf() for _ in range(2)]
        tp_ps = [psa("tp_ps%d" % i, [128, 8, 128], BF16) for i in range(2)]
        b_tp = [Buf() for _ in range(2)]
        NPJ = 4
        pj_ps = [psa("pj_ps%d" % i, [128, 512], F32) for i in range(NPJ)]
        b_pj = [Buf() for _ in range(NPJ)]
        fe = sba("fe", [2, 512], F32)
        b_fe = Buf()
        fl = fe
        b_fl = b_fe
        one2 = sba("one2", [2, 512], F32)
        b_one2 = Buf()
        cc = [sba("cc%d" % i, [2, 512], F32) for i in range(2)]
        b_cc = [Buf() for _ in range(2)]
        r1 = sba("r1", [2, 512], F32)
        b_r1 = Buf()
        r2 = r1
        b_r2 = b_r1
        cpart = [[sba("cp%d_%d" % (s_, i), [2, 512], BF16) for i in range(3)] for s_ in range(1)]
        npart = [[sba("np%d_%d" % (s_, i), [2, 512], BF16) for i in range(3)] for s_ in range(1)]
        b_cpart = [[Buf() for _ in range(3)] for _ in range(1)]
        b_npart = [[Buf() for _ in range(3)] for _ in range(1)]

        P.op("pool", lambda e: e.dma_start(out=cst[:], in_=consts), writes=[b_cst], dma="cstq")
        P.op("sp", lambda e: e.dma_start(out=gp[:], in_=gpre), writes=[b_gp], dma="all:setup")
        P.op("sp", lambda e: e.dma_start(out=nb[:], in_=bfg), writes=[b_nb], dma="all:setup")
        P.op("dve", lambda e: e.tensor_scalar(out=nb[:], in0=nb[:], scalar1=-1.0, scalar2=None, op0=ALU.mult),
             reads=[b_nb], writes=[b_nb])
        P.op("pool", lambda e: e.memset(mhalf[:], -0.5), writes=[b_mhalf])
        P.op("pool", lambda e: e.memset(one2[:], 1.0), writes=[b_one2])
        P.op("pool", lambda e: e.memset(V_fx[:, :, :, 64:66], 1.0), writes=[b_Vfx_ones])
        for h in range(2):
            P.op("pool", lambda e, h=h: e.memset(QT_fx[h][64:70, :], 1.0), writes=[b_aug_init])
            P.op("pool", lambda e, h=h: e.memset(KT_fx[h][64:70, :], 1.0), writes=[b_aug_init])
        wqk_v = wqk.rearrange("(k p) c -> p k c", p=128)
        wvf_v = wvf.rearrange("(k p) c -> p k c", p=128)
        for hf in range(2):
            P.op("sp", lambda e, hf=hf: e.dma_start(out=wstage[:], in_=wqk_v[:, 4 * hf:4 * hf + 4, :]),
                 writes=[b_wstage], dma="wstage")
            for kk in range(4):
                k = 4 * hf + kk
                P.op("dve", lambda e, k=k, kk=kk: e.tensor_scalar(out=wqk_bf[:, k, 0:256], in0=wstage[:, kk, 0:256],
                                                          scalar1=gp[:, k:k + 1], scalar2=0.125,
                                                          op0=ALU.mult, op1=ALU.mult),
                     reads=[b_wstage, b_gp], writes=[b_wqk])
                P.op("dve", lambda e, k=k, kk=kk: e.tensor_scalar(out=wqk_bf[:, k, 256:512], in0=wstage[:, kk, 256:512],
                                                          scalar1=gp[:, k:k + 1], scalar2=None, op0=ALU.mult),
                     reads=[b_wstage, b_gp], writes=[b_wqk])
        for hf in range(2):
            P.op("sp", lambda e, hf=hf: e.dma_start(out=wstage[:, :, 0:258], in_=wvf_v[:, 4 * hf:4 * hf + 4, :]),
                 writes=[b_wstage], dma="wstage")
            for kk in range(4):
                k = 4 * hf + kk
                P.op("dve", lambda e, k=k, kk=kk: e.tensor_scalar(out=wvf_bf[:, k, :], in0=wstage[:, kk, 0:258],
                                                          scalar1=gp[:, k:k + 1], scalar2=None, op0=ALU.mult),
                     reads=[b_wstage, b_gp], writes=[b_wvf])

        pj_ctr = [0]

        def next_pj():
            i = pj_ctr[0] % NPJ
            pj_ctr[0] += 1
            return i

        evac_ctr = [0]

        def evac_eng():
            evac_ctr[0] += 1
            return "act" if evac_ctr[0] % 2 == 0 else "dve"

        def copy_op(eng, out, in_, reads, writes):
            if eng == "act":
                return P.op("act", lambda e: e.activation(out=out, in_=in_, func=AF.Copy), reads=reads, writes=writes)
            return P.op("dve", lambda e: e.tensor_copy(out=out, in_=in_), reads=reads, writes=writes)

        def prep_tile(tt):
            xs = tt % NX
            ns = tt % NXN
            js = 0
            P.op("sp", lambda e: e.dma_start(out=xt[xs][:], in_=xb[tt * 128:(tt + 1) * 128, :]),
                 writes=[b_xt[xs]], dma="xt%d" % xs)
            P.op("act", lambda e: e.activation(out=junk[js][:], in_=xt[xs][:], func=AF.Square, accum_out=ss[xs][:]),
                 reads=[b_xt[xs]], writes=[b_junk[js], b_ss[xs]])
            P.op("dve", lambda e: e.tensor_scalar(out=ms[xs][:], in0=ss[xs][:], scalar1=1.0 / D, scalar2=EPS,
                                                 op0=ALU.mult, op1=ALU.add),
                 reads=[b_ss[xs]], writes=[b_ms[xs]])
            P.op("pool", lambda e: e.tensor_tensor(out=rs[xs][:], in0=ms[xs][:], in1=mhalf[:], op=ALU.pow),
                 reads=[b_ms[xs], b_mhalf], writes=[b_rs[xs]])
            P.op("dve", lambda e: e.tensor_scalar(out=xn[ns][:], in0=xt[xs][:], scalar1=rs[xs][:, 0:1],
                                                 scalar2=None, op0=ALU.mult),
                 reads=[b_xt[xs], b_rs[xs]], writes=[b_xn[ns]])

        def transpose_tile(tt):
            G, j = divmod(tt, 4)
            ns = tt % NXN
            ts_ = tt % 2
            hs = G % 2
            for k in range(8):
                P.op("pe", lambda e, k=k: e.transpose(out=tp_ps[ts_][:, k, :], in_=xn[ns][:, k * 128:(k + 1) * 128],
                                                     identity=IDENT),
                     reads=[b_xn[ns], b_cst], writes=[b_tp[ts_]])
            copy_op(evac_eng(), hT[hs][:, :, j * 128:(j + 1) * 128], tp_ps[ts_][:], [b_tp[ts_]], [b_hT[hs]])

        def project(G):
            hs = G % 2
            cols = slice(G * 512, (G + 1) * 512)
            outs = [
                (slice(0, 128), 128, QT_sb[:, cols], b_QTsb[G]),
                (slice(256, 384), 128, KT_sb[:, cols], b_KTsb[G]),
                (slice(128, 192), 64, QT_fx[0][0:64, cols], b_QTfx[0][G]),
                (slice(192, 256), 64, QT_fx[1][0:64, cols], b_QTfx[1][G]),
                (slice(384, 448), 64, KT_fx[0][0:64, cols], b_KTfx[0][G]),
                (slice(448, 512), 64, KT_fx[1][0:64, cols], b_KTfx[1][G]),
            ]
            for (ws, M, dst, bdst) in outs:
                pi = next_pj()
                for k in range(8):
                    P.op("pe", lambda e, k=k, ws=ws, M=M, pi=pi: e.matmul(
                        pj_ps[pi][0:M, :], lhsT=wqk_bf[:, k, ws], rhs=hT[hs][:, k, :],
                        start=(k == 0), stop=(k == 7)),
                        reads=[b_wqk, b_hT[hs]], writes=[b_pj[pi]])
                copy_op(evac_eng(), dst, pj_ps[pi][0:M, :], [b_pj[pi]], [bdst])
            pi = next_pj()
            for k in range(8):
                P.op("pe", lambda e, k=k, pi=pi: e.matmul(
                    pj_ps[pi][0:2, :], lhsT=wvf_bf[:, k, 256:258], rhs=hT[hs][:, k, :],
                    start=(k == 0), stop=(k == 7)),
                    reads=[b_wvf, b_hT[hs]], writes=[b_pj[pi]])
            P.op("act", lambda e, pi=pi: e.activation(out=fe[:], in_=pj_ps[pi][0:2, :], func=AF.Exp,
                                                     scale=-1.0, bias=nb[:, 0:1]),
                 reads=[b_pj[pi], b_nb], writes=[b_fe])
            P.op("act", lambda e: e.activation(out=fl[:], in_=fe[:], func=AF.Ln, bias=1.0, scale=1.0),
                 reads=[b_fe], writes=[b_fl])
            cs = G % 2
            if G == 0:
                P.op("dve", lambda e: e.tensor_tensor_scan(out=cc[cs][:], data0=one2[:], data1=fl[:], initial=0.0,
                                                          op0=ALU.mult, op1=ALU.subtract),
                     reads=[b_one2, b_fl], writes=[b_cc[cs]])
            else:
                P.op("dve", lambda e: e.tensor_tensor_scan(out=cc[cs][:], data0=one2[:], data1=fl[:],
                                                          initial=cc[1 - cs][:, 511:512],
                                                          op0=ALU.mult, op1=ALU.subtract),
                     reads=[b_one2, b_fl, b_cc[1 - cs]], writes=[b_cc[cs]])
            ps_ = 0
            cp, npp = cpart[ps_], npart[ps_]
            bcp, bnp = b_cpart[ps_], b_npart[ps_]
            P.op("dve", lambda e: e.tensor_copy(out=cp[0][:], in_=cc[cs][:]), reads=[b_cc[cs]], writes=[bcp[0]])
            P.op("dve", lambda e: e.tensor_tensor(out=r1[:], in0=cc[cs][:], in1=cp[0][:], op=ALU.subtract),
                 reads=[b_cc[cs], bcp[0]], writes=[b_r1])
            P.op("dve", lambda e: e.tensor_copy(out=cp[1][:], in_=r1[:]), reads=[b_r1], writes=[bcp[1]])
            P.op("dve", lambda e: e.tensor_tensor(out=r2[:], in0=r1[:], in1=cp[1][:], op=ALU.subtract),
                 reads=[b_r1, bcp[1]], writes=[b_r2])
            P.op("dve", lambda e: e.tensor_copy(out=cp[2][:], in_=r2[:]), reads=[b_r2], writes=[bcp[2]])
            for i in range(3):
                P.op("dve", lambda e, i=i: e.tensor_scalar(out=npp[i][:], in0=cp[i][:], scalar1=-1.0, scalar2=None,
                                                          op0=ALU.mult),
                     reads=[bcp[i]], writes=[bnp[i]])
            for h in range(2):
                for i in range(3):
                    P.op("sp", lambda e, h=h, i=i: e.dma_start(out=QT_fx[h][64 + i:65 + i, cols], in_=cp[i][h:h + 1, :]),
                         reads=[bcp[i], b_aug_init], writes=[b_QTfx_aug[h][G][i]], dma="augq%d" % ps_, bar=G)
                    P.op("sp", lambda e, h=h, i=i: e.dma_start(out=KT_fx[h][67 + i:68 + i, cols], in_=npp[i][h:h + 1, :]),
                         reads=[bnp[i], b_aug_init], writes=[b_KTfx_aug[h][G][i]], dma="augk%d" % ps_, bar=G)
            for j in range(4):
                tt = 4 * G + j
                pi = next_pj()
                for k in range(8):
                    P.op("pe", lambda e, k=k, pi=pi, j=j: e.matmul(
                        pj_ps[pi][:, 0:256], lhsT=hT[hs][:, k, j * 128:(j + 1) * 128], rhs=wvf_bf[:, k, 0:256],
                        start=(k == 0), stop=(k == 7)),
                        reads=[b_wvf, b_hT[hs]], writes=[b_pj[pi]])
                copy_op(evac_eng(), V_sb[:, tt, :, :],
                        pj_ps[pi][:, 0:128].rearrange("p (h d) -> p h d", h=2), [b_pj[pi]], [b_Vsb[G]])
                copy_op(evac_eng(), V_fx[:, tt, :, 0:64],
                        pj_ps[pi][:, 128:256].rearrange("p (h d) -> p h d", h=2), [b_pj[pi], b_Vfx_ones], [b_Vfx[G]])

        for j in range(4):
            prep_tile(j)
        for G in range(NGRP):
            for j in range(4):
                transpose_tile(4 * G + j)
                if G + 1 < NGRP:
                    prep_tile(4 * (G + 1) + j)
            project(G)

    P.barrier()
    ydma_ops = []
    tiles_per_head = []
    for qg in range(NGRP):
        for kb in range(4 * qg + 3, -1, -1):
            i = kb - 4 * qg
            tiles_per_head.append((qg, kb, i if i >= 0 else None))

    if debug == "p1a":
        tiles_per_head = []
    with ExitStack() as pb:
        def sbb(name, shape, dt):
            return pb.enter_context(nc.sbuf_tensor(name, list(shape), dt))

        def psb(name, shape, dt):
            return pb.enter_context(nc.psum_tensor(name, list(shape), dt))

        z_ps = [psb("z_ps%d" % i, [128, 512], F32) for i in range(2)]
        b_z = [Buf() for _ in range(2)]
        bt_ps = [psb("bt_ps%d" % i, [128, 512], F32) for i in range(2)]
        b_bt = [Buf() for _ in range(2)]
        acc_ps = [psb("acc_ps%d" % i, [128, 4, 128], F32) for i in range(2)]
        b_acc = [Buf() for _ in range(2)]
        e_sb = [sbb("e_sb%d" % i, [128, 512], F32) for i in range(2)]
        b_e = [Buf() for _ in range(2)]
        l_sb = [sbb("l_sb%d" % i, [128, 512], F32) for i in range(3)]
        b_l = [Buf() for _ in range(3)]
        lom = [sbb("lom%d" % i, [128, 512], BF16) for i in range(3)]
        b_lom = [Buf() for _ in range(3)]
        Ls = [[sbb("Ls%d_%d" % (a, b_), [128, 512], BF16) for b_ in range(2)] for a in range(2)]
        b_Ls = [[Buf() for _ in range(2)] for _ in range(2)]
        arg = [sbb("arg%d" % i, [128, 512], F32) for i in range(2)]
        b_arg = [Buf() for _ in range(2)]
        wt = [sbb("wt%d" % i, [128, 512], BF16) for i in range(3)]
        b_wt = [Buf() for _ in range(3)]
        yst = [sbb("yst%d" % i, [128, 4, 64], BF16) for i in range(3)]
        b_yst = [Buf() for _ in range(3)]
        rc = [sbb("rc%d" % i, [128, 4], F32) for i in range(2)]
        b_rc = [Buf() for _ in range(2)]
        yst_ctr = [0]

        ybv = ybuf_ap[HALO:, :].rearrange("(q j p) c -> q p j c", j=4, p=128)

        P.phase = 1
        sb_tiles = [(hh,) + t for hh in range(2) for t in tiles_per_head]
        NTL = len(sb_tiles)

        def cr(t):
            i = t[3]
            return (0 if i is None else 128 * i)

        def sbA(n):
            hh, qg, kb, i = sb_tiles[n]
            c0 = cr(sb_tiles[n])
            hp = slice(64 * hh, 64 * hh + 64)
            zb = n % 2
            P.op("pe", lambda e: e.matmul(z_ps[zb][:, c0:512], lhsT=KT_sb[hp, kb * 128:(kb + 1) * 128],
                                         rhs=QT_sb[hp, qg * 512 + c0:(qg + 1) * 512], start=True, stop=True),
                 reads=[b_KTsb[kb // 4], b_QTsb[qg]], writes=[b_z[zb]])

        def sbB1(n):
            c0 = cr(sb_tiles[n])
            zb, eb = n % 2, n % 2
            P.op("act", lambda e: e.activation(out=e_sb[eb][:, c0:512], in_=z_ps[zb][:, c0:512], func=AF.Exp, scale=-1.0),
                 reads=[b_z[zb]], writes=[b_e[eb]])

        def sbB2(n):
            c0 = cr(sb_tiles[n])
            eb, lb = n % 2, n % 3
            P.op("act", lambda e: e.activation(out=l_sb[lb][:, c0:512], in_=e_sb[eb][:, c0:512], func=AF.Ln,
                                               bias=1.0, scale=1.0),
                 reads=[b_e[eb]], writes=[b_l[lb]])

        def sbC(n):
            hh, qg, kb, i = sb_tiles[n]
            c0 = cr(sb_tiles[n])
            zb, lb, mb = n % 2, n % 3, n % 3
            P.op("dve", lambda e: e.scalar_tensor_tensor(out=lom[mb][:, c0:512], in0=z_ps[zb][:, c0:512], scalar=-1.0,
                                                        in1=l_sb[lb][:, c0:512], op0=ALU.mult, op1=ALU.subtract),
                 reads=[b_z[zb], b_l[lb]], writes=[b_lom[mb]])
            if i is not None:
                P.op("pool", lambda e: e.tensor_tensor(out=lom[mb][:, c0:c0 + 128], in0=lom[mb][:, c0:c0 + 128],
                                                      in1=MSB, op=ALU.mult),
                     reads=[b_lom[mb], b_cst], writes=[b_lom[mb]])

        def sbD(n):
            hh, qg, kb, i = sb_tiles[n]
            c0 = cr(sb_tiles[n])
            mb, bb = n % 3, n % 2
            gp_ = qg % 2
            first = (i == 3)
            if first:
                for b_ in range(2):
                    P.op("pool", lambda e, b_=b_: e.memset(Ls[gp_][b_][:], 0.0), writes=[b_Ls[gp_][b_]])
            k_in_grp = (4 * qg + 3) - kb
            old, new = k_in_grp % 2, 1 - (k_in_grp % 2)
            P.op("pe", lambda e: e.matmul(bt_ps[bb][:, c0:512], lhsT=UTRI, rhs=lom[mb][:, c0:512], start=True, stop=False),
                 reads=[b_lom[mb], b_cst], writes=[b_bt[bb]])
            P.op("pe", lambda e: e.matmul(bt_ps[bb][:, c0:512], lhsT=ONES, rhs=Ls[gp_][old][:, c0:512], start=False, stop=True),
                 reads=[b_Ls[gp_][old], b_cst], writes=[b_bt[bb]])
            if kb > 0:
                P.op("pool", lambda e: e.tensor_tensor(out=Ls[gp_][new][:, c0:512], in0=Ls[gp_][old][:, c0:512],
                                                      in1=lom[mb][:, c0:512], op=ALU.add),
                     reads=[b_Ls[gp_][old], b_lom[mb]], writes=[b_Ls[gp_][new]])

        def sbE(n):
            c0 = cr(sb_tiles[n])
            bb, lb, ab = n % 2, n % 3, n % 2
            P.op("dve", lambda e: e.tensor_tensor(out=arg[ab][:, c0:512], in0=bt_ps[bb][:, c0:512],
                                                 in1=l_sb[lb][:, c0:512], op=ALU.subtract),
                 reads=[b_bt[bb], b_l[lb]], writes=[b_arg[ab]])

        def sbF(n):
            hh, qg, kb, i = sb_tiles[n]
            c0 = cr(sb_tiles[n])
            ab, wb = n % 2, n % 3
            P.op("act", lambda e: e.activation(out=wt[wb][:, c0:512], in_=arg[ab][:, c0:512], func=AF.Exp),
                 reads=[b_arg[ab]], writes=[b_wt[wb]])
            if i is not None:
                P.op("pool", lambda e: e.tensor_tensor(out=wt[wb][:, c0:c0 + 128], in0=wt[wb][:, c0:c0 + 128],
                                                      in1=MSB, op=ALU.mult),
                     reads=[b_wt[wb], b_cst], writes=[b_wt[wb]])

        def emit_y(hc, qg, src_fn, reads):
            ys = yst_ctr[0] % 3
            yst_ctr[0] += 1
            src_fn(ys)
            o = P.op("sp", lambda e: e.dma_start(out=ybv[qg][:, :, hc * 64:(hc + 1) * 64], in_=yst[ys][:]),
                     reads=[b_yst[ys]], dma="yst%d" % ys)
            ydma_ops.append(o)

        def sbG(n):
            hh, qg, kb, i = sb_tiles[n]
            wb = n % 3
            ab_ = qg % 2
            j0 = 0 if i is None else i
            for j in range(j0, 4):
                first = (i == 3 and j == 3)
                P.op("pe", lambda e, j=j, first=first: e.matmul(
                    acc_ps[ab_][:, j, 0:64], lhsT=wt[wb][:, j * 128:(j + 1) * 128], rhs=V_sb[:, kb, hh, :],
                    start=first, stop=(kb == 0), skip_group_check=True),
                    reads=[b_wt[wb], b_Vsb[kb // 4]], writes=[b_acc[ab_]])
            if kb == 0:
                def src(ys):
                    P.op("dve", lambda e: e.tensor_copy(out=yst[ys][:], in_=acc_ps[ab_][:, :, 0:64]),
                         reads=[b_acc[ab_]], writes=[b_yst[ys]])
                emit_y(hh, qg, src, None)

        for s_ in range(NTL + 4):
            if s_ < NTL:
                sbA(s_)
                sbB1(s_)
            if 0 <= s_ - 3 < NTL:
                sbF(s_ - 3)
            if s_ < NTL:
                sbB2(s_)
            if 0 <= s_ - 1 < NTL:
                sbC(s_ - 1)
            if 0 <= s_ - 2 < NTL:
                sbD(s_ - 2)
                sbE(s_ - 2)
            if 0 <= s_ - 4 < NTL:
                sbG(s_ - 4)

        P.phase = 2
        fx_tiles = sb_tiles

        def fxA(n):
            hh, qg, kb, i = fx_tiles[n]
            c0 = cr(fx_tiles[n])
            zb = n % 2
            P.op("pe", lambda e: e.matmul(z_ps[zb][:, c0:512], lhsT=KT_fx[hh][0:70, kb * 128:(kb + 1) * 128],
                                         rhs=QT_fx[hh][0:70, qg * 512 + c0:(qg + 1) * 512], start=True, stop=True),
                 reads=[b_KTfx[hh][kb // 4], b_QTfx[hh][qg], b_aug_init] + b_KTfx_aug[hh][kb // 4] + b_QTfx_aug[hh][qg],
                 writes=[b_z[zb]])

        def fxB(n):
            hh, qg, kb, i = fx_tiles[n]
            c0 = cr(fx_tiles[n])
            zb, wb = n % 2, n % 3
            P.op("act", lambda e: e.activation(out=wt[wb][:, c0:512], in_=z_ps[zb][:, c0:512], func=AF.Exp),
                 reads=[b_z[zb]], writes=[b_wt[wb]])
            if i is not None:
                P.op("pool", lambda e: e.tensor_tensor(out=wt[wb][:, c0:c0 + 128], in0=wt[wb][:, c0:c0 + 128],
                                                      in1=MFX, op=ALU.mult),
                     reads=[b_wt[wb], b_cst], writes=[b_wt[wb]])

        def fxG(n):
            hh, qg, kb, i = fx_tiles[n]
            wb = n % 3
            ab_ = qg % 2
            j0 = 0 if i is None else i
            for j in range(j0, 4):
                first = (i == 3 and j == 3)
                P.op("pe", lambda e, j=j, first=first: e.matmul(
                    acc_ps[ab_][:, j, 0:65], lhsT=wt[wb][:, j * 128:(j + 1) * 128], rhs=V_fx[:, kb, hh, 0:65],
                    start=first, stop=(kb == 0), skip_group_check=True),
                    reads=[b_wt[wb], b_Vfx[kb // 4], b_Vfx_ones], writes=[b_acc[ab_]])
            if kb == 0:
                rb = qg % 2

                def src(ys):
                    P.op("dve", lambda e: e.reciprocal(out=rc[rb][:], in_=acc_ps[ab_][:, :, 64]),
                         reads=[b_acc[ab_]], writes=[b_rc[rb]])
                    for j in range(4):
                        P.op("dve", lambda e, j=j: e.tensor_scalar(out=yst[ys][:, j, :], in0=acc_ps[ab_][:, j, 0:64],
                                                                  scalar1=rc[rb][:, j:j + 1], scalar2=None, op0=ALU.mult),
                             reads=[b_acc[ab_], b_rc[rb]], writes=[b_yst[ys]])
                emit_y(2 + hh, qg, src, None)

        for s_ in range(NTL + 2):
            if s_ < NTL:
                fxA(s_)
            if 0 <= s_ - 1 < NTL:
                fxB(s_ - 1)
            if 0 <= s_ - 2 < NTL:
                fxG(s_ - 2)

    p1.close()
    if not full:
        P.final = list(ydma_ops)
        P.emit(nc, es)
        es.close()
        return nc

    P.barrier()
    P.phase = 3
    p2 = ExitStack()

    def sb2(name, shape, dt):
        return p2.enter_context(nc.sbuf_tensor(name, list(shape), dt))

    cst2 = sb2("cst2", [128, 128], BF16)
    b_cst2 = Buf()
    zpad = sb2("zpad", [128, 256], BF16)
    b_zpad = Buf()
    mh2 = sb2("mh2", [128, 1], F32)
    b_mh2 = Buf()
    g_post = sb2("g_post_t", [128, D], F32)
    g_fpost = sb2("g_fpost_t", [128, D], F32)
    gfp = sb2("gfp", [128, 8], F32)
    gp2 = sb2("gp2", [128, 8], F32)
    cw = sb2("cw_t", [128, 2 * NFF, 3], F32)
    cb = sb2("cb_t", [128, 2 * NFF], F32)
    flag = sb2("flag_t", [128, 1], F32)
    b_par = Buf()
    P.op("pool", lambda e: e.memset(zpad[:], 0.0), writes=[b_zpad])
    P.op("pool", lambda e: e.memset(mh2[:], -0.5), writes=[b_mh2])
    zp = P.op("sp", lambda e: e.dma_start(out=ybuf_ap[0:HALO, :], in_=zpad[:]), reads=[b_zpad], dma="zpad")
    P.op("pool", lambda e: e.dma_start(out=cst2[:], in_=consts[:, 0, :]), writes=[b_cst2], dma="cstq2")
    for (t_, d_) in ((g_post, g_post_d), (g_fpost, g_fpost_d), (gfp, gfpre_d), (gp2, gpre), (cw, cw_d), (cb, cb_d),
                     (flag, flag_d)):
        P.op("sp", lambda e, t_=t_, d_=d_: e.dma_start(out=t_[:], in_=d_), writes=[b_par], dma="all:par")
    b_yall = Buf()

    class CC:
        pass
    cc_sem = es.enter_context(nc.semaphore("cc_sem"))

    def cc_fn(e):
        if debug == "nocc":
            return e.dma_start(out=yall.ap()[0:S + HALO, :], in_=ybuf_ap)
        return e.collective_compute("AllGather", ALU.bypass, replica_groups=[list(range(8))],
                                    ins=[ybuf_ap.opt()], outs=[yall.ap().opt()])
    cc_op = P.op("pool", cc_fn, writes=[b_yall], extra=list(ydma_ops) + [zp])
    if debug != "nocc":
        cc_op.sem = cc_sem
        cc_op.ticket = 1
        cc_op.bar = "cc"
        P.pending.append(cc_op)
    else:
        cc_op.dma = "ccdbg"
        P.pending.append(cc_op)

    IDENT2 = cst2[:]
    rot = {"n": 0}

    h2buf = nc.dram_tensor("h2buf", [128, 8, TOK2], BF16).ap()
    b_h2buf = [Buf() for _ in range(NT2)]
    if True:
        with ExitStack() as pA:
            def sbA_(name, shape, dt):
                return pA.enter_context(nc.sbuf_tensor(name, list(shape), dt))

            def psA_(name, shape, dt):
                return pA.enter_context(nc.psum_tensor(name, list(shape), dt))

            wst = sbA_("wst2", [128, 4, 512], F32)
            b_wst = Buf()
            wg = sbA_("wg", [128, 8, 2048], BF16)
            b_wg = Buf()
            wbs = sbA_("wbs", [128, 4, D], BF16)
            wbf_ = sbA_("wbf", [128, 4, D], BF16)
            wo = sbA_("wo", [128, 8, D], BF16)
            b_wA = Buf()
            NS = 3
            xt2 = [sbA_("xt2_%d" % i, [128, D], F32) for i in range(NS)]
            b_xt2 = [Buf() for _ in range(NS)]
            yt2 = [sbA_("yt2_%d" % i, [128, 4, 256], BF16) for i in range(2)]
            b_yt2 = [Buf() for _ in range(2)]
            jk2 = sbA_("jk2", [128, D], BF16)
            b_jk2 = Buf()
            st2 = [sbA_("st2_%d" % i, [128, 4], F32) for i in range(6)]
            b_st2 = [Buf() for _ in range(6)]
            xn2 = [sbA_("xn2_%d" % i, [128, D], BF16) for i in range(2)]
            b_xn2 = [Buf() for _ in range(2)]
            hTt = [sbA_("hTt%d" % i, [128, 8, 128], BF16) for i in range(2)]
            b_hTt = [Buf() for _ in range(2)]
            yTt = [sbA_("yTt%d" % i, [128, 8, 128], BF16) for i in range(2)]
            b_yTt = [Buf() for _ in range(2)]
            Gs = [sbA_("Gs%d" % i, [128, 2048], F32) for i in range(2)]
            b_Gs = [[Buf(), Buf()] for _ in range(2)]
            m1 = [sbA_("m1_%d" % i, [128, D], F32) for i in range(2)]
            b_m1 = [Buf() for _ in range(2)]
            mx = [sbA_("mx%d" % i, [128, D], BF16) for i in range(2)]
            b_mx = [Buf() for _ in range(2)]
            mTt = [sbA_("mTt%d" % i, [128, 8, 128], BF16) for i in range(2)]
            b_mTt = [Buf() for _ in range(2)]
            t1 = [sbA_("t1_%d" % i, [128, D], F32) for i in range(2)]
            b_t1 = [Buf() for _ in range(2)]
            x1t = [sbA_("x1t%d" % i, [128, D], F32) for i in range(2)]
            b_x1t = [Buf() for _ in range(2)]
            h2b = [sbA_("h2b%d" % i, [128, D], BF16) for i in range(2)]
            b_h2b = [Buf() for _ in range(2)]
            h2Tt = [sbA_("h2Tt%d" % i, [128, 8, 128], BF16) for i in range(2)]
            b_h2Tt = [Buf() for _ in range(2)]
            tp2 = [psA_("tp2_%d" % i, [128, 8, 128], BF16) for i in range(2)]
            b_tp2 = [Buf() for _ in range(2)]
            big = [psA_("big%d" % i, [128, D], F32) for i in range(3)]
            b_big = [Buf() for _ in range(3)]
            b_x1buf = [Buf() for _ in range(NT2)]

            wg_v = w_gate.rearrange("(k p) c -> p k c", p=128)
            for hf in range(2):
                for cg in range(4):
                    P.op("sp", lambda e, hf=hf, cg=cg: e.dma_start(
                        out=wst[:], in_=wg_v[:, 4 * hf:4 * hf + 4, cg * 512:(cg + 1) * 512]),
                        writes=[b_wst], dma="wst2")
                    for kk in range(4):
                        k = 4 * hf + kk
                        P.op("pool", lambda e, k=k, kk=kk, cg=cg: e.tensor_scalar(
                            out=wg[:, k, cg * 512:(cg + 1) * 512], in0=wst[:, kk, :], scalar1=gp2[:, k:k + 1],
                            scalar2=None, op0=ALU.mult),
                            reads=[b_wst, b_par], writes=[b_wg])
            P.op("pool", lambda e: e.dma_start(out=wbs[:], in_=w_bsb.rearrange("(k p) c -> p k c", p=128)),
                 writes=[b_wA], dma="all:wA")
            P.op("pool", lambda e: e.dma_start(out=wbf_[:], in_=w_bfx.rearrange("(k p) c -> p k c", p=128)),
                 writes=[b_wA], dma="all:wA")
            P.op("pool", lambda e: e.dma_start(out=wo[:], in_=w_out.rearrange("(k p) c -> p k c", p=128)),
                 writes=[b_wA], dma="all:wA")

            def nbig():
                i = rot["n"] % 3
                rot["n"] += 1
                return i

            def rstd_chain(src_ap, src_bufs, sti, col, junk_ap, junk_buf):
                P.op("act", lambda e: e.activation(out=junk_ap, in_=src_ap, func=AF.Square, accum_out=st2[sti][:, 0:1]),
                     reads=src_bufs, writes=[junk_buf, b_st2[sti]])
                P.op("dve", lambda e: e.tensor_scalar(out=st2[sti][:, 1:2], in0=st2[sti][:, 0:1], scalar1=1.0 / D,
                                                     scalar2=EPS, op0=ALU.mult, op1=ALU.add),
                     reads=[b_st2[sti]], writes=[b_st2[sti]])
                P.op("pool", lambda e: e.tensor_tensor(out=st2[sti][:, 2:3], in0=st2[sti][:, 1:2], in1=mh2[:], op=ALU.pow),
                     reads=[b_st2[sti], b_mh2], writes=[b_st2[sti]])
                return st2[sti][:, 2:3]

            def transposes(src, src_buf, dst, dst_buf, tpi, chunk_fn=None):
                for k in range(8):
                    in_ap = src[:, k * 128:(k + 1) * 128] if chunk_fn is None else chunk_fn(k)
                    P.op("pe", lambda e, k=k, in_ap=in_ap: e.transpose(out=tp2[tpi][:, k, :], in_=in_ap, identity=IDENT2),
                         reads=[src_buf, b_cst2], writes=[b_tp2[tpi]])
                P.op("act", lambda e: e.activation(out=dst, in_=tp2[tpi][:], func=AF.Copy),
                     reads=[b_tp2[tpi]], writes=[dst_buf])

            tpc = {"n": 0}

            def ntp():
                tpc["n"] += 1
                return tpc["n"] % 2

            yall_v = yall.ap().rearrange("(r t) c -> t r c", r=8)
            pid_cache = {}

            def A0(t):
                xs, ys = t % NS, t % 2
                P.op("sp", lambda e: e.dma_start(out=xt2[xs][:], in_=x_own[t * 128:(t + 1) * 128, :]),
                     writes=[b_xt2[xs]], dma="xt2_%d" % xs)

                def yfn(e):
                    if "c" not in pid_cache:
                        pid = e.partition_id()
                        pid_cache["c"] = pid
                        pid_cache["base"] = yall_v[bass.ds((pid % 4) * TOKC, TOK2), bass.ds((pid // 4) * 4, 4), :]
                    return e.dma_start(out=yt2[ys][:], in_=pid_cache["base"][t * 128:(t + 1) * 128, :, :])
                P.op("sp", yfn, reads=[b_yall], writes=[b_yt2[ys]], dma="yt2_%d" % ys, extra=[cc_op])
                sti = (2 * t) % 6
                r = rstd_chain(xt2[xs][:], [b_xt2[xs]], sti, 0, jk2[:], b_jk2)
                P.op("dve", lambda e: e.tensor_scalar(out=xn2[t % 2][:], in0=xt2[xs][:], scalar1=r, scalar2=None,
                                                     op0=ALU.mult),
                     reads=[b_xt2[xs], b_st2[sti]], writes=[b_xn2[t % 2]])

            def A1(t):
                s2 = t % 2
                xs = t % NS
                transposes(xn2[s2], b_xn2[s2], hTt[s2][:], b_hTt[s2], ntp())
                transposes(None, b_yt2[s2], yTt[s2][:], b_yTt[s2], ntp(),
                           chunk_fn=lambda k: yt2[s2][:, k // 2, (k % 2) * 128:(k % 2) * 128 + 128])
                for br in range(2):
                    bi = nbig()
                    for cg in range(2):
                        for k in range(8):
                            P.op("pe", lambda e, k=k, cg=cg, bi=bi, br=br: e.matmul(
                                big[bi][:, cg * 512:(cg + 1) * 512], lhsT=hTt[s2][:, k, :],
                                rhs=wg[:, k, br * 1024 + cg * 512: br * 1024 + (cg + 1) * 512],
                                start=(k == 0), stop=(k == 7)),
                                reads=[b_hTt[s2], b_wg], writes=[b_big[bi]])
                    P.op("act", lambda e, bi=bi, br=br: e.activation(out=Gs[s2][:, br * 1024:(br + 1) * 1024], in_=big[bi][:],
                                                                     func=AF.Sigmoid),
                         reads=[b_big[bi]], writes=[b_Gs[s2][br]])
                bi = nbig()
                for cg in range(2):
                    for r in range(4):
                        P.op("pe", lambda e, r=r, cg=cg, bi=bi: e.matmul(
                            big[bi][:, cg * 512:(cg + 1) * 512], lhsT=yTt[s2][:, 2 * r, :],
                            rhs=wbs[:, r, cg * 512:(cg + 1) * 512], start=(r == 0), stop=(r == 3)),
                            reads=[b_yTt[s2], b_wA], writes=[b_big[bi]])
                P.op("dve", lambda e, bi=bi: e.tensor_tensor(out=m1[s2][:], in0=big[bi][:], in1=Gs[s2][:, 0:1024], op=ALU.mult),
                     reads=[b_big[bi], b_Gs[s2][0]], writes=[b_m1[s2]])
                bi = nbig()
                for cg in range(2):
                    for r in range(4):
                        P.op("pe", lambda e, r=r, cg=cg, bi=bi: e.matmul(
                            big[bi][:, cg * 512:(cg + 1) * 512], lhsT=yTt[s2][:, 2 * r + 1, :],
                            rhs=wbf_[:, r, cg * 512:(cg + 1) * 512], start=(r == 0), stop=(r == 3)),
                            reads=[b_yTt[s2], b_wA], writes=[b_big[bi]])
                P.op("dve", lambda e, bi=bi: e.tensor_tensor(out=t1[s2][:], in0=big[bi][:], in1=Gs[s2][:, 1024:2048], op=ALU.mult),
                     reads=[b_big[bi], b_Gs[s2][1]], writes=[b_t1[s2]])
                P.op("pool", lambda e: e.tensor_tensor(out=mx[s2][:], in0=m1[s2][:], in1=t1[s2][:], op=ALU.add),
                     reads=[b_m1[s2], b_t1[s2]], writes=[b_mx[s2]])

            def A2(t):
                s2 = t % 2
                xs = t % NS
                transposes(mx[s2], b_mx[s2], mTt[s2][:], b_mTt[s2], ntp())
                bi = nbig()
                for cg in range(2):
                    for k in range(8):
                        P.op("pe", lambda e, k=k, cg=cg, bi=bi: e.matmul(
                            big[bi][:, cg * 512:(cg + 1) * 512], lhsT=mTt[s2][:, k, :],
                            rhs=wo[:, k, cg * 512:(cg + 1) * 512], start=(k == 0), stop=(k == 7)),
                            reads=[b_mTt[s2], b_wA], writes=[b_big[bi]])
                sti = (2 * t + 1) % 6
                r = rstd_chain(big[bi][:], [b_big[bi]], sti, 0, jk2[:], b_jk2)
                P.op("dve", lambda e, bi=bi: e.scalar_tensor_tensor(out=t1[s2][:], in0=big[bi][:], scalar=r, in1=g_post[:],
                                                                   op0=ALU.mult, op1=ALU.mult),
                     reads=[b_big[bi], b_st2[sti], b_par], writes=[b_t1[s2]])
                P.op("pool", lambda e: e.tensor_tensor(out=x1t[s2][:], in0=xt2[xs][:], in1=t1[s2][:], op=ALU.add),
                     reads=[b_xt2[xs], b_t1[s2]], writes=[b_x1t[s2]])
                P.op("sp", lambda e: e.dma_start(out=x1buf[t * 128:(t + 1) * 128, :], in_=x1t[s2][:]),
                     reads=[b_x1t[s2]], writes=[b_x1buf[t]], dma="x1st%d" % s2)
                sti2 = (2 * t) % 6
                r2 = rstd_chain(x1t[s2][:], [b_x1t[s2]], sti2, 0, jk2[:], b_jk2)
                if t == 0:
                    P.op("dve", lambda e: e.tensor_scalar(out=h2b[s2][:], in0=x1t[s2][:], scalar1=r2, scalar2=flag[:, 0:1],
                                                         op0=ALU.mult, op1=ALU.mult),
                         reads=[b_x1t[s2], b_st2[sti2], b_par], writes=[b_h2b[s2]])
                else:
                    P.op("dve", lambda e: e.tensor_scalar(out=h2b[s2][:], in0=x1t[s2][:], scalar1=r2, scalar2=None,
                                                         op0=ALU.mult),
                         reads=[b_x1t[s2], b_st2[sti2]], writes=[b_h2b[s2]])
                transposes(h2b[s2], b_h2b[s2], h2Tt[s2][:], b_h2Tt[s2], ntp())
                P.op("sp", lambda e: e.dma_start(out=h2buf[:, :, t * 128:(t + 1) * 128], in_=h2Tt[s2][:]),
                     reads=[b_h2Tt[s2]], writes=[b_h2buf[t]], dma="h2st%d" % s2)

            A0(0)
            for t in range(NT2):
                if t + 1 < NT2:
                    A0(t + 1)
                A1(t)
                A2(t)

        P.barrier()
        P.phase = 4
        actT = p2.enter_context(nc.sbuf_tensor("actT", [128, NFF, TOKC], BF16))
        b_actT = [Buf() for _ in range(NFF)]
        with ExitStack() as pU:
            def sbU(name, shape, dt):
                return pU.enter_context(nc.sbuf_tensor(name, list(shape), dt))

            def psU(name, shape, dt):
                return pU.enter_context(nc.psum_tensor(name, list(shape), dt))

            h2T = sbU("h2T", [128, 8, TOK2], BF16)
            wus = [sbU("wus%d" % i, [128, 8, 2, 128], F32) for i in range(2)]
            b_wus = [Buf() for _ in range(2)]
            wub = [sbU("wub%d" % i, [128, 8, 2, 128], BF16) for i in range(2)]
            b_wub = [Buf() for _ in range(2)]
            ub = [sbU("ub%d" % i, [128, TOK2 + 2], F32) for i in range(2)]
            b_ub = [Buf() for _ in range(2)]
            ac = [sbU("ac%d" % i, [128, TOK2], F32) for i in range(2)]
            b_ac = [Buf() for _ in range(2)]
            ups = [psU("ups%d" % i, [128, 512], F32) for i in range(8)]
            b_ups = [Buf() for _ in range(8)]
            upc = {"n": 0}
            for i in range(2):
                P.op("pool", lambda e, i=i: e.memset(ub[i][:, 0:2], 0.0), writes=[b_ub[i]])
            wup_v = w_up.rearrange("(k p) (t f) -> p k t f", p=128, t=2)
            cgs = []
            c0_ = 0
            while c0_ < TOK2:
                w_ = min(512, TOK2 - c0_)
                cgs.append((c0_, w_))
                c0_ += w_

            b_h2Tg = [Buf() for _ in cgs]
            for gi, (cc0, cw_) in enumerate(cgs):
                P.op("sp", lambda e, cc0=cc0, cw_=cw_: e.dma_start(out=h2T[:, :, cc0:cc0 + cw_], in_=h2buf[:, :, cc0:cc0 + cw_]),
                     reads=b_h2buf[cc0 // 128:(cc0 + cw_) // 128], writes=[b_h2Tg[gi]], dma="all:h2ld")

            def U0(c):
                s_ = c % 2
                for part in range(2):
                    P.op("sp", lambda e, part=part: e.dma_start(out=wus[s_][:, :, part, :],
                                                               in_=wup_v[:, :, part, c * 128:(c + 1) * 128]),
                         writes=[b_wus[s_]], dma="wus%d" % s_, bar=c)
                for k in range(8):
                    P.op("pool", lambda e, k=k: e.tensor_scalar(out=wub[s_][:, k, :, :], in0=wus[s_][:, k, :, :],
                                                               scalar1=gfp[:, k:k + 1], scalar2=None, op0=ALU.mult),
                         reads=[b_wus[s_], b_par], writes=[b_wub[s_]])

            def U1(c):
                s_ = c % 2
                for part in range(2):
                    fi = part * NFF + c
                    for gi, (cc0, cw_) in enumerate(cgs):
                        pi = upc["n"] % 8
                        upc["n"] += 1
                        for k in range(8):
                            P.op("pe", lambda e, k=k, pi=pi, cc0=cc0, cw_=cw_, part=part: e.matmul(
                                ups[pi][:, 0:cw_], lhsT=wub[s_][:, k, part, :], rhs=h2T[:, k, cc0:cc0 + cw_],
                                start=(k == 0), stop=(k == 7)),
                                reads=[b_wub[s_], b_h2Tg[gi]], writes=[b_ups[pi]])
                        P.op("act", lambda e, pi=pi, cc0=cc0, cw_=cw_, part=part: e.activation(
                            out=ub[part][:, 2 + cc0:2 + cc0 + cw_], in_=ups[pi][:, 0:cw_], func=AF.Copy),
                            reads=[b_ups[pi]], writes=[b_ub[part]])
                    P.op("pool", lambda e, part=part, fi=fi: e.tensor_scalar(
                        out=ac[part][:], in0=ub[part][:, 2:TOK2 + 2], scalar1=cw[:, fi, 2:3], scalar2=cb[:, fi:fi + 1],
                        op0=ALU.mult, op1=ALU.add),
                        reads=[b_ub[part], b_par], writes=[b_ac[part]])
                    P.op("dve", lambda e, part=part, fi=fi: e.scalar_tensor_tensor(
                        out=ac[part][:], in0=ub[part][:, 1:TOK2 + 1], scalar=cw[:, fi, 1:2], in1=ac[part][:],
                        op0=ALU.mult, op1=ALU.add),
                        reads=[b_ub[part], b_par, b_ac[part]], writes=[b_ac[part]])
                    P.op("dve", lambda e, part=part, fi=fi: e.scalar_tensor_tensor(
                        out=ac[part][:], in0=ub[part][:, 0:TOK2], scalar=cw[:, fi, 0:1], in1=ac[part][:],
                        op0=ALU.mult, op1=ALU.add),
                        reads=[b_ub[part], b_par, b_ac[part]], writes=[b_ac[part]])
                P.op("act", lambda e: e.activation(out=ac[0][:, HALO:], in_=ac[0][:, HALO:], func=AF.Gelu_apprx_tanh),
                     reads=[b_ac[0]], writes=[b_ac[0]])
                P.op("dve", lambda e: e.tensor_tensor(out=actT[:, c, :], in0=ac[0][:, HALO:], in1=ac[1][:, HALO:], op=ALU.mult),
                     reads=[b_ac[0], b_ac[1]], writes=[b_actT[c]])

            U0(0)
            for c in range(NFF):
                if c + 1 < NFF:
                    U0(c + 1)
                U1(c)

    P.barrier()
    P.phase = 5
    out_ops = []
    with ExitStack() as pD:
        def sbD_(name, shape, dt):
            return pD.enter_context(nc.sbuf_tensor(name, list(shape), dt))

        def psD(name, shape, dt):
            return pD.enter_context(nc.psum_tensor(name, list(shape), dt))

        wd = sbD_("wd", [128, NFF, D], BF16)
        wpg = sbD_("wpg", [128, 8, D], BF16)
        wpl = sbD_("wpl", [128, 2, D], BF16)
        b_wD = Buf()
        P.op("pool", lambda e: e.dma_start(out=wd[:], in_=w_down.rearrange("(k p) c -> p k c", p=128)),
             writes=[b_wD], dma="all:wD")
        P.op("pool", lambda e: e.dma_start(out=wpg[:], in_=w_pg.rearrange("(k p) c -> p k c", p=128)),
             writes=[b_wD], dma="all:wD")
        P.op("pool", lambda e: e.dma_start(out=wpl[:], in_=w_ple.rearrange("(k p) c -> p k c", p=128)),
             writes=[b_wD], dma="all:wD")
        x1r = [sbD_("x1r%d" % i, [128, D], F32) for i in range(2)]
        b_x1r = [Buf() for _ in range(2)]
        jk3 = sbD_("jk3", [128, D], BF16)
        b_jk3 = Buf()
        st3 = [sbD_("st3_%d" % i, [128, 4], F32) for i in range(2)]
        b_st3 = [Buf() for _ in range(2)]
        t3 = [sbD_("t3_%d" % i, [128, D], F32) for i in range(1)] * 2
        b_t3 = [Buf()] * 2
        x2 = [sbD_("x2_%d" % i, [128, D], F32) for i in range(1)] * 2
        b_x2 = [Buf()] * 2
        x2b = [sbD_("x2b%d" % i, [128, D], BF16) for i in range(2)]
        b_x2b = [Buf() for _ in range(2)]
        x2T = [sbD_("x2T%d" % i, [128, 8, 128], BF16) for i in range(2)]
        b_x2T = [Buf() for _ in range(2)]
        ptl = [sbD_("ptl%d" % i, [128, PLE], F32) for i in range(2)]
        b_ptl = [Buf() for _ in range(2)]
        ptb = [sbD_("ptb%d" % i, [128, PLE], BF16) for i in range(2)]
        b_ptb = [Buf() for _ in range(2)]
        pT = [sbD_("pT%d" % i, [128, 2, 128], BF16) for i in range(2)]
        b_pT = [Buf() for _ in range(2)]
        sg = [sbD_("sg%d" % i, [128, D], F32) for i in range(1)] * 2
        b_sg = [Buf()] * 2
        ot = [sbD_("ot%d" % i, [128, D], F32) for i in range(2)]
        b_ot = [Buf() for _ in range(2)]
        tp3 = [psD("tp3_%d" % i, [128, 8, 128], BF16) for i in range(2)]
        b_tp3 = [Buf() for _ in range(2)]
        bg = [psD("bg%d" % i, [128, D], F32) for i in range(3)]
        b_bg = [Buf() for _ in range(3)]
        rot3 = {"n": 0, "t": 0}

        def nbg():
            i = rot3["n"] % 3
            rot3["n"] += 1
            return i

        def ntp3():
            rot3["t"] += 1
            return rot3["t"] % 2

        def D0(t):
            s2 = t % 2
            P.op("sp", lambda e: e.dma_start(out=x1r[s2][:], in_=x1buf[(t + 1) * 128:(t + 2) * 128, :]),
                 reads=[b_x1buf[t + 1]], writes=[b_x1r[s2]], dma="x1r%d" % s2)
            P.op("sp", lambda e: e.dma_start(out=ptl[s2][:], in_=p_own[(t + 1) * 128:(t + 2) * 128, :]),
                 writes=[b_ptl[s2]], dma="ptl%d" % s2)
            P.op("pool", lambda e: e.tensor_copy(out=ptb[s2][:], in_=ptl[s2][:]), reads=[b_ptl[s2]], writes=[b_ptb[s2]])

        def D1(t):
            s2 = t % 2
            bi = nbg()
            for cg in range(2):
                for c in range(NFF):
                    P.op("pe", lambda e, c=c, cg=cg, bi=bi: e.matmul(
                        bg[bi][:, cg * 512:(cg + 1) * 512], lhsT=actT[:, c, t * 128:(t + 1) * 128],
                        rhs=wd[:, c, cg * 512:(cg + 1) * 512], start=(c == 0), stop=(c == NFF - 1)),
                        reads=[b_actT[c], b_wD], writes=[b_bg[bi]])
            P.op("act", lambda e, bi=bi: e.activation(out=jk3[:], in_=bg[bi][:], func=AF.Square, accum_out=st3[s2][:, 0:1]),
                 reads=[b_bg[bi]], writes=[b_jk3, b_st3[s2]])
            P.op("dve", lambda e: e.tensor_scalar(out=st3[s2][:, 1:2], in0=st3[s2][:, 0:1], scalar1=1.0 / D, scalar2=EPS,
                                                 op0=ALU.mult, op1=ALU.add),
                 reads=[b_st3[s2]], writes=[b_st3[s2]])
            P.op("pool", lambda e: e.tensor_tensor(out=st3[s2][:, 2:3], in0=st3[s2][:, 1:2], in1=mh2[:], op=ALU.pow),
                 reads=[b_st3[s2], b_mh2], writes=[b_st3[s2]])
            P.op("dve", lambda e, bi=bi: e.scalar_tensor_tensor(out=t3[s2][:], in0=bg[bi][:], scalar=st3[s2][:, 2:3],
                                                               in1=g_fpost[:], op0=ALU.mult, op1=ALU.mult),
                 reads=[b_bg[bi], b_st3[s2], b_par], writes=[b_t3[s2]])
            P.op("pool", lambda e: e.tensor_tensor(out=x2[s2][:], in0=x1r[s2][:], in1=t3[s2][:], op=ALU.add),
                 reads=[b_x1r[s2], b_t3[s2]], writes=[b_x2[s2]])
            P.op("act", lambda e: e.activation(out=x2b[s2][:], in_=x2[s2][:], func=AF.Copy),
                 reads=[b_x2[s2]], writes=[b_x2b[s2]])
            ti = ntp3()
            for k in range(8):
                P.op("pe", lambda e, k=k, ti=ti: e.transpose(out=tp3[ti][:, k, :], in_=x2b[s2][:, k * 128:(k + 1) * 128],
                                                           identity=IDENT2),
                     reads=[b_x2b[s2], b_cst2], writes=[b_tp3[ti]])
            P.op("act", lambda e, ti=ti: e.activation(out=x2T[s2][:], in_=tp3[ti][:], func=AF.Copy),
                 reads=[b_tp3[ti]], writes=[b_x2T[s2]])
            ti = ntp3()
            for k in range(2):
                P.op("pe", lambda e, k=k, ti=ti: e.transpose(out=tp3[ti][:, k, :], in_=ptb[s2][:, k * 128:(k + 1) * 128],
                                                           identity=IDENT2),
                     reads=[b_ptb[s2], b_cst2], writes=[b_tp3[ti]])
            P.op("act", lambda e, ti=ti: e.activation(out=pT[s2][:], in_=tp3[ti][:, 0:2, :], func=AF.Copy),
                 reads=[b_tp3[ti]], writes=[b_pT[s2]])
            bi = nbg()
            for cg in range(2):
                for k in range(8):
                    P.op("pe", lambda e, k=k, cg=cg, bi=bi: e.matmul(
                        bg[bi][:, cg * 512:(cg + 1) * 512], lhsT=x2T[s2][:, k, :],
                        rhs=wpg[:, k, cg * 512:(cg + 1) * 512], start=(k == 0), stop=(k == 7)),
                        reads=[b_x2T[s2], b_wD], writes=[b_bg[bi]])
            P.op("act", lambda e, bi=bi: e.activation(out=sg[s2][:], in_=bg[bi][:], func=AF.Sigmoid),
                 reads=[b_bg[bi]], writes=[b_sg[s2]])
            bi = nbg()
            for cg in range(2):
                for k in range(2):
                    P.op("pe", lambda e, k=k, cg=cg, bi=bi: e.matmul(
                        bg[bi][:, cg * 512:(cg + 1) * 512], lhsT=pT[s2][:, k, :],
                        rhs=wpl[:, k, cg * 512:(cg + 1) * 512], start=(k == 0), stop=(k == 1)),
                        reads=[b_pT[s2], b_wD], writes=[b_bg[bi]])
            P.op("dve", lambda e, bi=bi: e.tensor_tensor(out=t3[s2][:], in0=bg[bi][:], in1=sg[s2][:], op=ALU.mult),
                 reads=[b_bg[bi], b_sg[s2]], writes=[b_t3[s2]])
            P.op("pool", lambda e: e.tensor_tensor(out=ot[s2][:], in0=x2[s2][:], in1=t3[s2][:], op=ALU.add),
                 reads=[b_x2[s2], b_t3[s2]], writes=[b_ot[s2]])
            o = P.op("sp", lambda e: e.dma_start(out=out_d[t * 128:(t + 1) * 128, :], in_=ot[s2][:]),
                     reads=[b_ot[s2]], dma="ot%d" % s2)
            out_ops.append(o)

        NTM = TOKC // 128
        D0(0)
        for t in range(NTM):
            if t + 1 < NTM:
                D0(t + 1)
            D1(t)

    p2.close()
    P.final = list(out_ops)
    P.emit(nc, es)
    es.close()
    return nc


def _consts():
    j = np.arange(128)[:, None]
    s = np.arange(128)[None, :]
    ident = (j == s)
    utri = (j > s)
    ones = np.ones((128, 128), bool)
    msb = (j < s)
    mfx = (j <= s)
    return np.stack([ident, utri, ones, msb, mfx], axis=1).astype(np.float32)


def _core_inputs(c, x, w_in, norm_attn_pre, b_forget, full=None):
    b, g = divmod(c, 4)
    wi = w_in[0]
    SS = x.shape[1]
    TOKC = SS // 4

    def hc(base, h):
        return slice(base + 64 * h, base + 64 * (h + 1))
    sbh = [2 * g, 2 * g + 1]
    q_sb = np.concatenate([wi[:, hc(0, h)] for h in sbh], 1)
    k_sb = np.concatenate([wi[:, hc(512, h)] for h in sbh], 1)
    v_sb = np.concatenate([wi[:, hc(1024, h)] for h in sbh], 1)
    q_fx = np.concatenate([wi[:, hc(1536, h)] for h in sbh], 1)
    k_fx = np.concatenate([wi[:, hc(2048, h)] for h in sbh], 1)
    v_fx = np.concatenate([wi[:, hc(2560, h)] for h in sbh], 1)
    f_l = wi[:, 3072 + 2 * g:3072 + 2 * g + 2]
    wqk = np.ascontiguousarray(np.concatenate([q_sb, q_fx, k_sb, k_fx], 1))
    wvf = np.ascontiguousarray(np.concatenate([v_sb, v_fx, f_l], 1))
    m = {
        "xb": np.ascontiguousarray(x[b]),
        "consts": _consts(),
        "wqk": wqk,
        "wvf": wvf,
        "gpre": np.ascontiguousarray(norm_attn_pre[0].reshape(8, 128).T),
        "bfg": np.ascontiguousarray(b_forget[0, 2 * g:2 * g + 2].reshape(2, 1)),
    }
    if full is not None:
        f = full
        x_own = np.zeros((TOKC + HALO, D), np.float32)
        p_own = np.zeros((TOKC + HALO, PLE), np.float32)
        lo = g * TOKC - HALO
        if lo >= 0:
            x_own[:] = x[b, lo:lo + TOKC + HALO]
            p_own[:] = f["p"][0, b, lo:lo + TOKC + HALO]
        else:
            x_own[HALO:] = x[b, 0:TOKC]
            p_own[HALO:] = f["p"][0, b, 0:TOKC]
        m.update({
            "x_own": x_own,
            "p_own": p_own,
            "w_gate": np.ascontiguousarray(wi[:, 3080:3080 + 2048]),
            "w_bsb": np.ascontiguousarray(f["w_branch_sb"][0]),
            "w_bfx": np.ascontiguousarray(f["w_branch_fox"][0]),
            "w_out": np.ascontiguousarray(f["w_out"][0]),
            "w_up": np.ascontiguousarray(f["w_up"][0]),
            "w_down": np.ascontiguousarray(f["w_down"][0]),
            "w_ple": np.ascontiguousarray(f["w_ple"][0]),
            "w_pg": np.ascontiguousarray(f["w_ple_gate"][0]),
            "g_post": np.ascontiguousarray(np.broadcast_to(f["norm_attn_post"][0][None, :], (128, D))),
            "g_fpost": np.ascontiguousarray(np.broadcast_to(f["norm_ffn_post"][0][None, :], (128, D))),
            "gfpre": np.ascontiguousarray(f["norm_ffn_pre"][0].reshape(8, 128).T),
            "cw": np.ascontiguousarray(f["conv_w"][0].reshape(3, 2 * NFF, 128).transpose(2, 1, 0)),
            "cb": np.ascontiguousarray(f["conv_b"][0].reshape(2 * NFF, 128).T),
            "flag": np.full((128, 1), 0.0 if g == 0 else 1.0, np.float32),
        })
    return m


_NC_CACHE = {}


def kernel(**inputs):
    inputs = {k: np.asarray(v, dtype=np.float32) for k, v in inputs.items()}
    x = inputs["x"]
    if "nc" not in _NC_CACHE:
        _NC_CACHE["nc"] = build()
    nc = _NC_CACHE["nc"]
    in_maps = [_core_inputs(c, x, inputs["w_in"], inputs["norm_attn_pre"], inputs["b_forget"], full=inputs)
               for c in range(8)]
    res = run_bass_kernel_spmd(nc, in_maps, core_ids=list(range(8)))
    TOKC = x.shape[1] // 4
    out = np.empty_like(x)
    for c in range(8):
        b, g = divmod(c, 4)
        out[b, g * TOKC:(g + 1) * TOKC] = np.asarray(res.results[c]["out"])
    return out
```

```python
import numpy as np
from contextlib import ExitStack

import concourse.bass as bass
import concourse.mybir as mybir
from concourse.bass_utils import run_bass_kernel_spmd

F32 = mybir.dt.float32
BF16 = mybir.dt.bfloat16
AF = mybir.ActivationFunctionType
ALU = mybir.AluOpType

S = 8192
D = 1024
NBLK = S // 128
NGRP = S // 512
DFF = 2816
NFF = DFF // 128
PLE = 256
HALO = 128
EPS = 1e-6

ENGS = ("sp", "act", "dve", "pool", "pe")


class Buf:
    __slots__ = ("w", "r", "rd")

    def __init__(self):
        self.w = None
        self.r = {}
        self.rd = []


class Op:
    __slots__ = ("eng", "fn", "deps", "sem", "ticket", "dma", "ndep", "phase", "bar")


class Prog:
    def __init__(self):
        self.q = {e: [] for e in ENGS}
        self.phase = 0
        self.final = []
        self.pending = []

    def op(self, eng, fn, reads=(), writes=(), dma=None, extra=(), bar=None):
        o = Op()
        o.bar = bar
        o.eng = eng
        o.fn = fn
        o.dma = dma
        o.ndep = 0
        o.phase = self.phase
        o.sem = None
        o.ticket = 0
        deps = set()
        for b in reads:
            if b.w is not None:
                deps.add(b.w)
        for b in writes:
            if b.w is not None:
                deps.add(b.w)
            for r in b.r.values():
                deps.add(r)
            for r in b.rd:
                deps.add(r)
        for d in extra:
            if d is not None:
                deps.add(d)
        deps.discard(o)
        o.deps = deps
        for d in deps:
            d.ndep += 1
        for b in reads:
            if dma is not None:
                b.rd.append(o)
            else:
                b.r[eng] = o
        for b in writes:
            b.w = o
            b.r = {}
            b.rd = []
        self.q[eng].append(o)
        if dma is not None:
            self.pending.append(o)
        return o

    def barrier(self):
        deps = list(self.pending)
        for e in ENGS:
            for o in reversed(self.q[e]):
                if o.dma is None and o.fn is not None:
                    deps.append(o)
                    break
        self.pending = []
        for e in ENGS:
            self.op(e, None, extra=deps)

    def emit(self, nc, es):
        sems = {}

        def get_sem(key):
            if key not in sems:
                sems[key] = [es.enter_context(nc.semaphore("s_%s" % str(key).replace(":", "_"))), 0]
            return sems[key]

        all_total = {}
        for e in ENGS:
            for o in self.q[e]:
                if o.bar == "cc":
                    continue
                if o.dma is not None:
                    s = get_sem("d:" + o.dma)
                    s[1] += 16
                    o.sem = s[0]
                    o.ticket = s[1]
                    if o.dma.startswith("all:"):
                        all_total[o.dma] = s[1]
                elif o.ndep > 0:
                    s = get_sem("e:%s:%d" % (e, o.phase))
                    s[1] += 1
                    o.sem = s[0]
                    o.ticket = s[1]
        bar_max = {}
        for e in ENGS:
            for o in self.q[e]:
                if o.dma is not None and o.dma.startswith("all:"):
                    o.ticket = all_total[o.dma]
                if o.dma is not None and o.bar is not None:
                    kk = (o.dma, o.bar)
                    bar_max[kk] = max(bar_max.get(kk, 0), o.ticket)
        for e in ENGS:
            for o in self.q[e]:
                if o.dma is not None and o.bar is not None:
                    o.ticket = bar_max[(o.dma, o.bar)]
        final = list(self.final)
        q = self.q
        handles = {"sp": "sync", "act": "scalar", "dve": "vector", "pool": "gpsimd", "pe": "tensor"}

        def run(ename, eng):
            waited = {}
            for o in q[ename]:
                for d in o.deps:
                    if d.sem is None:
                        continue
                    if ename == "pe" and d.eng == "pe" and d.dma is None:
                        continue
                    if o.dma is not None and d.dma == o.dma and (
                            o.dma.startswith("all:") or (o.bar is not None and o.bar == d.bar)):
                        continue
                    k = id(d.sem)
                    if waited.get(k, 0) >= d.ticket:
                        continue
                    waited[k] = d.ticket
                    eng.wait_ge(d.sem, d.ticket)
                if o.fn is None:
                    continue
                ins = o.fn(eng)
                if o.bar == "cc":
                    ins.then_inc(o.sem)
                elif o.sem is not None:
                    ins.then_inc(o.sem, 16 if o.dma is not None else 1)
            if ename == "sp":
                for d in final:
                    eng.wait_ge(d.sem, d.ticket)

        with nc.Block() as block:
            for ename in ENGS:
                deco = getattr(block, handles[ename])

                def mk(ename):
                    def f(eng):
                        run(ename, eng)
                    return f
                deco(mk(ename))


def build(debug=None):
    nc = bass.Bass("TRN2", target_bir_lowering=False)
    es = ExitStack()
    P = Prog()

    def din(name, shape, dt=F32):
        return nc.dram_tensor(name, list(shape), dt, kind="ExternalInput").ap()

    xb = din("xb", [S, D])
    consts = din("consts", [128, 5, 128])
    wqk = din("wqk", [D, 512])
    wvf = din("wvf", [D, 258])
    gpre = din("gpre", [128, 8])
    bfg = din("bfg", [2, 1])
    TOKC = S // 4
    TOK2 = TOKC + HALO
    NT2 = TOK2 // 128
    if debug in ("p1", "p1a"):
        ybuf = nc.dram_tensor("ybuf", [S + HALO, 256], BF16, kind="ExternalOutput")
    else:
        ybuf = nc.dram_tensor("ybuf", [S + HALO, 256], BF16)
    ybuf_ap = ybuf.ap()
    full = debug not in ("p1", "p1a")
    if full:
        x_own = din("x_own", [TOK2, D])
        p_own = din("p_own", [TOK2, PLE])
        w_gate = din("w_gate", [D, 2048])
        w_bsb = din("w_bsb", [512, D])
        w_bfx = din("w_bfx", [512, D])
        w_out = din("w_out", [D, D])
        w_up = din("w_up", [D, 2 * DFF])
        w_down = din("w_down", [DFF, D])
        w_ple = din("w_ple", [PLE, D])
        w_pg = din("w_pg", [D, D])
        g_post_d = din("g_post", [128, D])
        g_fpost_d = din("g_fpost", [128, D])
        gpre_b_d = din("gpre_b", [128, D])
        gfpre_b_d = din("gfpre_b", [128, D])
        cw_d = din("cw", [128, 2 * NFF, 3])
        cb_d = din("cb", [128, 2 * NFF])
        flag_d = din("flag", [128, 1])
        out_d = nc.dram_tensor("out", [TOKC, D], F32, kind="ExternalOutput").ap()
        yall = nc.dram_tensor("yall", [8 * (S + HALO), 256], BF16)
        x1buf = nc.dram_tensor("x1buf", [TOK2, D], F32).ap()

    def sb(name, shape, dt):
        return es.enter_context(nc.sbuf_tensor(name, list(shape), dt))

    p1 = ExitStack()

    def sb1(name, shape, dt):
        return p1.enter_context(nc.sbuf_tensor(name, list(shape), dt))

    cst = sb1("cst", [128, 5, 128], BF16)
    IDENT, UTRI, ONES, MSB, MFX = (cst[:, i, :] for i in range(5))
    b_cst = Buf()
    QT_sb = sb1("QT_sb", [128, S], BF16)
    KT_sb = sb1("KT_sb", [128, S], BF16)
    QT_fx = [sb1("QT_fx%d" % h, [128, S], BF16) for h in range(2)]
    KT_fx = [sb1("KT_fx%d" % h, [128, S], BF16) for h in range(2)]
    V_sb = sb1("V_sb", [128, NBLK, 2, 64], BF16)
    V_fx = sb1("V_fx", [128, NBLK, 2, 66], BF16)
    b_QTsb = [Buf() for _ in range(NGRP)]
    b_KTsb = [Buf() for _ in range(NGRP)]
    b_QTfx = [[Buf() for _ in range(NGRP)] for _ in range(2)]
    b_KTfx = [[Buf() for _ in range(NGRP)] for _ in range(2)]
    b_QTfx_aug = [[[Buf() for _ in range(3)] for _ in range(NGRP)] for _ in range(2)]
    b_KTfx_aug = [[[Buf() for _ in range(3)] for _ in range(NGRP)] for _ in range(2)]
    b_Vsb = [Buf() for _ in range(NGRP)]
    b_Vfx = [Buf() for _ in range(NGRP)]
    b_Vfx_ones = Buf()
    b_aug_init = Buf()

    mhalf = sb1("mhalf", [128, 1], F32)
    b_mhalf = Buf()

    P.phase = 0
    with ExitStack() as pa:
        def sba(name, shape, dt):
            return pa.enter_context(nc.sbuf_tensor(name, list(shape), dt))

        def psa(name, shape, dt):
            return pa.enter_context(nc.psum_tensor(name, list(shape), dt))

        wstage = sba("wstage", [128, 4, 512], F32)
        b_wstage = Buf()
        wqk_bf = sba("wqk_bf", [128, 8, 512], BF16)
        b_wqk = Buf()
        wvf_bf = sba("wvf_bf", [128, 8, 258], BF16)
        b_wvf = Buf()
        gp = sba("gp", [128, 8], F32)
        b_gp = Buf()
        nb = sba("nb", [2, 1], F32)
        b_nb = Buf()
        NX = 2
        xt = [sba("xt%d" % i, [128, D], F32) for i in range(NX)]
        b_xt = [Buf() for _ in range(NX)]
        junk = [sba("junk%d" % i, [128, D], BF16) for i in range(1)]
        b_junk = [Buf() for _ in range(1)]
        ss = [sba("ss%d" % i, [128, 1], F32) for i in range(NX)]
        b_ss = [Buf() for _ in range(NX)]
        ms = [sba("ms%d" % i, [128, 1], F32) for i in range(NX)]
        b_ms = [Buf() for _ in range(NX)]
        rs = [sba("rs%d" % i, [128, 1], F32) for i in range(NX)]
        b_rs = [Buf() for _ in range(NX)]
        NXN = 5
        xn = [sba("xn%d" % i, [128, D], BF16) for i in range(NXN)]
        b_xn = [Buf() for _ in range(NXN)]
        hT = [sba("hT%d" % i, [128, 8, 512], BF16) for i in range(2)]
        b_hT = [Buf() for _ in range(2)]
        tp_ps = [psa("tp_ps%d" % i, [128, 8, 128], BF16) for i in range(2)]
        b_tp = [Buf() for _ in range(2)]
        NPJ = 4
        pj_ps = [psa("pj_ps%d" % i, [128, 512], F32) for i in range(NPJ)]
        b_pj = [Buf() for _ in range(NPJ)]
        fe = sba("fe", [2, 512], F32)
        b_fe = Buf()
        fl = fe
        b_fl = b_fe
        one2 = sba("one2", [2, 512], F32)
        b_one2 = Buf()
        cc = [sba("cc%d" % i, [2, 512], F32) for i in range(2)]
        b_cc = [Buf() for _ in range(2)]
        r1 = sba("r1", [2, 512], F32)
        b_r1 = Buf()
        r2 = r1
        b_r2 = b_r1
        cpart = [[sba("cp%d_%d" % (s_, i), [2, 512], BF16) for i in range(3)] for s_ in range(1)]
        npart = [[sba("np%d_%d" % (s_, i), [2, 512], BF16) for i in range(3)] for s_ in range(1)]
        b_cpart = [[Buf() for _ in range(3)] for _ in range(1)]
        b_npart = [[Buf() for _ in range(3)] for _ in range(1)]

        P.op("pool", lambda e: e.dma_start(out=cst[:], in_=consts), writes=[b_cst], dma="cstq")
        P.op("sp", lambda e: e.dma_start(out=gp[:], in_=gpre), writes=[b_gp], dma="all:setup")
        P.op("sp", lambda e: e.dma_start(out=nb[:], in_=bfg), writes=[b_nb], dma="all:setup")
        P.op("dve", lambda e: e.tensor_scalar(out=nb[:], in0=nb[:], scalar1=-1.0, scalar2=None, op0=ALU.mult),
             reads=[b_nb], writes=[b_nb])
        P.op("pool", lambda e: e.memset(mhalf[:], -0.5), writes=[b_mhalf])
        P.op("pool", lambda e: e.memset(one2[:], 1.0), writes=[b_one2])
        P.op("pool", lambda e: e.memset(V_fx[:, :, :, 64:66], 1.0), writes=[b_Vfx_ones])
        for h in range(2):
            P.op("pool", lambda e, h=h: e.memset(QT_fx[h][64:70, :], 1.0), writes=[b_aug_init])
            P.op("pool", lambda e, h=h: e.memset(KT_fx[h][64:70, :], 1.0), writes=[b_aug_init])
        wqk_v = wqk.rearrange("(k p) c -> p k c", p=128)
        wvf_v = wvf.rearrange("(k p) c -> p k c", p=128)
        for hf in range(2):
            P.op("sp", lambda e, hf=hf: e.dma_start(out=wstage[:], in_=wqk_v[:, 4 * hf:4 * hf + 4, :]),
                 writes=[b_wstage], dma="wstage")
            for kk in range(4):
                k = 4 * hf + kk
                P.op("dve", lambda e, k=k, kk=kk: e.tensor_scalar(out=wqk_bf[:, k, 0:256], in0=wstage[:, kk, 0:256],
                                                          scalar1=gp[:, k:k + 1], scalar2=0.125,
                                                          op0=ALU.mult, op1=ALU.mult),
                     reads=[b_wstage, b_gp], writes=[b_wqk])
                P.op("dve", lambda e, k=k, kk=kk: e.tensor_scalar(out=wqk_bf[:, k, 256:512], in0=wstage[:, kk, 256:512],
                                                          scalar1=gp[:, k:k + 1], scalar2=None, op0=ALU.mult),
                     reads=[b_wstage, b_gp], writes=[b_wqk])
        for hf in range(2):
            P.op("sp", lambda e, hf=hf: e.dma_start(out=wstage[:, :, 0:258], in_=wvf_v[:, 4 * hf:4 * hf + 4, :]),
                 writes=[b_wstage], dma="wstage")
            for kk in range(4):
                k = 4 * hf + kk
                P.op("dve", lambda e, k=k, kk=kk: e.tensor_scalar(out=wvf_bf[:, k, :], in0=wstage[:, kk, 0:258],
                                                          scalar1=gp[:, k:k + 1], scalar2=None, op0=ALU.mult),
                     reads=[b_wstage, b_gp], writes=[b_wvf])

        pj_ctr = [0]

        def next_pj():
            i = pj_ctr[0] % NPJ
            pj_ctr[0] += 1
            return i

        evac_ctr = [0]

        def evac_eng():
            evac_ctr[0] += 1
            return "act" if evac_ctr[0] % 2 == 0 else "dve"

        def copy_op(eng, out, in_, reads, writes):
            if eng == "act":
                return P.op("act", lambda e: e.activation(out=out, in_=in_, func=AF.Copy), reads=reads, writes=writes)
            return P.op("dve", lambda e: e.tensor_copy(out=out, in_=in_), reads=reads, writes=writes)

        def prep_tile(tt):
            xs = tt % NX
            ns = tt % NXN
            js = 0
            P.op("sp", lambda e: e.dma_start(out=xt[xs][:], in_=xb[tt * 128:(tt + 1) * 128, :]),
                 writes=[b_xt[xs]], dma="xt%d" % xs)
            P.op("act", lambda e: e.activation(out=junk[js][:], in_=xt[xs][:], func=AF.Square, accum_out=ss[xs][:]),
                 reads=[b_xt[xs]], writes=[b_junk[js], b_ss[xs]])
            P.op("dve", lambda e: e.tensor_scalar(out=ms[xs][:], in0=ss[xs][:], scalar1=1.0 / D, scalar2=EPS,
                                                 op0=ALU.mult, op1=ALU.add),
                 reads=[b_ss[xs]], writes=[b_ms[xs]])
            P.op("pool", lambda e: e.tensor_tensor(out=rs[xs][:], in0=ms[xs][:], in1=mhalf[:], op=ALU.pow),
                 reads=[b_ms[xs], b_mhalf], writes=[b_rs[xs]])
            P.op("dve", lambda e: e.tensor_scalar(out=xn[ns][:], in0=xt[xs][:], scalar1=rs[xs][:, 0:1],
                                                 scalar2=None, op0=ALU.mult),
                 reads=[b_xt[xs], b_rs[xs]], writes=[b_xn[ns]])

        def transpose_tile(tt):
            G, j = divmod(tt, 4)
            ns = tt % NXN
            ts_ = tt % 2
            hs = G % 2
            for k in range(8):
                P.op("pe", lambda e, k=k: e.transpose(out=tp_ps[ts_][:, k, :], in_=xn[ns][:, k * 128:(k + 1) * 128],
                                                     identity=IDENT),
                     reads=[b_xn[ns], b_cst], writes=[b_tp[ts_]])
            copy_op(evac_eng(), hT[hs][:, :, j * 128:(j + 1) * 128], tp_ps[ts_][:], [b_tp[ts_]], [b_hT[hs]])

        def project(G):
            hs = G % 2
            cols = slice(G * 512, (G + 1) * 512)
            outs = [
                (slice(0, 128), 128, QT_sb[:, cols], b_QTsb[G]),
                (slice(256, 384), 128, KT_sb[:, cols], b_KTsb[G]),
                (slice(128, 192), 64, QT_fx[0][0:64, cols], b_QTfx[0][G]),
                (slice(192, 256), 64, QT_fx[1][0:64, cols], b_QTfx[1][G]),
                (slice(384, 448), 64, KT_fx[0][0:64, cols], b_KTfx[0][G]),
                (slice(448, 512), 64, KT_fx[1][0:64, cols], b_KTfx[1][G]),
            ]
            for (ws, M, dst, bdst) in outs:
                pi = next_pj()
                for k in range(8):
                    P.op("pe", lambda e, k=k, ws=ws, M=M, pi=pi: e.matmul(
                        pj_ps[pi][0:M, :], lhsT=wqk_bf[:, k, ws], rhs=hT[hs][:, k, :],
                        start=(k == 0), stop=(k == 7)),
                        reads=[b_wqk, b_hT[hs]], writes=[b_pj[pi]])
                copy_op(evac_eng(), dst, pj_ps[pi][0:M, :], [b_pj[pi]], [bdst])
            pi = next_pj()
            for k in range(8):
                P.op("pe", lambda e, k=k, pi=pi: e.matmul(
                    pj_ps[pi][0:2, :], lhsT=wvf_bf[:, k, 256:258], rhs=hT[hs][:, k, :],
                    start=(k == 0), stop=(k == 7)),
                    reads=[b_wvf, b_hT[hs]], writes=[b_pj[pi]])
            P.op("act", lambda e, pi=pi: e.activation(out=fe[:], in_=pj_ps[pi][0:2, :], func=AF.Exp,
                                                     scale=-1.0, bias=nb[:, 0:1]),
                 reads=[b_pj[pi], b_nb], writes=[b_fe])
            P.op("act", lambda e: e.activation(out=fl[:], in_=fe[:], func=AF.Ln, bias=1.0, scale=1.0),
                 reads=[b_fe], writes=[b_fl])
            cs = G % 2
            if G == 0:
                P.op("dve", lambda e: e.tensor_tensor_scan(out=cc[cs][:], data0=one2[:], data1=fl[:], initial=0.0,
                                                          op0=ALU.mult, op1=ALU.subtract),
                     reads=[b_one2, b_fl], writes=[b_cc[cs]])
            else:
                P.op("dve", lambda e: e.tensor_tensor_scan(out=cc[cs][:], data0=one2[:], data1=fl[:],
                                                          initial=cc[1 - cs][:, 511:512],
                                                          op0=ALU.mult, op1=ALU.subtract),
                     reads=[b_one2, b_fl, b_cc[1 - cs]], writes=[b_cc[cs]])
            ps_ = 0
            cp, npp = cpart[ps_], npart[ps_]
            bcp, bnp = b_cpart[ps_], b_npart[ps_]
            P.op("dve", lambda e: e.tensor_copy(out=cp[0][:], in_=cc[cs][:]), reads=[b_cc[cs]], writes=[bcp[0]])
            P.op("dve", lambda e: e.tensor_tensor(out=r1[:], in0=cc[cs][:], in1=cp[0][:], op=ALU.subtract),
                 reads=[b_cc[cs], bcp[0]], writes=[b_r1])
            P.op("dve", lambda e: e.tensor_copy(out=cp[1][:], in_=r1[:]), reads=[b_r1], writes=[bcp[1]])
            P.op("dve", lambda e: e.tensor_tensor(out=r2[:], in0=r1[:], in1=cp[1][:], op=ALU.subtract),
                 reads=[b_r1, bcp[1]], writes=[b_r2])
            P.op("dve", lambda e: e.tensor_copy(out=cp[2][:], in_=r2[:]), reads=[b_r2], writes=[bcp[2]])
            for i in range(3):
                P.op("dve", lambda e, i=i: e.tensor_scalar(out=npp[i][:], in0=cp[i][:], scalar1=-1.0, scalar2=None,
                                                          op0=ALU.mult),
                     reads=[bcp[i]], writes=[bnp[i]])
            for h in range(2):
                for i in range(3):
                    P.op("sp", lambda e, h=h, i=i: e.dma_start(out=QT_fx[h][64 + i:65 + i, cols], in_=cp[i][h:h + 1, :]),
                         reads=[bcp[i], b_aug_init], writes=[b_QTfx_aug[h][G][i]], dma="augq%d" % ps_, bar=G)
                    P.op("sp", lambda e, h=h, i=i: e.dma_start(out=KT_fx[h][67 + i:68 + i, cols], in_=npp[i][h:h + 1, :]),
                         reads=[bnp[i], b_aug_init], writes=[b_KTfx_aug[h][G][i]], dma="augk%d" % ps_, bar=G)
            for j in range(4):
                tt = 4 * G + j
                pi = next_pj()
                for k in range(8):
                    P.op("pe", lambda e, k=k, pi=pi, j=j: e.matmul(
                        pj_ps[pi][:, 0:256], lhsT=hT[hs][:, k, j * 128:(j + 1) * 128], rhs=wvf_bf[:, k, 0:256],
                        start=(k == 0), stop=(k == 7)),
                        reads=[b_wvf, b_hT[hs]], writes=[b_pj[pi]])
                copy_op(evac_eng(), V_sb[:, tt, :, :],
                        pj_ps[pi][:, 0:128].rearrange("p (h d) -> p h d", h=2), [b_pj[pi]], [b_Vsb[G]])
                copy_op(evac_eng(), V_fx[:, tt, :, 0:64],
                        pj_ps[pi][:, 128:256].rearrange("p (h d) -> p h d", h=2), [b_pj[pi], b_Vfx_ones], [b_Vfx[G]])

        for j in range(4):
            prep_tile(j)
        for G in range(NGRP):
            for j in range(4):
                transpose_tile(4 * G + j)
                if G + 1 < NGRP:
                    prep_tile(4 * (G + 1) + j)
            project(G)

    P.barrier()
    ydma_ops = []
    tiles_per_head = []
    for qg in range(NGRP):
        for kb in range(4 * qg + 3, -1, -1):
            i = kb - 4 * qg
            tiles_per_head.append((qg, kb, i if i >= 0 else None))

    if debug == "p1a":
        tiles_per_head = []
    with ExitStack() as pb:
        def sbb(name, shape, dt):
            return pb.enter_context(nc.sbuf_tensor(name, list(shape), dt))

        def psb(name, shape, dt):
            return pb.enter_context(nc.psum_tensor(name, list(shape), dt))

        z_ps = [psb("z_ps%d" % i, [128, 512], F32) for i in range(2)]
        b_z = [Buf() for _ in range(2)]
        bt_ps = [psb("bt_ps%d" % i, [128, 512], F32) for i in range(2)]
        b_bt = [Buf() for _ in range(2)]
        acc_ps = [psb("acc_ps%d" % i, [128, 4, 128], F32) for i in range(2)]
        b_acc = [Buf() for _ in range(2)]
        e_sb = [sbb("e_sb%d" % i, [128, 512], F32) for i in range(2)]
        b_e = [Buf() for _ in range(2)]
        l_sb = [sbb("l_sb%d" % i, [128, 512], F32) for i in range(3)]
        b_l = [Buf() for _ in range(3)]
        lom = [sbb("lom%d" % i, [128, 512], BF16) for i in range(3)]
        b_lom = [Buf() for _ in range(3)]
        Ls = [[sbb("Ls%d_%d" % (a, b_), [128, 512], BF16) for b_ in range(2)] for a in range(2)]
        b_Ls = [[Buf() for _ in range(2)] for _ in range(2)]
        arg = [sbb("arg%d" % i, [128, 512], F32) for i in range(2)]
        b_arg = [Buf() for _ in range(2)]
        wt = [sbb("wt%d" % i, [128, 512], BF16) for i in range(3)]
        b_wt = [Buf() for _ in range(3)]
        yst = [sbb("yst%d" % i, [128, 4, 64], BF16) for i in range(3)]
        b_yst = [Buf() for _ in range(3)]
        rc = [sbb("rc%d" % i, [128, 4], F32) for i in range(2)]
        b_rc = [Buf() for _ in range(2)]
        yst_ctr = [0]

        ybv = ybuf_ap[HALO:, :].rearrange("(q j p) c -> q p j c", j=4, p=128)

        P.phase = 1
        sb_tiles = [(hh,) + t for hh in range(2) for t in tiles_per_head]
        NTL = len(sb_tiles)

        def cr(t):
            i = t[3]
            return (0 if i is None else 128 * i)

        def sbA(n):
            hh, qg, kb, i = sb_tiles[n]
            c0 = cr(sb_tiles[n])
            hp = slice(64 * hh, 64 * hh + 64)
            zb = n % 2
            P.op("pe", lambda e: e.matmul(z_ps[zb][:, c0:512], lhsT=KT_sb[hp, kb * 128:(kb + 1) * 128],
                                         rhs=QT_sb[hp, qg * 512 + c0:(qg + 1) * 512], start=True, stop=True),
                 reads=[b_KTsb[kb // 4], b_QTsb[qg]], writes=[b_z[zb]])

        def sbB1(n):
            c0 = cr(sb_tiles[n])
            zb, eb = n % 2, n % 2
            P.op("act", lambda e: e.activation(out=e_sb[eb][:, c0:512], in_=z_ps[zb][:, c0:512], func=AF.Exp, scale=-1.0),
                 reads=[b_z[zb]], writes=[b_e[eb]])

        def sbB2(n):
            c0 = cr(sb_tiles[n])
            eb, lb = n % 2, n % 3
            P.op("act", lambda e: e.activation(out=l_sb[lb][:, c0:512], in_=e_sb[eb][:, c0:512], func=AF.Ln,
                                               bias=1.0, scale=1.0),
                 reads=[b_e[eb]], writes=[b_l[lb]])

        def sbC(n):
            hh, qg, kb, i = sb_tiles[n]
            c0 = cr(sb_tiles[n])
            zb, lb, mb = n % 2, n % 3, n % 3
            P.op("dve", lambda e: e.scalar_tensor_tensor(out=lom[mb][:, c0:512], in0=z_ps[zb][:, c0:512], scalar=-1.0,
                                                        in1=l_sb[lb][:, c0:512], op0=ALU.mult, op1=ALU.subtract),
                 reads=[b_z[zb], b_l[lb]], writes=[b_lom[mb]])
            if i is not None:
                P.op("pool", lambda e: e.tensor_tensor(out=lom[mb][:, c0:c0 + 128], in0=lom[mb][:, c0:c0 + 128],
                                                      in1=MSB, op=ALU.mult),
                     reads=[b_lom[mb], b_cst], writes=[b_lom[mb]])

        def sbD(n):
            hh, qg, kb, i = sb_tiles[n]
            c0 = cr(sb_tiles[n])
            mb, bb = n % 3, n % 2
            gp_ = qg % 2
            first = (i == 3)
            if first:
                for b_ in range(2):
                    P.op("pool", lambda e, b_=b_: e.memset(Ls[gp_][b_][:], 0.0), writes=[b_Ls[gp_][b_]])
            k_in_grp = (4 * qg + 3) - kb
            old, new = k_in_grp % 2, 1 - (k_in_grp % 2)
            P.op("pe", lambda e: e.matmul(bt_ps[bb][:, c0:512], lhsT=UTRI, rhs=lom[mb][:, c0:512], start=True, stop=False),
                 reads=[b_lom[mb], b_cst], writes=[b_bt[bb]])
            P.op("pe", lambda e: e.matmul(bt_ps[bb][:, c0:512], lhsT=ONES, rhs=Ls[gp_][old][:, c0:512], start=False, stop=True),
                 reads=[b_Ls[gp_][old], b_cst], writes=[b_bt[bb]])
            if kb > 0:
                P.op("pool", lambda e: e.tensor_tensor(out=Ls[gp_][new][:, c0:512], in0=Ls[gp_][old][:, c0:512],
                                                      in1=lom[mb][:, c0:512], op=ALU.add),
                     reads=[b_Ls[gp_][old], b_lom[mb]], writes=[b_Ls[gp_][new]])

        def sbE(n):
            c0 = cr(sb_tiles[n])
            bb, lb, ab = n % 2, n % 3, n % 2
            P.op("dve", lambda e: e.tensor_tensor(out=arg[ab][:, c0:512], in0=bt_ps[bb][:, c0:512],
                                                 in1=l_sb[lb][:, c0:512], op=ALU.subtract),
                 reads=[b_bt[bb], b_l[lb]], writes=[b_arg[ab]])

        def sbF(n):
            hh, qg, kb, i = sb_tiles[n]
            c0 = cr(sb_tiles[n])
            ab, wb = n % 2, n % 3
            P.op("act", lambda e: e.activation(out=wt[wb][:, c0:512], in_=arg[ab][:, c0:512], func=AF.Exp),
                 reads=[b_arg[ab]], writes=[b_wt[wb]])
            if i is not None:
                P.op("pool", lambda e: e.tensor_tensor(out=wt[wb][:, c0:c0 + 128], in0=wt[wb][:, c0:c0 + 128],
                                                      in1=MSB, op=ALU.mult),
                     reads=[b_wt[wb], b_cst], writes=[b_wt[wb]])

        def emit_y(hc, qg, src_fn, reads):
            ys = yst_ctr[0] % 3
            yst_ctr[0] += 1
            src_fn(ys)
            o = P.op("sp", lambda e: e.dma_start(out=ybv[qg][:, :, hc * 64:(hc + 1) * 64], in_=yst[ys][:]),
                     reads=[b_yst[ys]], dma="yst%d" % ys)
            ydma_ops.append(o)

        def sbG(n):
            hh, qg, kb, i = sb_tiles[n]
            wb = n % 3
            ab_ = qg % 2
            j0 = 0 if i is None else i
            for j in range(j0, 4):
                first = (i == 3 and j == 3)
                P.op("pe", lambda e, j=j, first=first: e.matmul(
                    acc_ps[ab_][:, j, 0:64], lhsT=wt[wb][:, j * 128:(j + 1) * 128], rhs=V_sb[:, kb, hh, :],
                    start=first, stop=(kb == 0), skip_group_check=True),
                    reads=[b_wt[wb], b_Vsb[kb // 4]], writes=[b_acc[ab_]])
            if kb == 0:
                def src(ys):
                    P.op("dve", lambda e: e.tensor_copy(out=yst[ys][:], in_=acc_ps[ab_][:, :, 0:64]),
                         reads=[b_acc[ab_]], writes=[b_yst[ys]])
                emit_y(hh, qg, src, None)

        for s_ in range(NTL + 4):
            if s_ < NTL:
                sbA(s_)
                sbB1(s_)
            if 0 <= s_ - 3 < NTL:
                sbF(s_ - 3)
            if s_ < NTL:
                sbB2(s_)
            if 0 <= s_ - 1 < NTL:
                sbC(s_ - 1)
            if 0 <= s_ - 2 < NTL:
                sbD(s_ - 2)
                sbE(s_ - 2)
            if 0 <= s_ - 4 < NTL:
                sbG(s_ - 4)

        P.phase = 2
        fx_tiles = sb_tiles

        def fxA(n):
            hh, qg, kb, i = fx_tiles[n]
            c0 = cr(fx_tiles[n])
            zb = n % 2
            P.op("pe", lambda e: e.matmul(z_ps[zb][:, c0:512], lhsT=KT_fx[hh][0:70, kb * 128:(kb + 1) * 128],
                                         rhs=QT_fx[hh][0:70, qg * 512 + c0:(qg + 1) * 512], start=True, stop=True),
                 reads=[b_KTfx[hh][kb // 4], b_QTfx[hh][qg], b_aug_init] + b_KTfx_aug[hh][kb // 4] + b_QTfx_aug[hh][qg],
                 writes=[b_z[zb]])

        def fxB(n):
            hh, qg, kb, i = fx_tiles[n]
            c0 = cr(fx_tiles[n])
            zb, wb = n % 2, n % 3
            P.op("act", lambda e: e.activation(out=wt[wb][:, c0:512], in_=z_ps[zb][:, c0:512], func=AF.Exp),
                 reads=[b_z[zb]], writes=[b_wt[wb]])
            if i is not None:
                P.op("pool", lambda e: e.tensor_tensor(out=wt[wb][:, c0:c0 + 128], in0=wt[wb][:, c0:c0 + 128],
                                                      in1=MFX, op=ALU.mult),
                     reads=[b_wt[wb], b_cst], writes=[b_wt[wb]])

        def fxG(n):
            hh, qg, kb, i = fx_tiles[n]
            wb = n % 3
            ab_ = qg % 2
            j0 = 0 if i is None else i
            for j in range(j0, 4):
                first = (i == 3 and j == 3)
                P.op("pe", lambda e, j=j, first=first: e.matmul(
                    acc_ps[ab_][:, j, 0:65], lhsT=wt[wb][:, j * 128:(j + 1) * 128], rhs=V_fx[:, kb, hh, 0:65],
                    start=first, stop=(kb == 0), skip_group_check=True),
                    reads=[b_wt[wb], b_Vfx[kb // 4], b_Vfx_ones], writes=[b_acc[ab_]])
            if kb == 0:
                rb = qg % 2

                def src(ys):
                    P.op("dve", lambda e: e.reciprocal(out=rc[rb][:], in_=acc_ps[ab_][:, :, 64]),
                         reads=[b_acc[ab_]], writes=[b_rc[rb]])
                    for j in range(4):
                        P.op("dve", lambda e, j=j: e.tensor_scalar(out=yst[ys][:, j, :], in0=acc_ps[ab_][:, j, 0:64],
                                                                  scalar1=rc[rb][:, j:j + 1], scalar2=None, op0=ALU.mult),
                             reads=[b_acc[ab_], b_rc[rb]], writes=[b_yst[ys]])
                emit_y(2 + hh, qg, src, None)

        for s_ in range(NTL + 2):
            if s_ < NTL:
                fxA(s_)
            if 0 <= s_ - 1 < NTL:
                fxB(s_ - 1)
            if 0 <= s_ - 2 < NTL:
                fxG(s_ - 2)

    p1.close()
    if not full:
        P.final = list(ydma_ops)
        P.emit(nc, es)
        es.close()
        return nc

    P.barrier()
    P.phase = 3
    p2 = ExitStack()

    def sb2(name, shape, dt):
        return p2.enter_context(nc.sbuf_tensor(name, list(shape), dt))

    cst2 = sb2("cst2", [128, 128], BF16)
    b_cst2 = Buf()
    zpad = sb2("zpad", [128, 256], BF16)
    b_zpad = Buf()
    mh2 = sb2("mh2", [128, 1], F32)
    b_mh2 = Buf()
    g_fpost = sb2("g_fpost_t", [128, D], F32)
    cw = sb2("cw_t", [128, 2 * NFF, 3], F32)
    cb = sb2("cb_t", [128, 2 * NFF], F32)
    flag = sb2("flag_t", [128, 1], F32)
    b_par = Buf()
    P.op("pool", lambda e: e.memset(zpad[:], 0.0), writes=[b_zpad])
    P.op("pool", lambda e: e.memset(mh2[:], -0.5), writes=[b_mh2])
    zp = P.op("sp", lambda e: e.dma_start(out=ybuf_ap[0:HALO, :], in_=zpad[:]), reads=[b_zpad], dma="zpad")
    P.op("pool", lambda e: e.dma_start(out=cst2[:], in_=consts[:, 0, :]), writes=[b_cst2], dma="cstq2")
    for (t_, d_) in ((g_fpost, g_fpost_d), (cw, cw_d), (cb, cb_d), (flag, flag_d)):
        P.op("sp", lambda e, t_=t_, d_=d_: e.dma_start(out=t_[:], in_=d_), writes=[b_par], dma="all:par")
    b_yall = Buf()
    cc_sem = es.enter_context(nc.semaphore("cc_sem"))

    def cc_fn(e):
        if debug == "nocc":
            return e.dma_start(out=yall.ap()[0:S + HALO, :], in_=ybuf_ap)
        return e.collective_compute("AllGather", ALU.bypass, replica_groups=[list(range(8))],
                                    ins=[ybuf_ap.opt()], outs=[yall.ap().opt()])
    cc_op = P.op("pool", cc_fn, writes=[b_yall], extra=list(ydma_ops) + [zp])
    if debug != "nocc":
        cc_op.sem = cc_sem
        cc_op.ticket = 1
        cc_op.bar = "cc"
        P.pending.append(cc_op)
    else:
        cc_op.dma = "ccdbg"
        P.pending.append(cc_op)

    IDENT2 = cst2[:]
    rot = {"n": 0}

    h2buf = nc.dram_tensor("h2buf", [128, 8, TOK2], BF16).ap()
    b_h2buf = [Buf() for _ in range(NT2)]
    b_x1buf = [Buf() for _ in range(NT2)]
    with ExitStack() as pA:
        def sbA_(name, shape, dt):
            return pA.enter_context(nc.sbuf_tensor(name, list(shape), dt))

        def psA_(name, shape, dt):
            return pA.enter_context(nc.psum_tensor(name, list(shape), dt))

        g_post = sbA_("g_post_t", [128, D], F32)
        gpb = sbA_("gpb_t", [128, D], F32)
        gfpb = sbA_("gfpb_t", [128, D], F32)
        b_parA = Buf()
        for (t_, d_) in ((g_post, g_post_d), (gpb, gpre_b_d), (gfpb, gfpre_b_d)):
            P.op("sp", lambda e, t_=t_, d_=d_: e.dma_start(out=t_[:], in_=d_), writes=[b_parA], dma="all:parA")
        wg = sbA_("wg", [128, 8, 2048], BF16)
        wbs = sbA_("wbs", [128, 4, D], BF16)
        wbf_ = sbA_("wbf", [128, 4, D], BF16)
        wo = sbA_("wo", [128, 8, D], BF16)
        b_wg = [Buf() for _ in range(4)]
        b_wA = Buf()
        NS = 3
        xt2 = [sbA_("xt2_%d" % i, [128, D], F32) for i in range(NS)]
        b_xt2 = [Buf() for _ in range(NS)]
        yt2 = [sbA_("yt2_%d" % i, [128, 4, 256], BF16) for i in range(2)]
        b_yt2 = [Buf() for _ in range(2)]
        jk2 = sbA_("jk2", [128, D], BF16)
        b_jk2 = Buf()
        st2 = [sbA_("st2_%d" % i, [128, 4], F32) for i in range(6)]
        b_st2 = [Buf() for _ in range(6)]
        xn2 = [sbA_("xn2_%d" % i, [128, D], BF16) for i in range(2)]
        b_xn2 = [Buf() for _ in range(2)]
        hTt = [sbA_("hTt%d" % i, [128, 8, 128], BF16) for i in range(2)]
        b_hTt = [Buf() for _ in range(2)]
        yTt = [sbA_("yTt%d" % i, [128, 8, 128], BF16) for i in range(2)]
        b_yTt = [Buf() for _ in range(2)]
        Gs = [sbA_("Gs%d" % i, [128, 2048], F32) for i in range(2)]
        b_Gs = [[Buf(), Buf()] for _ in range(2)]
        m1 = [sbA_("m1_%d" % i, [128, D], F32) for i in range(2)]
        b_m1 = [Buf() for _ in range(2)]
        mx = [sbA_("mx%d" % i, [128, D], BF16) for i in range(2)]
        b_mx = [Buf() for _ in range(2)]
        mTt = [sbA_("mTt%d" % i, [128, 8, 128], BF16) for i in range(2)]
        b_mTt = [Buf() for _ in range(2)]
        t1 = [sbA_("t1_%d" % i, [128, D], F32) for i in range(2)]
        b_t1 = [Buf() for _ in range(2)]
        t2_ = [sbA_("t2x_%d" % i, [128, D], F32) for i in range(2)]
        b_t2 = [Buf() for _ in range(2)]
        x1t = [sbA_("x1t%d" % i, [128, D], F32) for i in range(2)]
        b_x1t = [Buf() for _ in range(2)]
        h2b = [sbA_("h2b%d" % i, [128, D], BF16) for i in range(2)]
        b_h2b = [Buf() for _ in range(2)]
        h2Tt = [sbA_("h2Tt%d" % i, [128, 8, 128], BF16) for i in range(2)]
        b_h2Tt = [Buf() for _ in range(2)]
        tp2 = [psA_("tp2_%d" % i, [128, 8, 128], BF16) for i in range(2)]
        b_tp2 = [Buf() for _ in range(2)]
        big = [psA_("big%d" % i, [128, D], F32) for i in range(3)]
        b_big = [Buf() for _ in range(3)]

        wg_v = w_gate.rearrange("(k p) c -> p k c", p=128)
        for cg in range(4):
            P.op("pool", lambda e, cg=cg: e.dma_start(out=wg[:, :, cg * 512:(cg + 1) * 512],
                                                     in_=wg_v[:, :, cg * 512:(cg + 1) * 512]),
                 writes=[b_wg[cg]], dma="wgq%d" % cg)
        P.op("pool", lambda e: e.dma_start(out=wbs[:], in_=w_bsb.rearrange("(k p) c -> p k c", p=128)),
             writes=[b_wA], dma="all:wA")
        P.op("pool", lambda e: e.dma_start(out=wbf_[:], in_=w_bfx.rearrange("(k p) c -> p k c", p=128)),
             writes=[b_wA], dma="all:wA")
        P.op("pool", lambda e: e.dma_start(out=wo[:], in_=w_out.rearrange("(k p) c -> p k c", p=128)),
             writes=[b_wA], dma="all:wA")

        def nbig():
            i = rot["n"] % 3
            rot["n"] += 1
            return i

        def rstd_chain(src_ap, src_bufs, sti):
            P.op("act", lambda e: e.activation(out=jk2[:], in_=src_ap, func=AF.Square, accum_out=st2[sti][:, 0:1]),
                 reads=src_bufs, writes=[b_jk2, b_st2[sti]])
            P.op("dve", lambda e: e.tensor_scalar(out=st2[sti][:, 1:2], in0=st2[sti][:, 0:1], scalar1=1.0 / D,
                                                 scalar2=EPS, op0=ALU.mult, op1=ALU.add),
                 reads=[b_st2[sti]], writes=[b_st2[sti]])
            P.op("pool", lambda e: e.tensor_tensor(out=st2[sti][:, 2:3], in0=st2[sti][:, 1:2], in1=mh2[:], op=ALU.pow),
                 reads=[b_st2[sti], b_mh2], writes=[b_st2[sti]])
            return st2[sti][:, 2:3]

        tpc = {"n": 0}

        def transposes(src, src_buf, dst, dst_buf, chunk_fn=None):
            tpc["n"] += 1
            tpi = tpc["n"] % 2
            for k in range(8):
                in_ap = src[:, k * 128:(k + 1) * 128] if chunk_fn is None else chunk_fn(k)
                P.op("pe", lambda e, k=k, in_ap=in_ap: e.transpose(out=tp2[tpi][:, k, :], in_=in_ap, identity=IDENT2),
                     reads=[src_buf, b_cst2], writes=[b_tp2[tpi]])
            P.op("act", lambda e: e.activation(out=dst, in_=tp2[tpi][:], func=AF.Copy),
                 reads=[b_tp2[tpi]], writes=[dst_buf])

        yall_v = yall.ap().rearrange("(r t) c -> t r c", r=8)
        pid_cache = {}

        def A0(t):
            xs, ys = t % NS, t % 2
            P.op("sp", lambda e: e.dma_start(out=xt2[xs][:], in_=x_own[t * 128:(t + 1) * 128, :]),
                 writes=[b_xt2[xs]], dma="xt2_%d" % xs)

            def yfn(e):
                if "c" not in pid_cache:
                    pid = e.partition_id()
                    pid_cache["c"] = pid
                    pid_cache["base"] = yall_v[bass.ds((pid % 4) * TOKC, TOK2), bass.ds((pid // 4) * 4, 4), :]
                return e.dma_start(out=yt2[ys][:], in_=pid_cache["base"][t * 128:(t + 1) * 128, :, :])
            P.op("sp", yfn, reads=[b_yall], writes=[b_yt2[ys]], dma="yt2_%d" % ys, extra=[cc_op])
            sti = (2 * t) % 6
            r = rstd_chain(xt2[xs][:], [b_xt2[xs]], sti)
            P.op("dve", lambda e: e.scalar_tensor_tensor(out=xn2[t % 2][:], in0=xt2[xs][:], scalar=r, in1=gpb[:],
                                                        op0=ALU.mult, op1=ALU.mult),
                 reads=[b_xt2[xs], b_st2[sti], b_parA], writes=[b_xn2[t % 2]])

        def A1(t):
            s2 = t % 2
            transposes(xn2[s2], b_xn2[s2], hTt[s2][:], b_hTt[s2])
            transposes(None, b_yt2[s2], yTt[s2][:], b_yTt[s2],
                       chunk_fn=lambda k: yt2[s2][:, k // 2, (k % 2) * 128:(k % 2) * 128 + 128])
            for br in range(2):
                bi = nbig()
                for cg in range(2):
                    for k in range(8):
                        P.op("pe", lambda e, k=k, cg=cg, bi=bi, br=br: e.matmul(
                            big[bi][:, cg * 512:(cg + 1) * 512], lhsT=hTt[s2][:, k, :],
                            rhs=wg[:, k, br * 1024 + cg * 512: br * 1024 + (cg + 1) * 512],
                            start=(k == 0), stop=(k == 7)),
                            reads=[b_hTt[s2], b_wg[2 * br + cg]], writes=[b_big[bi]])
                P.op("act", lambda e, bi=bi, br=br: e.activation(out=Gs[s2][:, br * 1024:(br + 1) * 1024], in_=big[bi][:],
                                                                 func=AF.Sigmoid),
                     reads=[b_big[bi]], writes=[b_Gs[s2][br]])
            bi = nbig()
            for cg in range(2):
                for r in range(4):
                    P.op("pe", lambda e, r=r, cg=cg, bi=bi: e.matmul(
                        big[bi][:, cg * 512:(cg + 1) * 512], lhsT=yTt[s2][:, 2 * r, :],
                        rhs=wbs[:, r, cg * 512:(cg + 1) * 512], start=(r == 0), stop=(r == 3)),
                        reads=[b_yTt[s2], b_wA], writes=[b_big[bi]])
            P.op("dve", lambda e, bi=bi: e.tensor_tensor(out=m1[s2][:], in0=big[bi][:], in1=Gs[s2][:, 0:1024], op=ALU.mult),
                 reads=[b_big[bi], b_Gs[s2][0]], writes=[b_m1[s2]])
            bi = nbig()
            for cg in range(2):
                for r in range(4):
                    P.op("pe", lambda e, r=r, cg=cg, bi=bi: e.matmul(
                        big[bi][:, cg * 512:(cg + 1) * 512], lhsT=yTt[s2][:, 2 * r + 1, :],
                        rhs=wbf_[:, r, cg * 512:(cg + 1) * 512], start=(r == 0), stop=(r == 3)),
                        reads=[b_yTt[s2], b_wA], writes=[b_big[bi]])
            P.op("dve", lambda e, bi=bi: e.tensor_tensor(out=t2_[s2][:], in0=big[bi][:], in1=Gs[s2][:, 1024:2048], op=ALU.mult),
                 reads=[b_big[bi], b_Gs[s2][1]], writes=[b_t2[s2]])
            P.op("pool", lambda e: e.tensor_tensor(out=mx[s2][:], in0=m1[s2][:], in1=t2_[s2][:], op=ALU.add),
                 reads=[b_m1[s2], b_t2[s2]], writes=[b_mx[s2]])

        def A2(t):
            s2 = t % 2
            xs = t % NS
            transposes(mx[s2], b_mx[s2], mTt[s2][:], b_mTt[s2])
            bi = nbig()
            for cg in range(2):
                for k in range(8):
                    P.op("pe", lambda e, k=k, cg=cg, bi=bi: e.matmul(
                        big[bi][:, cg * 512:(cg + 1) * 512], lhsT=mTt[s2][:, k, :],
                        rhs=wo[:, k, cg * 512:(cg + 1) * 512], start=(k == 0), stop=(k == 7)),
                        reads=[b_mTt[s2], b_wA], writes=[b_big[bi]])
            sti = (2 * t + 1) % 6
            r = rstd_chain(big[bi][:], [b_big[bi]], sti)
            P.op("dve", lambda e, bi=bi: e.scalar_tensor_tensor(out=t1[s2][:], in0=big[bi][:], scalar=r, in1=g_post[:],
                                                               op0=ALU.mult, op1=ALU.mult),
                 reads=[b_big[bi], b_st2[sti], b_parA], writes=[b_t1[s2]])
            P.op("pool", lambda e: e.tensor_tensor(out=x1t[s2][:], in0=xt2[xs][:], in1=t1[s2][:], op=ALU.add),
                 reads=[b_xt2[xs], b_t1[s2]], writes=[b_x1t[s2]])
            P.op("sp", lambda e: e.dma_start(out=x1buf[t * 128:(t + 1) * 128, :], in_=x1t[s2][:]),
                 reads=[b_x1t[s2]], writes=[b_x1buf[t]], dma="x1st%d" % s2)
            sti2 = (2 * t) % 6
            r2 = rstd_chain(x1t[s2][:], [b_x1t[s2]], sti2)
            P.op("dve", lambda e: e.scalar_tensor_tensor(out=h2b[s2][:], in0=x1t[s2][:], scalar=r2, in1=gfpb[:],
                                                        op0=ALU.mult, op1=ALU.mult),
                 reads=[b_x1t[s2], b_st2[sti2], b_parA], writes=[b_h2b[s2]])
            if t == 0:
                P.op("dve", lambda e: e.tensor_scalar(out=h2b[s2][:], in0=h2b[s2][:], scalar1=flag[:, 0:1], scalar2=None,
                                                     op0=ALU.mult),
                     reads=[b_h2b[s2], b_par], writes=[b_h2b[s2]])
            transposes(h2b[s2], b_h2b[s2], h2Tt[s2][:], b_h2Tt[s2])
            P.op("sp", lambda e: e.dma_start(out=h2buf[:, :, t * 128:(t + 1) * 128], in_=h2Tt[s2][:]),
                 reads=[b_h2Tt[s2]], writes=[b_h2buf[t]], dma="h2st%d" % s2)

        A0(0)
        if NT2 > 1:
            A0(1)
        A1(0)
        for t in range(NT2):
            if t + 2 < NT2:
                A0(t + 2)
            if t + 1 < NT2:
                A1(t + 1)
            A2(t)

    P.barrier()
    P.phase = 4
    actT = p2.enter_context(nc.sbuf_tensor("actT", [128, NFF, TOKC], BF16))
    b_actT = [Buf() for _ in range(NFF)]
    with ExitStack() as pU:
        def sbU(name, shape, dt):
            return pU.enter_context(nc.sbuf_tensor(name, list(shape), dt))

        def psU(name, shape, dt):
            return pU.enter_context(nc.psum_tensor(name, list(shape), dt))

        h2T = sbU("h2T", [128, 8, TOK2 + 2], BF16)
        NW = 3
        wub = [sbU("wub%d" % i, [128, 8, 2, 128], BF16) for i in range(NW)]
        b_wub = [Buf() for _ in range(NW)]
        ac = [[sbU("ac%d_%d" % (i, part), [128, TOK2], F32) for part in range(2)] for i in range(2)]
        ups = [psU("ups%d" % i, [128, 512], F32) for i in range(8)]
        b_ups = [Buf() for _ in range(8)]
        upc = {"n": 0}
        wup_v = w_up.rearrange("(k p) (t f) -> p k t f", p=128, t=2)
        grp = []
        t0_ = 0
        while t0_ < TOK2:
            n_ = min(510, TOK2 - t0_)
            grp.append((t0_, n_))
            t0_ += n_
        b_ac = [[[Buf() for _ in grp] for _ in range(2)] for _ in range(2)]
        cgs = []
        c0_ = 0
        while c0_ < TOK2:
            w_ = min(512, TOK2 - c0_)
            cgs.append((c0_, w_))
            c0_ += w_
        b_h2Tg = [Buf() for _ in cgs]
        b_h2pad = Buf()
        P.op("dve", lambda e: e.memset(h2T[:, :, 0:2], 0.0), writes=[b_h2pad])
        for gi, (cc0, cw_) in enumerate(cgs):
            P.op("sp", lambda e, cc0=cc0, cw_=cw_: e.dma_start(out=h2T[:, :, 2 + cc0:2 + cc0 + cw_],
                                                             in_=h2buf[:, :, cc0:cc0 + cw_]),
                 reads=b_h2buf[cc0 // 128:(cc0 + cw_) // 128], writes=[b_h2Tg[gi]], dma="h2ld%d" % gi)

        def h2deps(t0, n):
            lo, hi = max(t0 - 2, 0), t0 + n
            return [b_h2Tg[gi] for gi, (cc0, cw_) in enumerate(cgs) if cc0 < hi and cc0 + cw_ > lo] + [b_h2pad]

        def U0(c):
            s_ = c % NW
            for part in range(2):
                P.op("pool", lambda e, part=part: e.dma_start(out=wub[s_][:, :, part, :],
                                                            in_=wup_v[:, :, part, c * 128:(c + 1) * 128]),
                     writes=[b_wub[s_]], dma="wub%d" % s_, bar=c)

        def U1(c):
            s_ = c % NW
            a = c % 2
            for part in (1, 0):
                fi = part * NFF + c
                for gi, (t0, n) in enumerate(grp):
                    pi = upc["n"] % 8
                    upc["n"] += 1
                    for k in range(8):
                        P.op("pe", lambda e, k=k, pi=pi, t0=t0, n=n, part=part: e.matmul(
                            ups[pi][:, 0:n + 2], lhsT=wub[s_][:, k, part, :], rhs=h2T[:, k, t0:t0 + n + 2],
                            start=(k == 0), stop=(k == 7)),
                            reads=[b_wub[s_]] + h2deps(t0, n), writes=[b_ups[pi]])
                    dst = ac[a][part][:, t0:t0 + n]
                    bd = b_ac[a][part][gi]
                    P.op("act", lambda e, pi=pi, n=n, dst=dst, fi=fi: e.activation(
                        out=dst, in_=ups[pi][:, 2:n + 2], func=AF.Identity, scale=cw[:, fi, 2:3], bias=cb[:, fi:fi + 1]),
                        reads=[b_ups[pi], b_par], writes=[bd])
                    P.op("dve", lambda e, pi=pi, n=n, dst=dst, fi=fi: e.scalar_tensor_tensor(
                        out=dst, in0=ups[pi][:, 1:n + 1], scalar=cw[:, fi, 1:2], in1=dst, op0=ALU.mult, op1=ALU.add),
                        reads=[b_ups[pi], b_par, bd], writes=[bd])
                    P.op("dve", lambda e, pi=pi, n=n, dst=dst, fi=fi: e.scalar_tensor_tensor(
                        out=dst, in0=ups[pi][:, 0:n], scalar=cw[:, fi, 0:1], in1=dst, op0=ALU.mult, op1=ALU.add),
                        reads=[b_ups[pi], b_par, bd], writes=[bd])
            P.op("act", lambda e: e.activation(out=ac[a][0][:, HALO:], in_=ac[a][0][:, HALO:], func=AF.Gelu_apprx_tanh),
                 reads=b_ac[a][0], writes=b_ac[a][0])
            P.op("pool", lambda e: e.tensor_tensor(out=actT[:, c, :], in0=ac[a][0][:, HALO:], in1=ac[a][1][:, HALO:], op=ALU.mult),
                 reads=b_ac[a][0] + b_ac[a][1], writes=[b_actT[c]])

        U0(0)
        if NFF > 1:
            U0(1)
        for c in range(NFF):
            if c + 2 < NFF:
                U0(c + 2)
            U1(c)

    P.barrier()
    P.phase = 5
    out_ops = []
    with ExitStack() as pD:
        def sbD_(name, shape, dt):
            return pD.enter_context(nc.sbuf_tensor(name, list(shape), dt))

        def psD(name, shape, dt):
            return pD.enter_context(nc.psum_tensor(name, list(shape), dt))

        wd = sbD_("wd", [128, NFF, D], BF16)
        wpg = sbD_("wpg", [128, 8, D], BF16)
        wpl = sbD_("wpl", [128, 2, D], BF16)
        b_wd = [Buf() for _ in range(2)]
        b_wD = Buf()
        hf_ = NFF // 2
        wd_v = w_down.rearrange("(k p) c -> p k c", p=128)
        P.op("pool", lambda e: e.dma_start(out=wd[:, 0:hf_, :], in_=wd_v[:, 0:hf_, :]), writes=[b_wd[0]], dma="wdq0")
        P.op("pool", lambda e: e.dma_start(out=wd[:, hf_:NFF, :], in_=wd_v[:, hf_:NFF, :]), writes=[b_wd[1]], dma="wdq1")
        P.op("pool", lambda e: e.dma_start(out=wpg[:], in_=w_pg.rearrange("(k p) c -> p k c", p=128)),
             writes=[b_wD], dma="all:wD")
        P.op("pool", lambda e: e.dma_start(out=wpl[:], in_=w_ple.rearrange("(k p) c -> p k c", p=128)),
             writes=[b_wD], dma="all:wD")
        x1r = [sbD_("x1r%d" % i, [128, D], F32) for i in range(2)]
        b_x1r = [Buf() for _ in range(2)]
        jk3 = sbD_("jk3", [128, D], BF16)
        b_jk3 = Buf()
        st3 = [sbD_("st3_%d" % i, [128, 4], F32) for i in range(2)]
        b_st3 = [Buf() for _ in range(2)]
        t3a = sbD_("t3a", [128, D], F32)
        b_t3a = Buf()
        t3b = sbD_("t3b", [128, D], F32)
        b_t3b = Buf()
        x2 = [sbD_("x2_%d" % i, [128, D], F32) for i in range(2)]
        b_x2 = [Buf() for _ in range(2)]
        x2b = [sbD_("x2b%d" % i, [128, D], BF16) for i in range(2)]
        b_x2b = [Buf() for _ in range(2)]
        x2T = [sbD_("x2T%d" % i, [128, 8, 128], BF16) for i in range(2)]
        b_x2T = [Buf() for _ in range(2)]
        ptl = [sbD_("ptl%d" % i, [128, PLE], F32) for i in range(2)]
        b_ptl = [Buf() for _ in range(2)]
        ptb = [sbD_("ptb%d" % i, [128, PLE], BF16) for i in range(2)]
        b_ptb = [Buf() for _ in range(2)]
        pT = [sbD_("pT%d" % i, [128, 2, 128], BF16) for i in range(2)]
        b_pT = [Buf() for _ in range(2)]
        sg = sbD_("sg", [128, D], F32)
        b_sg = Buf()
        ot = [sbD_("ot%d" % i, [128, D], F32) for i in range(2)]
        b_ot = [Buf() for _ in range(2)]
        tp3 = [psD("tp3_%d" % i, [128, 8, 128], BF16) for i in range(2)]
        b_tp3 = [Buf() for _ in range(2)]
        bg = [psD("bg%d" % i, [128, D], F32) for i in range(3)]
        b_bg = [Buf() for _ in range(3)]
        rot3 = {"n": 0, "t": 0}

        def nbg():
            i = rot3["n"] % 3
            rot3["n"] += 1
            return i

        def ntp3():
            rot3["t"] += 1
            return rot3["t"] % 2

        def D0(t):
            s2 = t % 2
            P.op("sp", lambda e: e.dma_start(out=x1r[s2][:], in_=x1buf[(t + 1) * 128:(t + 2) * 128, :]),
                 reads=[b_x1buf[t + 1]], writes=[b_x1r[s2]], dma="x1r%d" % s2)
            P.op("sp", lambda e: e.dma_start(out=ptl[s2][:], in_=p_own[(t + 1) * 128:(t + 2) * 128, :]),
                 writes=[b_ptl[s2]], dma="ptl%d" % s2)
            P.op("act", lambda e: e.activation(out=ptb[s2][:], in_=ptl[s2][:], func=AF.Copy),
                 reads=[b_ptl[s2]], writes=[b_ptb[s2]])

        def D1a(t):
            s2 = t % 2
            bi = nbg()
            for cg in range(2):
                for c in range(NFF):
                    P.op("pe", lambda e, c=c, cg=cg, bi=bi: e.matmul(
                        bg[bi][:, cg * 512:(cg + 1) * 512], lhsT=actT[:, c, t * 128:(t + 1) * 128],
                        rhs=wd[:, c, cg * 512:(cg + 1) * 512], start=(c == 0), stop=(c == NFF - 1)),
                        reads=[b_actT[c], b_wd[0 if c < hf_ else 1]], writes=[b_bg[bi]])
            P.op("act", lambda e, bi=bi: e.activation(out=jk3[:], in_=bg[bi][:], func=AF.Square, accum_out=st3[s2][:, 0:1]),
                 reads=[b_bg[bi]], writes=[b_jk3, b_st3[s2]])
            P.op("dve", lambda e: e.tensor_scalar(out=st3[s2][:, 1:2], in0=st3[s2][:, 0:1], scalar1=1.0 / D, scalar2=EPS,
                                                 op0=ALU.mult, op1=ALU.add),
                 reads=[b_st3[s2]], writes=[b_st3[s2]])
            P.op("pool", lambda e: e.tensor_tensor(out=st3[s2][:, 2:3], in0=st3[s2][:, 1:2], in1=mh2[:], op=ALU.pow),
                 reads=[b_st3[s2], b_mh2], writes=[b_st3[s2]])
            P.op("dve", lambda e, bi=bi: e.scalar_tensor_tensor(out=t3a[:], in0=bg[bi][:], scalar=st3[s2][:, 2:3],
                                                               in1=g_fpost[:], op0=ALU.mult, op1=ALU.mult),
                 reads=[b_bg[bi], b_st3[s2], b_par], writes=[b_t3a])
            P.op("pool", lambda e: e.tensor_tensor(out=x2[s2][:], in0=x1r[s2][:], in1=t3a[:], op=ALU.add),
                 reads=[b_x1r[s2], b_t3a], writes=[b_x2[s2]])
            P.op("act", lambda e: e.activation(out=x2b[s2][:], in_=x2[s2][:], func=AF.Copy),
                 reads=[b_x2[s2]], writes=[b_x2b[s2]])
            ti = ntp3()
            for k in range(8):
                P.op("pe", lambda e, k=k, ti=ti: e.transpose(out=tp3[ti][:, k, :], in_=x2b[s2][:, k * 128:(k + 1) * 128],
                                                           identity=IDENT2),
                     reads=[b_x2b[s2], b_cst2], writes=[b_tp3[ti]])
            P.op("act", lambda e, ti=ti: e.activation(out=x2T[s2][:], in_=tp3[ti][:], func=AF.Copy),
                 reads=[b_tp3[ti]], writes=[b_x2T[s2]])
            ti = ntp3()
            for k in range(2):
                P.op("pe", lambda e, k=k, ti=ti: e.transpose(out=tp3[ti][:, k, :], in_=ptb[s2][:, k * 128:(k + 1) * 128],
                                                           identity=IDENT2),
                     reads=[b_ptb[s2], b_cst2], writes=[b_tp3[ti]])
            P.op("act", lambda e, ti=ti: e.activation(out=pT[s2][:], in_=tp3[ti][:, 0:2, :], func=AF.Copy),
                 reads=[b_tp3[ti]], writes=[b_pT[s2]])

        def D1b(t):
            s2 = t % 2
            bi = nbg()
            for cg in range(2):
                for k in range(8):
                    P.op("pe", lambda e, k=k, cg=cg, bi=bi: e.matmul(
                        bg[bi][:, cg * 512:(cg + 1) * 512], lhsT=x2T[s2][:, k, :],
                        rhs=wpg[:, k, cg * 512:(cg + 1) * 512], start=(k == 0), stop=(k == 7)),
                        reads=[b_x2T[s2], b_wD], writes=[b_bg[bi]])
            P.op("act", lambda e, bi=bi: e.activation(out=sg[:], in_=bg[bi][:], func=AF.Sigmoid),
                 reads=[b_bg[bi]], writes=[b_sg])
            bi = nbg()
            for cg in range(2):
                for k in range(2):
                    P.op("pe", lambda e, k=k, cg=cg, bi=bi: e.matmul(
                        bg[bi][:, cg * 512:(cg + 1) * 512], lhsT=pT[s2][:, k, :],
                        rhs=wpl[:, k, cg * 512:(cg + 1) * 512], start=(k == 0), stop=(k == 1)),
                        reads=[b_pT[s2], b_wD], writes=[b_bg[bi]])
            P.op("dve", lambda e, bi=bi: e.tensor_tensor(out=t3b[:], in0=bg[bi][:], in1=sg[:], op=ALU.mult),
                 reads=[b_bg[bi], b_sg], writes=[b_t3b])
            P.op("pool", lambda e: e.tensor_tensor(out=ot[s2][:], in0=x2[s2][:], in1=t3b[:], op=ALU.add),
                 reads=[b_x2[s2], b_t3b], writes=[b_ot[s2]])
            o = P.op("sp", lambda e: e.dma_start(out=out_d[t * 128:(t + 1) * 128, :], in_=ot[s2][:]),
                     reads=[b_ot[s2]], dma="ot%d" % s2)
            out_ops.append(o)

        NTM = TOKC // 128
        D0(0)
        if NTM > 1:
            D0(1)
        D1a(0)
        for t in range(NTM):
            if t + 2 < NTM:
                D0(t + 2)
            if t + 1 < NTM:
                D1a(t + 1)
            D1b(t)

    p2.close()
    P.final = list(out_ops)
    P.emit(nc, es)
    es.close()
    return nc


def _consts():
    j = np.arange(128)[:, None]
    s = np.arange(128)[None, :]
    ident = (j == s)
    utri = (j > s)
    ones = np.ones((128, 128), bool)
    msb = (j < s)
    mfx = (j <= s)
    return np.stack([ident, utri, ones, msb, mfx], axis=1).astype(np.float32)


def _core_inputs(c, x, w_in, norm_attn_pre, b_forget, full=None):
    b, g = divmod(c, 4)
    wi = w_in[0]
    SS = x.shape[1]
    TOKC = SS // 4

    def hc(base, h):
        return slice(base + 64 * h, base + 64 * (h + 1))
    sbh = [2 * g, 2 * g + 1]
    q_sb = np.concatenate([wi[:, hc(0, h)] for h in sbh], 1)
    k_sb = np.concatenate([wi[:, hc(512, h)] for h in sbh], 1)
    v_sb = np.concatenate([wi[:, hc(1024, h)] for h in sbh], 1)
    q_fx = np.concatenate([wi[:, hc(1536, h)] for h in sbh], 1)
    k_fx = np.concatenate([wi[:, hc(2048, h)] for h in sbh], 1)
    v_fx = np.concatenate([wi[:, hc(2560, h)] for h in sbh], 1)
    f_l = wi[:, 3072 + 2 * g:3072 + 2 * g + 2]
    wqk = np.ascontiguousarray(np.concatenate([q_sb, q_fx, k_sb, k_fx], 1))
    wvf = np.ascontiguousarray(np.concatenate([v_sb, v_fx, f_l], 1))
    m = {
        "xb": np.ascontiguousarray(x[b]),
        "consts": _consts(),
        "wqk": wqk,
        "wvf": wvf,
        "gpre": np.ascontiguousarray(norm_attn_pre[0].reshape(8, 128).T),
        "bfg": np.ascontiguousarray(b_forget[0, 2 * g:2 * g + 2].reshape(2, 1)),
    }
    if full is not None:
        f = full
        x_own = np.zeros((TOKC + HALO, D), np.float32)
        p_own = np.zeros((TOKC + HALO, PLE), np.float32)
        lo = g * TOKC - HALO
        if lo >= 0:
            x_own[:] = x[b, lo:lo + TOKC + HALO]
            p_own[:] = f["p"][0, b, lo:lo + TOKC + HALO]
        else:
            x_own[HALO:] = x[b, 0:TOKC]
            p_own[HALO:] = f["p"][0, b, 0:TOKC]
        m.update({
            "x_own": x_own,
            "p_own": p_own,
            "w_gate": np.ascontiguousarray(wi[:, 3080:3080 + 2048]),
            "w_bsb": np.ascontiguousarray(f["w_branch_sb"][0]),
            "w_bfx": np.ascontiguousarray(f["w_branch_fox"][0]),
            "w_out": np.ascontiguousarray(f["w_out"][0]),
            "w_up": np.ascontiguousarray(f["w_up"][0]),
            "w_down": np.ascontiguousarray(f["w_down"][0]),
            "w_ple": np.ascontiguousarray(f["w_ple"][0]),
            "w_pg": np.ascontiguousarray(f["w_ple_gate"][0]),
            "g_post": np.ascontiguousarray(np.broadcast_to(f["norm_attn_post"][0][None, :], (128, D))),
            "g_fpost": np.ascontiguousarray(np.broadcast_to(f["norm_ffn_post"][0][None, :], (128, D))),
            "gpre_b": np.ascontiguousarray(np.broadcast_to(norm_attn_pre[0][None, :], (128, D))),
            "gfpre_b": np.ascontiguousarray(np.broadcast_to(f["norm_ffn_pre"][0][None, :], (128, D))),
            "cw": np.ascontiguousarray(f["conv_w"][0].reshape(3, 2 * NFF, 128).transpose(2, 1, 0)),
            "cb": np.ascontiguousarray(f["conv_b"][0].reshape(2 * NFF, 128).T),
            "flag": np.full((128, 1), 0.0 if g == 0 else 1.0, np.float32),
        })
    return m


_NC_CACHE = {}


def kernel(**inputs):
    inputs = {k: np.asarray(v, dtype=np.float32) for k, v in inputs.items()}
    x = inputs["x"]
    if "nc" not in _NC_CACHE:
        _NC_CACHE["nc"] = build()
    nc = _NC_CACHE["nc"]
    in_maps = [_core_inputs(c, x, inputs["w_in"], inputs["norm_attn_pre"], inputs["b_forget"], full=inputs)
               for c in range(8)]
    res = run_bass_kernel_spmd(nc, in_maps, core_ids=list(range(8)))
    TOKC = x.shape[1] // 4
    out = np.empty_like(x)
    for c in range(8):
        b, g = divmod(c, 4)
        out[b, g * TOKC:(g + 1) * TOKC] = np.asarray(res.results[c]["out"])
    return out
```

```python
import numpy as np
from contextlib import ExitStack

import concourse.bass as bass
import concourse.mybir as mybir
from concourse.bass_utils import run_bass_kernel_spmd

F32 = mybir.dt.float32
BF16 = mybir.dt.bfloat16
AF = mybir.ActivationFunctionType
ALU = mybir.AluOpType

S = 8192
D = 1024
NBLK = S // 128
NGRP = S // 512
DFF = 2816
NFF = DFF // 128
PLE = 256
HALO = 128
EPS = 1e-6

ENGS = ("sp", "act", "dve", "pool", "pe")


class Buf:
    __slots__ = ("w", "r", "rd")

    def __init__(self):
        self.w = None
        self.r = {}
        self.rd = []


class Op:
    __slots__ = ("eng", "fn", "deps", "sem", "ticket", "dma", "ndep", "phase", "bar")


class Prog:
    def __init__(self):
        self.q = {e: [] for e in ENGS}
        self.phase = 0
        self.final = []
        self.pending = []

    def op(self, eng, fn, reads=(), writes=(), dma=None, extra=(), bar=None):
        o = Op()
        o.bar = bar
        o.eng = eng
        o.fn = fn
        o.dma = dma
        o.ndep = 0
        o.phase = self.phase
        o.sem = None
        o.ticket = 0
        deps = set()
        for b in reads:
            if b.w is not None:
                deps.add(b.w)
        for b in writes:
            if b.w is not None:
                deps.add(b.w)
            for r in b.r.values():
                deps.add(r)
            for r in b.rd:
                deps.add(r)
        for d in extra:
            if d is not None:
                deps.add(d)
        deps.discard(o)
        o.deps = deps
        for d in deps:
            d.ndep += 1
        for b in reads:
            if dma is not None:
                b.rd.append(o)
            else:
                b.r[eng] = o
        for b in writes:
            b.w = o
            b.r = {}
            b.rd = []
        self.q[eng].append(o)
        if dma is not None:
            self.pending.append(o)
        return o

    def barrier(self):
        deps = list(self.pending)
        for e in ENGS:
            for o in reversed(self.q[e]):
                if o.dma is None and o.fn is not None:
                    deps.append(o)
                    break
        self.pending = []
        for e in ENGS:
            self.op(e, None, extra=deps)

    def emit(self, nc, es):
        sems = {}

        def get_sem(key):
            if key not in sems:
                sems[key] = [es.enter_context(nc.semaphore("s_%s" % str(key).replace(":", "_"))), 0]
            return sems[key]

        all_total = {}
        for e in ENGS:
            for o in self.q[e]:
                if o.bar == "cc":
                    continue
                if o.dma is not None:
                    s = get_sem("d:" + o.dma)
                    s[1] += 16
                    o.sem = s[0]
                    o.ticket = s[1]
                    if o.dma.startswith("all:"):
                        all_total[o.dma] = s[1]
                elif o.ndep > 0:
                    s = get_sem("e:%s:%d" % (e, o.phase))
                    s[1] += 1
                    o.sem = s[0]
                    o.ticket = s[1]
        bar_max = {}
        for e in ENGS:
            for o in self.q[e]:
                if o.dma is not None and o.dma.startswith("all:"):
                    o.ticket = all_total[o.dma]
                if o.dma is not None and o.bar is not None:
                    kk = (o.dma, o.bar)
                    bar_max[kk] = max(bar_max.get(kk, 0), o.ticket)
        for e in ENGS:
            for o in self.q[e]:
                if o.dma is not None and o.bar is not None:
                    o.ticket = bar_max[(o.dma, o.bar)]
        final = list(self.final)
        q = self.q
        handles = {"sp": "sync", "act": "scalar", "dve": "vector", "pool": "gpsimd", "pe": "tensor"}

        def run(ename, eng):
            waited = {}
            for o in q[ename]:
                for d in o.deps:
                    if d.sem is None:
                        continue
                    if ename == "pe" and d.eng == "pe" and d.dma is None:
                        continue
                    if o.dma is not None and d.dma == o.dma and (
                            o.dma.startswith("all:") or (o.bar is not None and o.bar == d.bar)):
                        continue
                    k = id(d.sem)
                    if waited.get(k, 0) >= d.ticket:
                        continue
                    waited[k] = d.ticket
                    eng.wait_ge(d.sem, d.ticket)
                if o.fn is None:
                    continue
                ins = o.fn(eng)
                if o.bar == "cc":
                    ins.then_inc(o.sem)
                elif o.sem is not None:
                    ins.then_inc(o.sem, 16 if o.dma is not None else 1)
            if ename == "sp":
                for d in final:
                    eng.wait_ge(d.sem, d.ticket)

        with nc.Block() as block:
            for ename in ENGS:
                deco = getattr(block, handles[ename])

                def mk(ename):
                    def f(eng):
                        run(ename, eng)
                    return f
                deco(mk(ename))


def build(debug=None):
    nc = bass.Bass("TRN2", target_bir_lowering=False)
    es = ExitStack()
    P = Prog()

    def din(name, shape, dt=F32):
        return nc.dram_tensor(name, list(shape), dt, kind="ExternalInput").ap()

    xb = din("xb", [S, D])
    consts = din("consts", [128, 5, 128])
    wqk = din("wqk", [D, 512])
    wvf = din("wvf", [D, 258])
    gpre = din("gpre", [128, 8])
    bfg = din("bfg", [2, 1])
    TOKC = S // 4
    TOK2 = TOKC + HALO
    NT2 = TOK2 // 128
    if debug in ("p1", "p1a"):
        ybuf = nc.dram_tensor("ybuf", [S + HALO, 256], BF16, kind="ExternalOutput")
    else:
        ybuf = nc.dram_tensor("ybuf", [S + HALO, 256], BF16)
    ybuf_ap = ybuf.ap()
    full = debug not in ("p1", "p1a")
    if full:
        x_own = din("x_own", [TOK2, D])
        p_own = din("p_own", [TOK2, PLE])
        w_gate = din("w_gate", [D, 2048])
        w_bsb = din("w_bsb", [512, D])
        w_bfx = din("w_bfx", [512, D])
        w_out = din("w_out", [D, D])
        w_up = din("w_up", [D, 2 * DFF])
        w_down = din("w_down", [DFF, D])
        w_ple = din("w_ple", [PLE, D])
        w_pg = din("w_pg", [D, D])
        g_post_d = din("g_post", [128, D])
        g_fpost_d = din("g_fpost", [128, D])
        gpre_b_d = din("gpre_b", [128, D])
        gfpre_b_d = din("gfpre_b", [128, D])
        cw_d = din("cw", [128, 2 * NFF, 3])
        cb_d = din("cb", [128, 2 * NFF])
        flag_d = din("flag", [128, 1])
        out_d = nc.dram_tensor("out", [TOKC, D], F32, kind="ExternalOutput").ap()
        yall = nc.dram_tensor("yall", [8 * (S + HALO), 256], BF16)
        x1buf = nc.dram_tensor("x1buf", [TOK2, D], F32).ap()

    def sb(name, shape, dt):
        return es.enter_context(nc.sbuf_tensor(name, list(shape), dt))

    p1 = ExitStack()

    def sb1(name, shape, dt):
        return p1.enter_context(nc.sbuf_tensor(name, list(shape), dt))

    cst = sb1("cst", [128, 5, 128], BF16)
    IDENT, UTRI, ONES, MSB, MFX = (cst[:, i, :] for i in range(5))
    b_cst = Buf()
    QT_sb = sb1("QT_sb", [128, S], BF16)
    KT_sb = sb1("KT_sb", [128, S], BF16)
    QT_fx = [sb1("QT_fx%d" % h, [128, S], BF16) for h in range(2)]
    KT_fx = [sb1("KT_fx%d" % h, [128, S], BF16) for h in range(2)]
    V_sb = sb1("V_sb", [128, NBLK, 2, 64], BF16)
    V_fx = sb1("V_fx", [128, NBLK, 2, 66], BF16)
    b_QTsb = [Buf() for _ in range(NGRP)]
    b_KTsb = [Buf() for _ in range(NGRP)]
    b_QTfx = [[Buf() for _ in range(NGRP)] for _ in range(2)]
    b_KTfx = [[Buf() for _ in range(NGRP)] for _ in range(2)]
    b_QTfx_aug = [[[Buf() for _ in range(3)] for _ in range(NGRP)] for _ in range(2)]
    b_KTfx_aug = [[[Buf() for _ in range(3)] for _ in range(NGRP)] for _ in range(2)]
    b_Vsb = [Buf() for _ in range(NGRP)]
    b_Vfx = [Buf() for _ in range(NGRP)]
    b_Vfx_ones = Buf()
    b_aug_init = Buf()

    mhalf = sb1("mhalf", [128, 1], F32)
    b_mhalf = Buf()

    P.phase = 0
    with ExitStack() as pa:
        def sba(name, shape, dt):
            return pa.enter_context(nc.sbuf_tensor(name, list(shape), dt))

        def psa(name, shape, dt):
            return pa.enter_context(nc.psum_tensor(name, list(shape), dt))

        wstage = sba("wstage", [128, 4, 512], F32)
        b_wstage = Buf()
        wqk_bf = sba("wqk_bf", [128, 8, 512], BF16)
        b_wqk = Buf()
        wvf_bf = sba("wvf_bf", [128, 8, 258], BF16)
        b_wvf = Buf()
        gp = sba("gp", [128, 8], F32)
        b_gp = Buf()
        nb = sba("nb", [2, 1], F32)
        b_nb = Buf()
        NX = 4
        xt = [sba("xt%d" % i, [128, D], F32) for i in range(2)]
        wst_v = wstage[:].rearrange("p a b -> p (a b)")
        xt_ap = [xt[0][:], xt[1][:], wst_v[:, 0:D], wst_v[:, D:2 * D]]
        b_xt = [Buf(), Buf(), None, None]
        junk = [sba("junk%d" % i, [128, D], BF16) for i in range(1)]
        b_junk = [Buf() for _ in range(1)]
        ss = [sba("ss%d" % i, [128, 1], F32) for i in range(NX)]
        b_ss = [Buf() for _ in range(NX)]
        ms = [sba("ms%d" % i, [128, 1], F32) for i in range(NX)]
        b_ms = [Buf() for _ in range(NX)]
        rs = [sba("rs%d" % i, [128, 1], F32) for i in range(NX)]
        b_rs = [Buf() for _ in range(NX)]
        NXN = 5
        xn = [sba("xn%d" % i, [128, D], BF16) for i in range(NXN)]
        b_xn = [Buf() for _ in range(NXN)]
        hT = [sba("hT%d" % i, [128, 8, 512], BF16) for i in range(2)]
        b_hT = [Buf() for _ in range(2)]
        tp_ps = [psa("tp_ps%d" % i, [128, 8, 128], BF16) for i in range(2)]
        b_tp = [Buf() for _ in range(2)]
        NPJ = 4
        pj_ps = [psa("pj_ps%d" % i, [128, 512], F32) for i in range(NPJ)]
        b_pj = [Buf() for _ in range(NPJ)]
        fe = sba("fe", [2, 512], F32)
        b_fe = Buf()
        fl = fe
        b_fl = b_fe
        one2 = sba("one2", [2, 512], F32)
        b_one2 = Buf()
        cc = [sba("cc%d" % i, [2, 512], F32) for i in range(2)]
        b_cc = [Buf() for _ in range(2)]
        r1 = sba("r1", [2, 512], F32)
        b_r1 = Buf()
        r2 = r1
        b_r2 = b_r1
        cpart = [[sba("cp%d_%d" % (s_, i), [2, 512], BF16) for i in range(3)] for s_ in range(1)]
        npart = [[sba("np%d_%d" % (s_, i), [2, 512], BF16) for i in range(3)] for s_ in range(1)]
        b_cpart = [[Buf() for _ in range(3)] for _ in range(1)]
        b_npart = [[Buf() for _ in range(3)] for _ in range(1)]

        P.op("pool", lambda e: e.dma_start(out=cst[:], in_=consts), writes=[b_cst], dma="cstq")
        P.op("sp", lambda e: e.dma_start(out=gp[:], in_=gpre), writes=[b_gp], dma="all:setup")
        P.op("sp", lambda e: e.dma_start(out=nb[:], in_=bfg), writes=[b_nb], dma="all:setup")
        P.op("dve", lambda e: e.tensor_scalar(out=nb[:], in0=nb[:], scalar1=-1.0, scalar2=None, op0=ALU.mult),
             reads=[b_nb], writes=[b_nb])
        P.op("pool", lambda e: e.memset(mhalf[:], -0.5), writes=[b_mhalf])
        P.op("pool", lambda e: e.memset(one2[:], 1.0), writes=[b_one2])
        P.op("pool", lambda e: e.memset(V_fx[:, :, :, 64:66], 1.0), writes=[b_Vfx_ones])
        for h in range(2):
            P.op("pool", lambda e, h=h: e.memset(QT_fx[h][64:128, :], 0.0), writes=[b_aug_init])
            P.op("pool", lambda e, h=h: e.memset(KT_fx[h][64:128, :], 0.0), writes=[b_aug_init])
            P.op("pool", lambda e, h=h: e.memset(QT_fx[h][64:70, :], 1.0), writes=[b_aug_init])
            P.op("pool", lambda e, h=h: e.memset(KT_fx[h][64:70, :], 1.0), writes=[b_aug_init])
        wqk_v = wqk.rearrange("(k p) c -> p k c", p=128)
        wvf_v = wvf.rearrange("(k p) c -> p k c", p=128)
        for hf in range(2):
            P.op("sp", lambda e, hf=hf: e.dma_start(out=wstage[:], in_=wqk_v[:, 4 * hf:4 * hf + 4, :]),
                 writes=[b_wstage], dma="wstage")
            for kk in range(4):
                k = 4 * hf + kk
                P.op("dve", lambda e, k=k, kk=kk: e.tensor_scalar(out=wqk_bf[:, k, 0:256], in0=wstage[:, kk, 0:256],
                                                          scalar1=gp[:, k:k + 1], scalar2=0.125,
                                                          op0=ALU.mult, op1=ALU.mult),
                     reads=[b_wstage, b_gp], writes=[b_wqk])
                P.op("dve", lambda e, k=k, kk=kk: e.tensor_scalar(out=wqk_bf[:, k, 256:512], in0=wstage[:, kk, 256:512],
                                                          scalar1=gp[:, k:k + 1], scalar2=None, op0=ALU.mult),
                     reads=[b_wstage, b_gp], writes=[b_wqk])
        for hf in range(2):
            P.op("sp", lambda e, hf=hf: e.dma_start(out=wstage[:, :, 0:258], in_=wvf_v[:, 4 * hf:4 * hf + 4, :]),
                 writes=[b_wstage], dma="wstage")
            for kk in range(4):
                k = 4 * hf + kk
                P.op("dve", lambda e, k=k, kk=kk: e.tensor_scalar(out=wvf_bf[:, k, :], in0=wstage[:, kk, 0:258],
                                                          scalar1=gp[:, k:k + 1], scalar2=None, op0=ALU.mult),
                     reads=[b_wstage, b_gp], writes=[b_wvf])

        for i_ in (2, 3):
            nb_ = Buf()
            nb_.w = b_wstage.w
            nb_.r = dict(b_wstage.r)
            nb_.rd = list(b_wstage.rd)
            b_xt[i_] = nb_

        pj_ctr = [0]

        def next_pj():
            i = pj_ctr[0] % NPJ
            pj_ctr[0] += 1
            return i

        evac_ctr = [0]

        def evac_eng():
            evac_ctr[0] += 1
            return "act" if evac_ctr[0] % 2 == 0 else "dve"

        def copy_op(eng, out, in_, reads, writes):
            if eng == "act":
                return P.op("act", lambda e: e.activation(out=out, in_=in_, func=AF.Copy), reads=reads, writes=writes)
            return P.op("dve", lambda e: e.tensor_copy(out=out, in_=in_), reads=reads, writes=writes)

        def prep_tile(tt):
            xs = tt % NX
            ns = tt % NXN
            js = 0
            P.op("sp", lambda e: e.dma_start(out=xt_ap[xs], in_=xb[tt * 128:(tt + 1) * 128, :]),
                 writes=[b_xt[xs]], dma="xt%d" % xs)
            P.op("act", lambda e: e.activation(out=junk[js][:], in_=xt_ap[xs], func=AF.Square, accum_out=ss[xs][:]),
                 reads=[b_xt[xs]], writes=[b_junk[js], b_ss[xs]])
            P.op("dve", lambda e: e.tensor_scalar(out=ms[xs][:], in0=ss[xs][:], scalar1=1.0 / D, scalar2=EPS,
                                                 op0=ALU.mult, op1=ALU.add),
                 reads=[b_ss[xs]], writes=[b_ms[xs]])
            P.op("pool", lambda e: e.tensor_tensor(out=rs[xs][:], in0=ms[xs][:], in1=mhalf[:], op=ALU.pow),
                 reads=[b_ms[xs], b_mhalf], writes=[b_rs[xs]])
            P.op("dve", lambda e: e.tensor_scalar(out=xn[ns][:], in0=xt_ap[xs], scalar1=rs[xs][:, 0:1],
                                                 scalar2=None, op0=ALU.mult),
                 reads=[b_xt[xs], b_rs[xs]], writes=[b_xn[ns]])

        def transpose_tile(tt):
            G, j = divmod(tt, 4)
            ns = tt % NXN
            ts_ = tt % 2
            hs = G % 2
            for k in range(8):
                P.op("pe", lambda e, k=k: e.transpose(out=tp_ps[ts_][:, k, :], in_=xn[ns][:, k * 128:(k + 1) * 128],
                                                     identity=IDENT),
                     reads=[b_xn[ns], b_cst], writes=[b_tp[ts_]])
            copy_op(evac_eng(), hT[hs][:, :, j * 128:(j + 1) * 128], tp_ps[ts_][:], [b_tp[ts_]], [b_hT[hs]])

        def project(G):
            hs = G % 2
            cols = slice(G * 512, (G + 1) * 512)
            outs = [
                (slice(0, 128), 128, QT_sb[:, cols], b_QTsb[G]),
                (slice(256, 384), 128, KT_sb[:, cols], b_KTsb[G]),
                (slice(128, 192), 64, QT_fx[0][0:64, cols], b_QTfx[0][G]),
                (slice(192, 256), 64, QT_fx[1][0:64, cols], b_QTfx[1][G]),
                (slice(384, 448), 64, KT_fx[0][0:64, cols], b_KTfx[0][G]),
                (slice(448, 512), 64, KT_fx[1][0:64, cols], b_KTfx[1][G]),
            ]
            for (ws, M, dst, bdst) in outs:
                pi = next_pj()
                for k in range(8):
                    P.op("pe", lambda e, k=k, ws=ws, M=M, pi=pi: e.matmul(
                        pj_ps[pi][0:M, :], lhsT=wqk_bf[:, k, ws], rhs=hT[hs][:, k, :],
                        start=(k == 0), stop=(k == 7)),
                        reads=[b_wqk, b_hT[hs]], writes=[b_pj[pi]])
                copy_op(evac_eng(), dst, pj_ps[pi][0:M, :], [b_pj[pi]], [bdst])
            pi = next_pj()
            for k in range(8):
                P.op("pe", lambda e, k=k, pi=pi: e.matmul(
                    pj_ps[pi][0:2, :], lhsT=wvf_bf[:, k, 256:258], rhs=hT[hs][:, k, :],
                    start=(k == 0), stop=(k == 7)),
                    reads=[b_wvf, b_hT[hs]], writes=[b_pj[pi]])
            P.op("act", lambda e, pi=pi: e.activation(out=fe[:], in_=pj_ps[pi][0:2, :], func=AF.Exp,
                                                     scale=-1.0, bias=nb[:, 0:1]),
                 reads=[b_pj[pi], b_nb], writes=[b_fe])
            P.op("act", lambda e: e.activation(out=fl[:], in_=fe[:], func=AF.Ln, bias=1.0, scale=1.0),
                 reads=[b_fe], writes=[b_fl])
            cs = G % 2
            if G == 0:
                P.op("dve", lambda e: e.tensor_tensor_scan(out=cc[cs][:], data0=one2[:], data1=fl[:], initial=0.0,
                                                          op0=ALU.mult, op1=ALU.subtract),
                     reads=[b_one2, b_fl], writes=[b_cc[cs]])
            else:
                P.op("dve", lambda e: e.tensor_tensor_scan(out=cc[cs][:], data0=one2[:], data1=fl[:],
                                                          initial=cc[1 - cs][:, 511:512],
                                                          op0=ALU.mult, op1=ALU.subtract),
                     reads=[b_one2, b_fl, b_cc[1 - cs]], writes=[b_cc[cs]])
            ps_ = 0
            cp, npp = cpart[ps_], npart[ps_]
            bcp, bnp = b_cpart[ps_], b_npart[ps_]
            P.op("dve", lambda e: e.tensor_copy(out=cp[0][:], in_=cc[cs][:]), reads=[b_cc[cs]], writes=[bcp[0]])
            P.op("dve", lambda e: e.tensor_tensor(out=r1[:], in0=cc[cs][:], in1=cp[0][:], op=ALU.subtract),
                 reads=[b_cc[cs], bcp[0]], writes=[b_r1])
            P.op("dve", lambda e: e.tensor_copy(out=cp[1][:], in_=r1[:]), reads=[b_r1], writes=[bcp[1]])
            P.op("dve", lambda e: e.tensor_tensor(out=r2[:], in0=r1[:], in1=cp[1][:], op=ALU.subtract),
                 reads=[b_r1, bcp[1]], writes=[b_r2])
            P.op("dve", lambda e: e.tensor_copy(out=cp[2][:], in_=r2[:]), reads=[b_r2], writes=[bcp[2]])
            for i in range(3):
                P.op("dve", lambda e, i=i: e.tensor_scalar(out=npp[i][:], in0=cp[i][:], scalar1=-1.0, scalar2=None,
                                                          op0=ALU.mult),
                     reads=[bcp[i]], writes=[bnp[i]])
            for h in range(2):
                for i in range(3):
                    P.op("sp", lambda e, h=h, i=i: e.dma_start(out=QT_fx[h][64 + i:65 + i, cols], in_=cp[i][h:h + 1, :]),
                         reads=[bcp[i], b_aug_init], writes=[b_QTfx_aug[h][G][i]], dma="augq%d" % ps_, bar=G)
                    P.op("sp", lambda e, h=h, i=i: e.dma_start(out=KT_fx[h][67 + i:68 + i, cols], in_=npp[i][h:h + 1, :]),
                         reads=[bnp[i], b_aug_init], writes=[b_KTfx_aug[h][G][i]], dma="augk%d" % ps_, bar=G)
            for j in range(4):
                tt = 4 * G + j
                pi = next_pj()
                for k in range(8):
                    P.op("pe", lambda e, k=k, pi=pi, j=j: e.matmul(
                        pj_ps[pi][:, 0:256], lhsT=hT[hs][:, k, j * 128:(j + 1) * 128], rhs=wvf_bf[:, k, 0:256],
                        start=(k == 0), stop=(k == 7)),
                        reads=[b_wvf, b_hT[hs]], writes=[b_pj[pi]])
                copy_op(evac_eng(), V_sb[:, tt, :, :],
                        pj_ps[pi][:, 0:128].rearrange("p (h d) -> p h d", h=2), [b_pj[pi]], [b_Vsb[G]])
                copy_op(evac_eng(), V_fx[:, tt, :, 0:64],
                        pj_ps[pi][:, 128:256].rearrange("p (h d) -> p h d", h=2), [b_pj[pi], b_Vfx_ones], [b_Vfx[G]])

        for j in range(4):
            prep_tile(j)
        for G in range(NGRP):
            for j in range(4):
                transpose_tile(4 * G + j)
                if G + 1 < NGRP:
                    prep_tile(4 * (G + 1) + j)
            project(G)

    P.barrier()
    ydma_ops = []
    tiles_per_head = []
    for qg in range(NGRP):
        for kb in range(4 * qg + 3, -1, -1):
            i = kb - 4 * qg
            tiles_per_head.append((qg, kb, i if i >= 0 else None))

    if debug == "p1a":
        tiles_per_head = []
    with ExitStack() as pb:
        def sbb(name, shape, dt):
            return pb.enter_context(nc.sbuf_tensor(name, list(shape), dt))

        def psb(name, shape, dt):
            return pb.enter_context(nc.psum_tensor(name, list(shape), dt))

        z_ps = [psb("z_ps%d" % i, [128, 512], F32) for i in range(2)]
        b_z = [Buf() for _ in range(2)]
        bt_ps = [psb("bt_ps%d" % i, [128, 512], F32) for i in range(2)]
        b_bt = [Buf() for _ in range(2)]
        acc_ps = [psb("acc_ps%d" % i, [128, 4, 128], F32) for i in range(2)]
        b_acc = [Buf() for _ in range(2)]
        e_sb = [sbb("e_sb%d" % i, [128, 512], F32) for i in range(2)]
        b_e = [Buf() for _ in range(2)]
        l_sb = [sbb("l_sb%d" % i, [128, 512], F32) for i in range(3)]
        b_l = [Buf() for _ in range(3)]
        lom = [sbb("lom%d" % i, [128, 512], BF16) for i in range(3)]
        b_lom = [Buf() for _ in range(3)]
        Ls = [[sbb("Ls%d_%d" % (a, b_), [128, 512], BF16) for b_ in range(2)] for a in range(2)]
        b_Ls = [[Buf() for _ in range(2)] for _ in range(2)]
        arg = [sbb("arg%d" % i, [128, 512], F32) for i in range(2)]
        b_arg = [Buf() for _ in range(2)]
        wt = [sbb("wt%d" % i, [128, 512], BF16) for i in range(3)]
        b_wt = [Buf() for _ in range(3)]
        yst = [sbb("yst%d" % i, [128, 4, 64], BF16) for i in range(3)]
        b_yst = [Buf() for _ in range(3)]
        rc = [sbb("rc%d" % i, [128, 4], F32) for i in range(2)]
        b_rc = [Buf() for _ in range(2)]
        yst_ctr = [0]

        ybv = ybuf_ap[HALO:, :].rearrange("(q j p) c -> q p j c", j=4, p=128)

        P.phase = 1
        sb_tiles = [(hh,) + t for hh in range(2) for t in tiles_per_head]
        NTL = len(sb_tiles)

        def cr(t):
            i = t[3]
            return (0 if i is None else 128 * i)

        def sbA(n):
            hh, qg, kb, i = sb_tiles[n]
            c0 = cr(sb_tiles[n])
            hp = slice(64 * hh, 64 * hh + 64)
            zb = n % 2
            P.op("pe", lambda e: e.matmul(z_ps[zb][:, c0:512], lhsT=KT_sb[hp, kb * 128:(kb + 1) * 128],
                                         rhs=QT_sb[hp, qg * 512 + c0:(qg + 1) * 512], start=True, stop=True),
                 reads=[b_KTsb[kb // 4], b_QTsb[qg]], writes=[b_z[zb]])

        def sbB1(n):
            c0 = cr(sb_tiles[n])
            zb, eb = n % 2, n % 2
            P.op("act", lambda e: e.activation(out=e_sb[eb][:, c0:512], in_=z_ps[zb][:, c0:512], func=AF.Exp, scale=-1.0),
                 reads=[b_z[zb]], writes=[b_e[eb]])

        def sbB2(n):
            c0 = cr(sb_tiles[n])
            eb, lb = n % 2, n % 3
            P.op("act", lambda e: e.activation(out=l_sb[lb][:, c0:512], in_=e_sb[eb][:, c0:512], func=AF.Ln,
                                               bias=1.0, scale=1.0),
                 reads=[b_e[eb]], writes=[b_l[lb]])

        def sbC(n):
            hh, qg, kb, i = sb_tiles[n]
            c0 = cr(sb_tiles[n])
            zb, lb, mb = n % 2, n % 3, n % 3
            P.op("dve", lambda e: e.scalar_tensor_tensor(out=lom[mb][:, c0:512], in0=z_ps[zb][:, c0:512], scalar=-1.0,
                                                        in1=l_sb[lb][:, c0:512], op0=ALU.mult, op1=ALU.subtract),
                 reads=[b_z[zb], b_l[lb]], writes=[b_lom[mb]])
            if i is not None:
                P.op("pool", lambda e: e.tensor_tensor(out=lom[mb][:, c0:c0 + 128], in0=lom[mb][:, c0:c0 + 128],
                                                      in1=MSB, op=ALU.mult),
                     reads=[b_lom[mb], b_cst], writes=[b_lom[mb]])

        def sbD(n):
            hh, qg, kb, i = sb_tiles[n]
            c0 = cr(sb_tiles[n])
            mb, bb = n % 3, n % 2
            gp_ = qg % 2
            first = (i == 3)
            if first:
                for b_ in range(2):
                    P.op("pool", lambda e, b_=b_: e.memset(Ls[gp_][b_][:], 0.0), writes=[b_Ls[gp_][b_]])
            k_in_grp = (4 * qg + 3) - kb
            old, new = k_in_grp % 2, 1 - (k_in_grp % 2)
            P.op("pe", lambda e: e.matmul(bt_ps[bb][:, c0:512], lhsT=UTRI, rhs=lom[mb][:, c0:512], start=True, stop=False),
                 reads=[b_lom[mb], b_cst], writes=[b_bt[bb]])
            P.op("pe", lambda e: e.matmul(bt_ps[bb][:, c0:512], lhsT=ONES, rhs=Ls[gp_][old][:, c0:512], start=False, stop=True),
                 reads=[b_Ls[gp_][old], b_cst], writes=[b_bt[bb]])
            if kb > 0:
                P.op("pool", lambda e: e.tensor_tensor(out=Ls[gp_][new][:, c0:512], in0=Ls[gp_][old][:, c0:512],
                                                      in1=lom[mb][:, c0:512], op=ALU.add),
                     reads=[b_Ls[gp_][old], b_lom[mb]], writes=[b_Ls[gp_][new]])

        def sbE(n):
            c0 = cr(sb_tiles[n])
            bb, lb, ab = n % 2, n % 3, n % 2
            P.op("dve", lambda e: e.tensor_tensor(out=arg[ab][:, c0:512], in0=bt_ps[bb][:, c0:512],
                                                 in1=l_sb[lb][:, c0:512], op=ALU.subtract),
                 reads=[b_bt[bb], b_l[lb]], writes=[b_arg[ab]])

        def sbF(n):
            hh, qg, kb, i = sb_tiles[n]
            c0 = cr(sb_tiles[n])
            ab, wb = n % 2, n % 3
            P.op("act", lambda e: e.activation(out=wt[wb][:, c0:512], in_=arg[ab][:, c0:512], func=AF.Exp),
                 reads=[b_arg[ab]], writes=[b_wt[wb]])
            if i is not None:
                P.op("pool", lambda e: e.tensor_tensor(out=wt[wb][:, c0:c0 + 128], in0=wt[wb][:, c0:c0 + 128],
                                                      in1=MSB, op=ALU.mult),
                     reads=[b_wt[wb], b_cst], writes=[b_wt[wb]])

        def emit_y(hc, qg, src_fn, reads):
            ys = yst_ctr[0] % 3
            yst_ctr[0] += 1
            src_fn(ys)
            o = P.op("sp", lambda e: e.dma_start(out=ybv[qg][:, :, hc * 64:(hc + 1) * 64], in_=yst[ys][:]),
                     reads=[b_yst[ys]], dma="yst%d" % ys)
            ydma_ops.append(o)

        def sbG(n):
            hh, qg, kb, i = sb_tiles[n]
            wb = n % 3
            ab_ = qg % 2
            j0 = 0 if i is None else i
            for j in range(j0, 4):
                first = (i == 3 and j == 3)
                P.op("pe", lambda e, j=j, first=first: e.matmul(
                    acc_ps[ab_][:, j, 0:64], lhsT=wt[wb][:, j * 128:(j + 1) * 128], rhs=V_sb[:, kb, hh, :],
                    start=first, stop=(kb == 0), skip_group_check=True),
                    reads=[b_wt[wb], b_Vsb[kb // 4]], writes=[b_acc[ab_]])
            if kb == 0:
                def src(ys):
                    P.op("dve", lambda e: e.tensor_copy(out=yst[ys][:], in_=acc_ps[ab_][:, :, 0:64]),
                         reads=[b_acc[ab_]], writes=[b_yst[ys]])
                emit_y(hh, qg, src, None)

        for s_ in range(NTL + 4):
            if s_ < NTL:
                sbA(s_)
                sbB1(s_)
            if 0 <= s_ - 3 < NTL:
                sbF(s_ - 3)
            if s_ < NTL:
                sbB2(s_)
            if 0 <= s_ - 1 < NTL:
                sbC(s_ - 1)
            if 0 <= s_ - 2 < NTL:
                sbD(s_ - 2)
                sbE(s_ - 2)
            if 0 <= s_ - 4 < NTL:
                sbG(s_ - 4)

        P.phase = 2
        fx_tiles = sb_tiles

        def fxA(n):
            hh, qg, kb, i = fx_tiles[n]
            c0 = cr(fx_tiles[n])
            zb = n % 2
            P.op("pe", lambda e: e.matmul(z_ps[zb][:, c0:512], lhsT=KT_fx[hh][:, kb * 128:(kb + 1) * 128],
                                         rhs=QT_fx[hh][:, qg * 512 + c0:(qg + 1) * 512], start=True, stop=True),
                 reads=[b_KTfx[hh][kb // 4], b_QTfx[hh][qg], b_aug_init] + b_KTfx_aug[hh][kb // 4] + b_QTfx_aug[hh][qg],
                 writes=[b_z[zb]])

        def fxB(n):
            hh, qg, kb, i = fx_tiles[n]
            c0 = cr(fx_tiles[n])
            zb, wb = n % 2, n % 3
            P.op("act", lambda e: e.activation(out=wt[wb][:, c0:512], in_=z_ps[zb][:, c0:512], func=AF.Exp),
                 reads=[b_z[zb]], writes=[b_wt[wb]])
            if i is not None:
                P.op("pool", lambda e: e.tensor_tensor(out=wt[wb][:, c0:c0 + 128], in0=wt[wb][:, c0:c0 + 128],
                                                      in1=MFX, op=ALU.mult),
                     reads=[b_wt[wb], b_cst], writes=[b_wt[wb]])

        def fxG(n):
            hh, qg, kb, i = fx_tiles[n]
            wb = n % 3
            ab_ = qg % 2
            j0 = 0 if i is None else i
            for j in range(j0, 4):
                first = (i == 3 and j == 3)
                P.op("pe", lambda e, j=j, first=first: e.matmul(
                    acc_ps[ab_][:, j, 0:65], lhsT=wt[wb][:, j * 128:(j + 1) * 128], rhs=V_fx[:, kb, hh, 0:65],
                    start=first, stop=(kb == 0), skip_group_check=True),
                    reads=[b_wt[wb], b_Vfx[kb // 4], b_Vfx_ones], writes=[b_acc[ab_]])
            if kb == 0:
                rb = qg % 2

                def src(ys):
                    P.op("dve", lambda e: e.reciprocal(out=rc[rb][:], in_=acc_ps[ab_][:, :, 64]),
                         reads=[b_acc[ab_]], writes=[b_rc[rb]])
                    for j in range(4):
                        P.op("dve", lambda e, j=j: e.tensor_scalar(out=yst[ys][:, j, :], in0=acc_ps[ab_][:, j, 0:64],
                                                                  scalar1=rc[rb][:, j:j + 1], scalar2=None, op0=ALU.mult),
                             reads=[b_acc[ab_], b_rc[rb]], writes=[b_yst[ys]])
                emit_y(2 + hh, qg, src, None)

        for s_ in range(NTL + 2):
            if s_ < NTL:
                fxA(s_)
            if 0 <= s_ - 1 < NTL:
                fxB(s_ - 1)
            if 0 <= s_ - 2 < NTL:
                fxG(s_ - 2)

    p1.close()
    if not full:
        P.final = list(ydma_ops)
        P.emit(nc, es)
        es.close()
        return nc

    P.barrier()
    P.phase = 3
    p2 = ExitStack()

    def sb2(name, shape, dt):
        return p2.enter_context(nc.sbuf_tensor(name, list(shape), dt))

    cst2 = sb2("cst2", [128, 128], BF16)
    b_cst2 = Buf()
    zpad = sb2("zpad", [128, 256], BF16)
    b_zpad = Buf()
    mh2 = sb2("mh2", [128, 1], F32)
    b_mh2 = Buf()
    g_fpost = sb2("g_fpost_t", [128, D], F32)
    cw = sb2("cw_t", [128, 2 * NFF, 3], F32)
    cb = sb2("cb_t", [128, 2 * NFF], F32)
    flag = sb2("flag_t", [128, 1], F32)
    b_par = Buf()
    P.op("pool", lambda e: e.memset(zpad[:], 0.0), writes=[b_zpad])
    P.op("pool", lambda e: e.memset(mh2[:], -0.5), writes=[b_mh2])
    zp = P.op("sp", lambda e: e.dma_start(out=ybuf_ap[0:HALO, :], in_=zpad[:]), reads=[b_zpad], dma="zpad")
    P.op("pool", lambda e: e.dma_start(out=cst2[:], in_=consts[:, 0, :]), writes=[b_cst2], dma="cstq2")
    for (t_, d_) in ((g_fpost, g_fpost_d), (cw, cw_d), (cb, cb_d), (flag, flag_d)):
        P.op("sp", lambda e, t_=t_, d_=d_: e.dma_start(out=t_[:], in_=d_), writes=[b_par], dma="all:par")
    b_yall = Buf()
    cc_sem = es.enter_context(nc.semaphore("cc_sem"))

    def cc_fn(e):
        if debug == "nocc":
            return e.dma_start(out=yall.ap()[0:S + HALO, :], in_=ybuf_ap)
        return e.collective_compute("AllGather", ALU.bypass, replica_groups=[list(range(8))],
                                    ins=[ybuf_ap.opt()], outs=[yall.ap().opt()])
    cc_box = {}

    def issue_cc():
        cc_op = P.op("pool", cc_fn, writes=[b_yall], extra=list(ydma_ops) + [zp])
        if debug != "nocc":
            cc_op.sem = cc_sem
            cc_op.ticket = 1
            cc_op.bar = "cc"
        else:
            cc_op.dma = "ccdbg"
        P.pending.append(cc_op)
        cc_box["op"] = cc_op

    IDENT2 = cst2[:]
    rot = {"n": 0}

    h2buf = nc.dram_tensor("h2buf", [128, 8, TOK2], BF16).ap()
    b_h2buf = [Buf() for _ in range(NT2)]
    b_x1buf = [Buf() for _ in range(NT2)]
    with ExitStack() as pA:
        def sbA_(name, shape, dt):
            return pA.enter_context(nc.sbuf_tensor(name, list(shape), dt))

        def psA_(name, shape, dt):
            return pA.enter_context(nc.psum_tensor(name, list(shape), dt))

        g_post = sbA_("g_post_t", [128, D], F32)
        gpb = sbA_("gpb_t", [128, D], F32)
        gfpb = sbA_("gfpb_t", [128, D], F32)
        b_parA = Buf()
        for (t_, d_) in ((g_post, g_post_d), (gpb, gpre_b_d), (gfpb, gfpre_b_d)):
            P.op("sp", lambda e, t_=t_, d_=d_: e.dma_start(out=t_[:], in_=d_), writes=[b_parA], dma="all:parA")
        wg = sbA_("wg", [128, 8, 2048], BF16)
        wbs = sbA_("wbs", [128, 4, D], BF16)
        wbf_ = sbA_("wbf", [128, 4, D], BF16)
        wo = sbA_("wo", [128, 8, D], BF16)
        b_wg = [Buf() for _ in range(4)]
        b_wA = Buf()
        NS = 4
        xt2 = [sbA_("xt2_%d" % i, [128, D], F32) for i in range(NS)]
        b_xt2 = [Buf() for _ in range(NS)]
        yt2 = [sbA_("yt2_%d" % i, [128, 4, 256], BF16) for i in range(2)]
        b_yt2 = [Buf() for _ in range(2)]
        jk2 = sbA_("jk2", [128, D], BF16)
        b_jk2 = Buf()
        st2 = [sbA_("st2_%d" % i, [128, 4], F32) for i in range(8)]
        b_st2 = [Buf() for _ in range(8)]
        xn2 = [sbA_("xn2_%d" % i, [128, D], BF16) for i in range(2)]
        b_xn2 = [Buf() for _ in range(2)]
        hTt = [sbA_("hTt%d" % i, [128, 8, 128], BF16) for i in range(2)]
        b_hTt = [Buf() for _ in range(2)]
        yTt = [sbA_("yTt%d" % i, [128, 8, 128], BF16) for i in range(2)]
        b_yTt = [Buf() for _ in range(2)]
        Gs = [sbA_("Gs%d" % i, [128, 2048], F32) for i in range(2)]
        b_Gs = [[Buf(), Buf()] for _ in range(2)]
        m1 = [sbA_("m1_%d" % i, [128, D], F32) for i in range(2)]
        b_m1 = [Buf() for _ in range(2)]
        mx = [sbA_("mx%d" % i, [128, D], BF16) for i in range(2)]
        b_mx = [Buf() for _ in range(2)]
        mTt = [sbA_("mTt%d" % i, [128, 8, 128], BF16) for i in range(2)]
        b_mTt = [Buf() for _ in range(2)]
        t1 = [sbA_("t1_%d" % i, [128, D], F32) for i in range(2)]
        b_t1 = [Buf() for _ in range(2)]
        t2_ = [sbA_("t2x_%d" % i, [128, D], F32) for i in range(2)]
        b_t2 = [Buf() for _ in range(2)]
        x1t = [sbA_("x1t%d" % i, [128, D], F32) for i in range(2)]
        b_x1t = [Buf() for _ in range(2)]
        h2b = [sbA_("h2b%d" % i, [128, D], BF16) for i in range(2)]
        b_h2b = [Buf() for _ in range(2)]
        h2Tt = [sbA_("h2Tt%d" % i, [128, 8, 128], BF16) for i in range(2)]
        b_h2Tt = [Buf() for _ in range(2)]
        tp2 = [psA_("tp2_%d" % i, [128, 8, 128], BF16) for i in range(2)]
        b_tp2 = [Buf() for _ in range(2)]
        big = [psA_("big%d" % i, [128, D], F32) for i in range(3)]
        b_big = [Buf() for _ in range(3)]

        wg_v = w_gate.rearrange("(k p) c -> p k c", p=128)
        for cg in range(4):
            P.op("pool", lambda e, cg=cg: e.dma_start(out=wg[:, :, cg * 512:(cg + 1) * 512],
                                                     in_=wg_v[:, :, cg * 512:(cg + 1) * 512]),
                 writes=[b_wg[cg]], dma="wgq%d" % cg)
        P.op("pool", lambda e: e.dma_start(out=wbs[:], in_=w_bsb.rearrange("(k p) c -> p k c", p=128)),
             writes=[b_wA], dma="all:wA")
        P.op("pool", lambda e: e.dma_start(out=wbf_[:], in_=w_bfx.rearrange("(k p) c -> p k c", p=128)),
             writes=[b_wA], dma="all:wA")
        P.op("pool", lambda e: e.dma_start(out=wo[:], in_=w_out.rearrange("(k p) c -> p k c", p=128)),
             writes=[b_wA], dma="all:wA")
        issue_cc()

        def nbig():
            i = rot["n"] % 3
            rot["n"] += 1
            return i

        def rstd_chain(src_ap, src_bufs, sti):
            P.op("act", lambda e: e.activation(out=jk2[:], in_=src_ap, func=AF.Square, accum_out=st2[sti][:, 0:1]),
                 reads=src_bufs, writes=[b_jk2, b_st2[sti]])
            P.op("dve", lambda e: e.tensor_scalar(out=st2[sti][:, 1:2], in0=st2[sti][:, 0:1], scalar1=1.0 / D,
                                                 scalar2=EPS, op0=ALU.mult, op1=ALU.add),
                 reads=[b_st2[sti]], writes=[b_st2[sti]])
            P.op("pool", lambda e: e.tensor_tensor(out=st2[sti][:, 2:3], in0=st2[sti][:, 1:2], in1=mh2[:], op=ALU.pow),
                 reads=[b_st2[sti], b_mh2], writes=[b_st2[sti]])
            return st2[sti][:, 2:3]

        tpc = {"n": 0}

        def transposes(src, src_buf, dst, dst_buf, chunk_fn=None):
            tpc["n"] += 1
            tpi = tpc["n"] % 2
            for k in range(8):
                in_ap = src[:, k * 128:(k + 1) * 128] if chunk_fn is None else chunk_fn(k)
                P.op("pe", lambda e, k=k, in_ap=in_ap: e.transpose(out=tp2[tpi][:, k, :], in_=in_ap, identity=IDENT2),
                     reads=[src_buf, b_cst2], writes=[b_tp2[tpi]])
            P.op("act", lambda e: e.activation(out=dst, in_=tp2[tpi][:], func=AF.Copy),
                 reads=[b_tp2[tpi]], writes=[dst_buf])

        yall_v = yall.ap().rearrange("(r t) c -> t r c", r=8)
        pid_cache = {}

        def A0(t):
            xs, ys = t % NS, t % 2
            P.op("sp", lambda e: e.dma_start(out=xt2[xs][:], in_=x_own[t * 128:(t + 1) * 128, :]),
                 writes=[b_xt2[xs]], dma="xt2_%d" % xs)

            def yfn(e):
                if "c" not in pid_cache:
                    pid = e.partition_id()
                    pid_cache["c"] = pid
                    pid_cache["base"] = yall_v[bass.ds((pid % 4) * TOKC, TOK2), bass.ds((pid // 4) * 4, 4), :]
                return e.dma_start(out=yt2[ys][:], in_=pid_cache["base"][t * 128:(t + 1) * 128, :, :])
            P.op("sp", yfn, reads=[b_yall], writes=[b_yt2[ys]], dma="yt2_%d" % ys, extra=[cc_box["op"]])
            sti = t % 4
            r = rstd_chain(xt2[xs][:], [b_xt2[xs]], sti)
            P.op("dve", lambda e: e.scalar_tensor_tensor(out=xn2[t % 2][:], in0=xt2[xs][:], scalar=r, in1=gpb[:],
                                                        op0=ALU.mult, op1=ALU.mult),
                 reads=[b_xt2[xs], b_st2[sti], b_parA], writes=[b_xn2[t % 2]])

        def A1a(t):
            s2 = t % 2
            transposes(xn2[s2], b_xn2[s2], hTt[s2][:], b_hTt[s2])
            transposes(None, b_yt2[s2], yTt[s2][:], b_yTt[s2],
                       chunk_fn=lambda k: yt2[s2][:, k // 2, (k % 2) * 128:(k % 2) * 128 + 128])

        def A1b(t):
            s2 = t % 2
            for br in range(2):
                bi = nbig()
                for cg in range(2):
                    for k in range(8):
                        P.op("pe", lambda e, k=k, cg=cg, bi=bi, br=br: e.matmul(
                            big[bi][:, cg * 512:(cg + 1) * 512], lhsT=hTt[s2][:, k, :],
                            rhs=wg[:, k, br * 1024 + cg * 512: br * 1024 + (cg + 1) * 512],
                            start=(k == 0), stop=(k == 7)),
                            reads=[b_hTt[s2], b_wg[2 * br + cg]], writes=[b_big[bi]])
                P.op("act", lambda e, bi=bi, br=br: e.activation(out=Gs[s2][:, br * 1024:(br + 1) * 1024], in_=big[bi][:],
                                                                 func=AF.Sigmoid),
                     reads=[b_big[bi]], writes=[b_Gs[s2][br]])
            bi = nbig()
            for cg in range(2):
                for r in range(4):
                    P.op("pe", lambda e, r=r, cg=cg, bi=bi: e.matmul(
                        big[bi][:, cg * 512:(cg + 1) * 512], lhsT=yTt[s2][:, 2 * r, :],
                        rhs=wbs[:, r, cg * 512:(cg + 1) * 512], start=(r == 0), stop=(r == 3)),
                        reads=[b_yTt[s2], b_wA], writes=[b_big[bi]])
            P.op("dve", lambda e, bi=bi: e.tensor_tensor(out=m1[s2][:], in0=big[bi][:], in1=Gs[s2][:, 0:1024], op=ALU.mult),
                 reads=[b_big[bi], b_Gs[s2][0]], writes=[b_m1[s2]])
            bi = nbig()
            for cg in range(2):
                for r in range(4):
                    P.op("pe", lambda e, r=r, cg=cg, bi=bi: e.matmul(
                        big[bi][:, cg * 512:(cg + 1) * 512], lhsT=yTt[s2][:, 2 * r + 1, :],
                        rhs=wbf_[:, r, cg * 512:(cg + 1) * 512], start=(r == 0), stop=(r == 3)),
                        reads=[b_yTt[s2], b_wA], writes=[b_big[bi]])
            P.op("dve", lambda e, bi=bi: e.tensor_tensor(out=t2_[s2][:], in0=big[bi][:], in1=Gs[s2][:, 1024:2048], op=ALU.mult),
                 reads=[b_big[bi], b_Gs[s2][1]], writes=[b_t2[s2]])
            P.op("pool", lambda e: e.tensor_tensor(out=mx[s2][:], in0=m1[s2][:], in1=t2_[s2][:], op=ALU.add),
                 reads=[b_m1[s2], b_t2[s2]], writes=[b_mx[s2]])

        def A2a(t):
            s2 = t % 2
            xs = t % NS
            transposes(mx[s2], b_mx[s2], mTt[s2][:], b_mTt[s2])
            bi = nbig()
            for cg in range(2):
                for k in range(8):
                    P.op("pe", lambda e, k=k, cg=cg, bi=bi: e.matmul(
                        big[bi][:, cg * 512:(cg + 1) * 512], lhsT=mTt[s2][:, k, :],
                        rhs=wo[:, k, cg * 512:(cg + 1) * 512], start=(k == 0), stop=(k == 7)),
                        reads=[b_mTt[s2], b_wA], writes=[b_big[bi]])
            sti = 4 + (t % 2)
            r = rstd_chain(big[bi][:], [b_big[bi]], sti)
            P.op("dve", lambda e, bi=bi: e.scalar_tensor_tensor(out=t1[s2][:], in0=big[bi][:], scalar=r, in1=g_post[:],
                                                               op0=ALU.mult, op1=ALU.mult),
                 reads=[b_big[bi], b_st2[sti], b_parA], writes=[b_t1[s2]])
            P.op("pool", lambda e: e.tensor_tensor(out=x1t[s2][:], in0=xt2[xs][:], in1=t1[s2][:], op=ALU.add),
                 reads=[b_xt2[xs], b_t1[s2]], writes=[b_x1t[s2]])
            P.op("sp", lambda e: e.dma_start(out=x1buf[t * 128:(t + 1) * 128, :], in_=x1t[s2][:]),
                 reads=[b_x1t[s2]], writes=[b_x1buf[t]], dma="x1st%d" % s2)
            sti2 = 6 + (t % 2)
            r2 = rstd_chain(x1t[s2][:], [b_x1t[s2]], sti2)
            P.op("dve", lambda e: e.scalar_tensor_tensor(out=h2b[s2][:], in0=x1t[s2][:], scalar=r2, in1=gfpb[:],
                                                        op0=ALU.mult, op1=ALU.mult),
                 reads=[b_x1t[s2], b_st2[sti2], b_parA], writes=[b_h2b[s2]])
            if t == 0:
                P.op("dve", lambda e: e.tensor_scalar(out=h2b[s2][:], in0=h2b[s2][:], scalar1=flag[:, 0:1], scalar2=None,
                                                     op0=ALU.mult),
                     reads=[b_h2b[s2], b_par], writes=[b_h2b[s2]])

        def A2b(t):
            s2 = t % 2
            transposes(h2b[s2], b_h2b[s2], h2Tt[s2][:], b_h2Tt[s2])
            P.op("sp", lambda e: e.dma_start(out=h2buf[:, :, t * 128:(t + 1) * 128], in_=h2Tt[s2][:]),
                 reads=[b_h2Tt[s2]], writes=[b_h2buf[t]], dma="h2st%d" % s2)

        for it in range(-3, NT2 + 1):
            if 0 <= it + 3 < NT2:
                A0(it + 3)
            if 0 <= it + 2 < NT2:
                A1a(it + 2)
            if 0 <= it + 1 < NT2:
                A1b(it + 1)
            if 0 <= it < NT2:
                A2a(it)
            if 0 <= it - 1 < NT2:
                A2b(it - 1)

    P.barrier()
    P.phase = 4
    actT = p2.enter_context(nc.sbuf_tensor("actT", [128, NFF, TOKC], BF16))
    b_actT = [Buf() for _ in range(NFF)]
    with ExitStack() as pU:
        def sbU(name, shape, dt):
            return pU.enter_context(nc.sbuf_tensor(name, list(shape), dt))

        def psU(name, shape, dt):
            return pU.enter_context(nc.psum_tensor(name, list(shape), dt))

        h2T = sbU("h2T", [128, 8, TOK2 + 2], BF16)
        NW = 3
        wub = [sbU("wub%d" % i, [128, 8, 2, 128], BF16) for i in range(NW)]
        b_wub = [Buf() for _ in range(NW)]
        ac = [[sbU("ac%d_%d" % (i, part), [128, TOK2], F32) for part in range(2)] for i in range(2)]
        ups = [psU("ups%d" % i, [128, 512], F32) for i in range(8)]
        b_ups = [Buf() for _ in range(8)]
        upc = {"n": 0}
        wup_v = w_up.rearrange("(k p) (t f) -> p k t f", p=128, t=2)
        grp = []
        t0_ = 0
        while t0_ < TOK2:
            n_ = min(510, TOK2 - t0_)
            grp.append((t0_, n_))
            t0_ += n_
        b_ac = [[[Buf() for _ in grp] for _ in range(2)] for _ in range(2)]
        cgs = []
        c0_ = 0
        while c0_ < TOK2:
            w_ = min(512, TOK2 - c0_)
            cgs.append((c0_, w_))
            c0_ += w_
        b_h2Tg = [Buf() for _ in cgs]
        b_h2pad = Buf()
        P.op("dve", lambda e: e.memset(h2T[:, :, 0:2], 0.0), writes=[b_h2pad])
        for gi, (cc0, cw_) in enumerate(cgs):
            P.op("sp", lambda e, cc0=cc0, cw_=cw_: e.dma_start(out=h2T[:, :, 2 + cc0:2 + cc0 + cw_],
                                                             in_=h2buf[:, :, cc0:cc0 + cw_]),
                 reads=b_h2buf[cc0 // 128:(cc0 + cw_) // 128], writes=[b_h2Tg[gi]], dma="h2ld%d" % gi)

        def h2deps(t0, n):
            lo, hi = max(t0 - 2, 0), t0 + n
            return [b_h2Tg[gi] for gi, (cc0, cw_) in enumerate(cgs) if cc0 < hi and cc0 + cw_ > lo] + [b_h2pad]

        def U0(c):
            s_ = c % NW
            for part in range(2):
                P.op("pool", lambda e, part=part: e.dma_start(out=wub[s_][:, :, part, :],
                                                            in_=wup_v[:, :, part, c * 128:(c + 1) * 128]),
                     writes=[b_wub[s_]], dma="wub%d" % s_, bar=c)

        def U1(c):
            s_ = c % NW
            a = c % 2
            for part in (1, 0):
                fi = part * NFF + c
                for gi, (t0, n) in enumerate(grp):
                    pi = upc["n"] % 8
                    upc["n"] += 1
                    for k in range(8):
                        P.op("pe", lambda e, k=k, pi=pi, t0=t0, n=n, part=part: e.matmul(
                            ups[pi][:, 0:n + 2], lhsT=wub[s_][:, k, part, :], rhs=h2T[:, k, t0:t0 + n + 2],
                            start=(k == 0), stop=(k == 7)),
                            reads=[b_wub[s_]] + h2deps(t0, n), writes=[b_ups[pi]])
                    dst = ac[a][part][:, t0:t0 + n]
                    bd = b_ac[a][part][gi]
                    P.op("act", lambda e, pi=pi, n=n, dst=dst, fi=fi: e.activation(
                        out=dst, in_=ups[pi][:, 2:n + 2], func=AF.Identity, scale=cw[:, fi, 2:3], bias=cb[:, fi:fi + 1]),
                        reads=[b_ups[pi], b_par], writes=[bd])
                    P.op("dve", lambda e, pi=pi, n=n, dst=dst, fi=fi: e.scalar_tensor_tensor(
                        out=dst, in0=ups[pi][:, 1:n + 1], scalar=cw[:, fi, 1:2], in1=dst, op0=ALU.mult, op1=ALU.add),
                        reads=[b_ups[pi], b_par, bd], writes=[bd])
                    P.op("dve", lambda e, pi=pi, n=n, dst=dst, fi=fi: e.scalar_tensor_tensor(
                        out=dst, in0=ups[pi][:, 0:n], scalar=cw[:, fi, 0:1], in1=dst, op0=ALU.mult, op1=ALU.add),
                        reads=[b_ups[pi], b_par, bd], writes=[bd])
            P.op("act", lambda e: e.activation(out=ac[a][0][:, HALO:], in_=ac[a][0][:, HALO:], func=AF.Gelu_apprx_tanh),
                 reads=b_ac[a][0], writes=b_ac[a][0])
            P.op("pool", lambda e: e.tensor_tensor(out=actT[:, c, :], in0=ac[a][0][:, HALO:], in1=ac[a][1][:, HALO:], op=ALU.mult),
                 reads=b_ac[a][0] + b_ac[a][1], writes=[b_actT[c]])

        U0(0)
        if NFF > 1:
            U0(1)
        for c in range(NFF):
            if c + 2 < NFF:
                U0(c + 2)
            U1(c)

    P.barrier()
    P.phase = 5
    out_ops = []
    with ExitStack() as pD:
        def sbD_(name, shape, dt):
            return pD.enter_context(nc.sbuf_tensor(name, list(shape), dt))

        def psD(name, shape, dt):
            return pD.enter_context(nc.psum_tensor(name, list(shape), dt))

        wd = sbD_("wd", [128, NFF, D], BF16)
        wpg = sbD_("wpg", [128, 8, D], BF16)
        wpl = sbD_("wpl", [128, 2, D], BF16)
        b_wd = [Buf() for _ in range(2)]
        b_wD = Buf()
        hf_ = NFF // 2
        wd_v = w_down.rearrange("(k p) c -> p k c", p=128)
        P.op("pool", lambda e: e.dma_start(out=wd[:, 0:hf_, :], in_=wd_v[:, 0:hf_, :]), writes=[b_wd[0]], dma="wdq0")
        P.op("pool", lambda e: e.dma_start(out=wd[:, hf_:NFF, :], in_=wd_v[:, hf_:NFF, :]), writes=[b_wd[1]], dma="wdq1")
        P.op("pool", lambda e: e.dma_start(out=wpg[:], in_=w_pg.rearrange("(k p) c -> p k c", p=128)),
             writes=[b_wD], dma="all:wD")
        P.op("pool", lambda e: e.dma_start(out=wpl[:], in_=w_ple.rearrange("(k p) c -> p k c", p=128)),
             writes=[b_wD], dma="all:wD")
        x1r = [sbD_("x1r%d" % i, [128, D], F32) for i in range(2)]
        b_x1r = [Buf() for _ in range(2)]
        jk3 = sbD_("jk3", [128, D], BF16)
        b_jk3 = Buf()
        st3 = [sbD_("st3_%d" % i, [128, 4], F32) for i in range(2)]
        b_st3 = [Buf() for _ in range(2)]
        t3a = sbD_("t3a", [128, D], F32)
        b_t3a = Buf()
        x2 = [sbD_("x2_%d" % i, [128, D], F32) for i in range(3)]
        b_x2 = [Buf() for _ in range(3)]
        x2b = [sbD_("x2b%d" % i, [128, D], BF16) for i in range(2)]
        b_x2b = [Buf() for _ in range(2)]
        x2T = [sbD_("x2T%d" % i, [128, 8, 128], BF16) for i in range(2)]
        b_x2T = [Buf() for _ in range(2)]
        ptl = [sbD_("ptl%d" % i, [128, PLE], F32) for i in range(4)]
        b_ptl = [Buf() for _ in range(4)]
        ptb = [sbD_("ptb%d" % i, [128, PLE], BF16) for i in range(4)]
        b_ptb = [Buf() for _ in range(4)]
        pT = [sbD_("pT%d" % i, [128, 2, 128], BF16) for i in range(2)]
        b_pT = [Buf() for _ in range(2)]
        sg = sbD_("sg", [128, D], F32)
        b_sg = Buf()
        t3b = sg
        b_t3b = b_sg
        ot = [sbD_("ot%d" % i, [128, D], F32) for i in range(1)] * 2
        b_ot = [Buf()] * 2
        tp3 = [psD("tp3_%d" % i, [128, 8, 128], BF16) for i in range(2)]
        b_tp3 = [Buf() for _ in range(2)]
        bg = [psD("bg%d" % i, [128, D], F32) for i in range(3)]
        b_bg = [Buf() for _ in range(3)]
        rot3 = {"n": 0, "t": 0}

        def nbg():
            i = rot3["n"] % 3
            rot3["n"] += 1
            return i

        def ntp3():
            rot3["t"] += 1
            return rot3["t"] % 2

        def D0(t):
            s2 = t % 2
            P.op("sp", lambda e: e.dma_start(out=x1r[s2][:], in_=x1buf[(t + 1) * 128:(t + 2) * 128, :]),
                 reads=[b_x1buf[t + 1]], writes=[b_x1r[s2]], dma="x1r%d" % s2)
            s4 = t % 4
            P.op("sp", lambda e: e.dma_start(out=ptl[s4][:], in_=p_own[(t + 1) * 128:(t + 2) * 128, :]),
                 writes=[b_ptl[s4]], dma="ptl%d" % s4)
            P.op("act", lambda e: e.activation(out=ptb[s4][:], in_=ptl[s4][:], func=AF.Copy),
                 reads=[b_ptl[s4]], writes=[b_ptb[s4]])

        def D1f(t):
            s2 = t % 2
            x3 = t % 3
            bi = nbg()
            for cg in range(2):
                for c in range(NFF):
                    P.op("pe", lambda e, c=c, cg=cg, bi=bi: e.matmul(
                        bg[bi][:, cg * 512:(cg + 1) * 512], lhsT=actT[:, c, t * 128:(t + 1) * 128],
                        rhs=wd[:, c, cg * 512:(cg + 1) * 512], start=(c == 0), stop=(c == NFF - 1)),
                        reads=[b_actT[c], b_wd[0 if c < hf_ else 1]], writes=[b_bg[bi]])
            P.op("act", lambda e, bi=bi: e.activation(out=jk3[:], in_=bg[bi][:], func=AF.Square, accum_out=st3[s2][:, 0:1]),
                 reads=[b_bg[bi]], writes=[b_jk3, b_st3[s2]])
            P.op("dve", lambda e: e.tensor_scalar(out=st3[s2][:, 1:2], in0=st3[s2][:, 0:1], scalar1=1.0 / D, scalar2=EPS,
                                                 op0=ALU.mult, op1=ALU.add),
                 reads=[b_st3[s2]], writes=[b_st3[s2]])
            P.op("pool", lambda e: e.tensor_tensor(out=st3[s2][:, 2:3], in0=st3[s2][:, 1:2], in1=mh2[:], op=ALU.pow),
                 reads=[b_st3[s2], b_mh2], writes=[b_st3[s2]])
            P.op("dve", lambda e, bi=bi: e.scalar_tensor_tensor(out=t3a[:], in0=bg[bi][:], scalar=st3[s2][:, 2:3],
                                                               in1=g_fpost[:], op0=ALU.mult, op1=ALU.mult),
                 reads=[b_bg[bi], b_st3[s2], b_par], writes=[b_t3a])
            P.op("pool", lambda e: e.tensor_tensor(out=x2[x3][:], in0=x1r[s2][:], in1=t3a[:], op=ALU.add),
                 reads=[b_x1r[s2], b_t3a], writes=[b_x2[x3]])
            P.op("act", lambda e: e.activation(out=x2b[s2][:], in_=x2[x3][:], func=AF.Copy),
                 reads=[b_x2[x3]], writes=[b_x2b[s2]])

        def D1t(t):
            s2 = t % 2
            ti = ntp3()
            for k in range(8):
                P.op("pe", lambda e, k=k, ti=ti: e.transpose(out=tp3[ti][:, k, :], in_=x2b[s2][:, k * 128:(k + 1) * 128],
                                                           identity=IDENT2),
                     reads=[b_x2b[s2], b_cst2], writes=[b_tp3[ti]])
            P.op("act", lambda e, ti=ti: e.activation(out=x2T[s2][:], in_=tp3[ti][:], func=AF.Copy),
                 reads=[b_tp3[ti]], writes=[b_x2T[s2]])
            ti = ntp3()
            for k in range(2):
                P.op("pe", lambda e, k=k, ti=ti: e.transpose(out=tp3[ti][:, k, :], in_=ptb[t % 4][:, k * 128:(k + 1) * 128],
                                                           identity=IDENT2),
                     reads=[b_ptb[t % 4], b_cst2], writes=[b_tp3[ti]])
            P.op("act", lambda e, ti=ti: e.activation(out=pT[s2][:], in_=tp3[ti][:, 0:2, :], func=AF.Copy),
                 reads=[b_tp3[ti]], writes=[b_pT[s2]])

        def D1b(t):
            s2 = t % 2
            bi = nbg()
            for cg in range(2):
                for k in range(8):
                    P.op("pe", lambda e, k=k, cg=cg, bi=bi: e.matmul(
                        bg[bi][:, cg * 512:(cg + 1) * 512], lhsT=x2T[s2][:, k, :],
                        rhs=wpg[:, k, cg * 512:(cg + 1) * 512], start=(k == 0), stop=(k == 7)),
                        reads=[b_x2T[s2], b_wD], writes=[b_bg[bi]])
            P.op("act", lambda e, bi=bi: e.activation(out=sg[:], in_=bg[bi][:], func=AF.Sigmoid),
                 reads=[b_bg[bi]], writes=[b_sg])
            bi = nbg()
            for cg in range(2):
                for k in range(2):
                    P.op("pe", lambda e, k=k, cg=cg, bi=bi: e.matmul(
                        bg[bi][:, cg * 512:(cg + 1) * 512], lhsT=pT[s2][:, k, :],
                        rhs=wpl[:, k, cg * 512:(cg + 1) * 512], start=(k == 0), stop=(k == 1)),
                        reads=[b_pT[s2], b_wD], writes=[b_bg[bi]])
            P.op("dve", lambda e, bi=bi: e.tensor_tensor(out=sg[:], in0=bg[bi][:], in1=sg[:], op=ALU.mult),
                 reads=[b_bg[bi], b_sg], writes=[b_sg])
            P.op("pool", lambda e: e.tensor_tensor(out=ot[s2][:], in0=x2[t % 3][:], in1=t3b[:], op=ALU.add),
                 reads=[b_x2[t % 3], b_t3b], writes=[b_ot[s2]])
            o = P.op("sp", lambda e: e.dma_start(out=out_d[t * 128:(t + 1) * 128, :], in_=ot[s2][:]),
                     reads=[b_ot[s2]], dma="ot0")
            out_ops.append(o)

        NTM = TOKC // 128
        for it in range(-3, NTM):
            if 0 <= it + 3 < NTM:
                D0(it + 3)
            if 0 <= it + 2 < NTM:
                D1f(it + 2)
            if 0 <= it + 1 < NTM:
                D1t(it + 1)
            if 0 <= it < NTM:
                D1b(it)

    p2.close()
    P.final = list(out_ops)
    P.emit(nc, es)
    es.close()
    return nc


def _consts():
    j = np.arange(128)[:, None]
    s = np.arange(128)[None, :]
    ident = (j == s)
    utri = (j > s)
    ones = np.ones((128, 128), bool)
    msb = (j < s)
    mfx = (j <= s)
    return np.stack([ident, utri, ones, msb, mfx], axis=1).astype(np.float32)


def _core_inputs(c, x, w_in, norm_attn_pre, b_forget, full=None):
    b, g = divmod(c, 4)
    wi = w_in[0]
    SS = x.shape[1]
    TOKC = SS // 4

    def hc(base, h):
        return slice(base + 64 * h, base + 64 * (h + 1))
    sbh = [2 * g, 2 * g + 1]
    q_sb = np.concatenate([wi[:, hc(0, h)] for h in sbh], 1)
    k_sb = np.concatenate([wi[:, hc(512, h)] for h in sbh], 1)
    v_sb = np.concatenate([wi[:, hc(1024, h)] for h in sbh], 1)
    q_fx = np.concatenate([wi[:, hc(1536, h)] for h in sbh], 1)
    k_fx = np.concatenate([wi[:, hc(2048, h)] for h in sbh], 1)
    v_fx = np.concatenate([wi[:, hc(2560, h)] for h in sbh], 1)
    f_l = wi[:, 3072 + 2 * g:3072 + 2 * g + 2]
    wqk = np.ascontiguousarray(np.concatenate([q_sb, q_fx, k_sb, k_fx], 1))
    wvf = np.ascontiguousarray(np.concatenate([v_sb, v_fx, f_l], 1))
    m = {
        "xb": np.ascontiguousarray(x[b]),
        "consts": _consts(),
        "wqk": wqk,
        "wvf": wvf,
        "gpre": np.ascontiguousarray(norm_attn_pre[0].reshape(8, 128).T),
        "bfg": np.ascontiguousarray(b_forget[0, 2 * g:2 * g + 2].reshape(2, 1)),
    }
    if full is not None:
        f = full
        x_own = np.zeros((TOKC + HALO, D), np.float32)
        p_own = np.zeros((TOKC + HALO, PLE), np.float32)
        lo = g * TOKC - HALO
        if lo >= 0:
            x_own[:] = x[b, lo:lo + TOKC + HALO]
            p_own[:] = f["p"][0, b, lo:lo + TOKC + HALO]
        else:
            x_own[HALO:] = x[b, 0:TOKC]
            p_own[HALO:] = f["p"][0, b, 0:TOKC]
        m.update({
            "x_own": x_own,
            "p_own": p_own,
            "w_gate": np.ascontiguousarray(wi[:, 3080:3080 + 2048]),
            "w_bsb": np.ascontiguousarray(f["w_branch_sb"][0]),
            "w_bfx": np.ascontiguousarray(f["w_branch_fox"][0]),
            "w_out": np.ascontiguousarray(f["w_out"][0]),
            "w_up": np.ascontiguousarray(f["w_up"][0]),
            "w_down": np.ascontiguousarray(f["w_down"][0]),
            "w_ple": np.ascontiguousarray(f["w_ple"][0]),
            "w_pg": np.ascontiguousarray(f["w_ple_gate"][0]),
            "g_post": np.ascontiguousarray(np.broadcast_to(f["norm_attn_post"][0][None, :], (128, D))),
            "g_fpost": np.ascontiguousarray(np.broadcast_to(f["norm_ffn_post"][0][None, :], (128, D))),
            "gpre_b": np.ascontiguousarray(np.broadcast_to(norm_attn_pre[0][None, :], (128, D))),
            "gfpre_b": np.ascontiguousarray(np.broadcast_to(f["norm_ffn_pre"][0][None, :], (128, D))),
            "cw": np.ascontiguousarray(f["conv_w"][0].reshape(3, 2 * NFF, 128).transpose(2, 1, 0)),
            "cb": np.ascontiguousarray(f["conv_b"][0].reshape(2 * NFF, 128).T),
            "flag": np.full((128, 1), 0.0 if g == 0 else 1.0, np.float32),
        })
    return m


_NC_CACHE = {}


def kernel(**inputs):
    inputs = {k: np.asarray(v, dtype=np.float32) for k, v in inputs.items()}
    x = inputs["x"]
    if "nc" not in _NC_CACHE:
        _NC_CACHE["nc"] = build()
    nc = _NC_CACHE["nc"]
    in_maps = [_core_inputs(c, x, inputs["w_in"], inputs["norm_attn_pre"], inputs["b_forget"], full=inputs)
               for c in range(8)]
    res = run_bass_kernel_spmd(nc, in_maps, core_ids=list(range(8)))
    TOKC = x.shape[1] // 4
    out = np.empty_like(x)
    for c in range(8):
        b, g = divmod(c, 4)
        out[b, g * TOKC:(g + 1) * TOKC] = np.asarray(res.results[c]["out"])
    return out
```
